# Optimizing a Trainium2 kernel written in Bass

```python
import math
import jax, jax.numpy as jnp
from jax import lax
import numpy as np

D_MODEL = 1024
BATCH = 16
SEQ = 2048
DEPTH = 1

RW_WIDTH = 512
RW_HEAD = 64
RW_HEADS = RW_WIDTH // RW_HEAD
RW_DECAY_RANK = 64
RW_AAA_RANK = 64
RW_GN_EPS = RW_HEAD * 1e-5
GD_WIDTH = 512
GD_HEAD = 128
GD_HEADS = GD_WIDTH // GD_HEAD
GD_CONV = 4
GD_CHUNK = 64
NORM_EPS = 1e-6

RW_SHIFT_COLS = 3 * RW_WIDTH + RW_DECAY_RANK + RW_AAA_RANK
SPLIT_RW_Z = RW_SHIFT_COLS
SPLIT_GD_QKV = SPLIT_RW_Z + RW_WIDTH
SPLIT_GD_Z = SPLIT_GD_QKV + 3 * GD_WIDTH
SPLIT_GD_BETA = SPLIT_GD_Z + GD_WIDTH
SPLIT_GD_ALPHA = SPLIT_GD_BETA + GD_HEADS
SPLIT_GATES = SPLIT_GD_ALPHA + GD_HEADS
IN_COLS = SPLIT_GATES + 2 * D_MODEL

kernel_name = "rwkv7_gdn_gated_parallel_block"


def rms_norm(x, g, eps=NORM_EPS):
    xf = x.astype(jnp.float32)
    y = xf * lax.rsqrt(jnp.mean(xf * xf, axis=-1, keepdims=True) + eps)
    return (y * g.astype(jnp.float32)).astype(x.dtype)


def l2_normalize(x, eps=1e-12):
    xf = x.astype(jnp.float32)
    return (xf * lax.rsqrt(jnp.sum(xf * xf, axis=-1, keepdims=True) + eps)).astype(x.dtype)


def token_shift(p, mu):
    prev = jnp.pad(p, ((0, 0), (1, 0), (0, 0)))[:, :-1]
    return p + (prev - p) * mu


def causal_depthwise_conv(x, w):
    K, C = w.shape
    return lax.conv_general_dilated(
        x, w[:, None, :].astype(x.dtype), window_strides=(1,), padding=[(K - 1, 0)],
        dimension_numbers=("NWC", "WIO", "NWC"), feature_group_count=C)


def rwkv7_recurrence(r, w, k, v, kk, b):
    f32 = jnp.float32
    B, T, H, N = r.shape

    def step(S, inp):
        r_t, w_t, k_t, v_t, kk_t, b_t = inp
        sa = jnp.einsum("bhvk,bhk->bhv", S, -kk_t)
        S = S * w_t[:, :, None, :] + sa[..., :, None] * b_t[..., None, :] + v_t[..., :, None] * k_t[..., None, :]
        return S, jnp.einsum("bhvk,bhk->bhv", S, r_t)

    xs = tuple(jnp.moveaxis(t.astype(f32), 1, 0) for t in (r, w, k, v, kk, b))
    _, y = lax.scan(step, jnp.zeros((B, H, N, N), f32), xs)
    return jnp.moveaxis(y, 0, 1)


def rwkv7_branch(p_rw, z_rw, mu, w0, w2, a0, a2, k_k, k_a, r_k, gn_w, gn_b):
    B, T, _ = p_rw.shape
    xs = token_shift(p_rw, mu)
    r, k, v, wd, ad = jnp.split(
        xs, [RW_WIDTH, 2 * RW_WIDTH, 3 * RW_WIDTH, 3 * RW_WIDTH + RW_DECAY_RANK], axis=-1)
    log_w = -jax.nn.softplus(-(w0 + jnp.tanh(wd) @ w2)) - 0.5
    decay = jnp.exp(-jnp.exp(log_w.astype(jnp.float32)))
    a = jax.nn.sigmoid(a0 + ad @ a2)
    heads = lambda t: t.reshape(B, T, RW_HEADS, RW_HEAD)
    kk = l2_normalize(heads(k * k_k))
    k = k * (1 + (a - 1) * k_a)
    r_h, k_h, v_h, a_h, w_h = heads(r), heads(k), heads(v), heads(a), heads(decay)
    y = rwkv7_recurrence(r_h, w_h, k_h, v_h, kk, kk * a_h)
    mean = jnp.mean(y, axis=-1, keepdims=True)
    var = jnp.mean(jnp.square(y - mean), axis=-1, keepdims=True)
    y = ((y - mean) * lax.rsqrt(var + RW_GN_EPS)).reshape(B, T, RW_WIDTH) * gn_w + gn_b
    bonus = jnp.sum(r_h * k_h * r_k, axis=-1, keepdims=True) * v_h
    y = (y + bonus.reshape(B, T, RW_WIDTH)).astype(p_rw.dtype)
    return y * jax.nn.silu(z_rw)


def chunk_gated_delta_rule(q, k, v, g, beta):
    f32 = jnp.float32
    B, H, T, D = q.shape
    C = GD_CHUNK
    N = T // C
    chunks = lambda t: t.astype(f32).reshape((B, H, N, C) + t.shape[3:])
    q = chunks(q) * (D ** -0.5)
    k, v = chunks(k), chunks(v)
    g = jnp.cumsum(chunks(g), axis=-1)
    beta = chunks(beta)
    k_beta = k * beta[..., None]
    v_beta = v * beta[..., None]
    causal = jnp.tril(jnp.ones((C, C), bool))
    strict = jnp.tril(jnp.ones((C, C), bool), -1)
    decay = jnp.exp(jnp.where(causal, g[..., :, None] - g[..., None, :], -jnp.inf))
    eye = jnp.eye(C, dtype=f32)
    A = jnp.where(strict, jnp.einsum("bhnid,bhnjd->bhnij", k_beta, k) * decay, 0.0)
    t_inv = lax.linalg.triangular_solve(eye + A, jnp.broadcast_to(eye, A.shape),
                                        left_side=True, lower=True, unit_diagonal=True)
    u = t_inv @ v_beta
    w = t_inv @ (k_beta * jnp.exp(g)[..., None])
    qk = jnp.einsum("bhnid,bhnjd->bhnij", q, k) * decay
    q_decayed = q * jnp.exp(g)[..., None]
    g_last = g[..., -1]
    k_to_end = k * jnp.exp(g_last[..., None] - g)[..., None]

    def step(S, inp):
        qd_c, kte_c, u_c, w_c, qk_c, gl_c = inp
        v_new = u_c - w_c @ S
        o = qd_c @ S + qk_c @ v_new
        S = S * jnp.exp(gl_c)[..., None, None] + jnp.einsum("bhck,bhcv->bhkv", kte_c, v_new)
        return S, o

    xs = tuple(jnp.moveaxis(t, 2, 0) for t in (q_decayed, k_to_end, u, w, qk, g_last))
    _, o = lax.scan(step, jnp.zeros((B, H, D, v.shape[-1]), f32), xs)
    return jnp.moveaxis(o, 0, 2).reshape(B, H, T, -1)


def gdn_branch(qkv, z, beta_logit, alpha, conv_w, A_log, dt_bias, o_norm_w):
    B, T, _ = qkv.shape
    f32 = jnp.float32
    qkv = jax.nn.silu(causal_depthwise_conv(qkv, conv_w))
    q, k, v = jnp.split(qkv, 3, axis=-1)
    heads = lambda t: jnp.swapaxes(t.reshape(B, T, GD_HEADS, GD_HEAD), 1, 2)
    q, k, v = l2_normalize(heads(q)), l2_normalize(heads(k)), heads(v)
    beta = jnp.swapaxes(jax.nn.sigmoid(beta_logit.astype(f32)), 1, 2)
    g = -jnp.exp(A_log.astype(f32)) * jax.nn.softplus(alpha.astype(f32) + dt_bias.astype(f32))
    g = jnp.swapaxes(g, 1, 2)
    o = jnp.swapaxes(chunk_gated_delta_rule(q, k, v, g, beta), 1, 2)
    o = rms_norm(o, o_norm_w) * jax.nn.silu(z.reshape(B, T, GD_HEADS, GD_HEAD).astype(f32))
    return o.reshape(B, T, GD_WIDTH).astype(qkv.dtype)


def setup_inputs(seed: int = 0) -> dict:
    key = jax.random.key(seed)
    ks = jax.random.split(key, 24)
    f32 = jnp.float32
    L = DEPTH
    nrm = lambda k, shape, scale: jax.random.normal(k, shape, f32) * scale
    x = nrm(ks[0], (BATCH, SEQ, D_MODEL), 1.0)
    norm_in_w = 1.0 + nrm(ks[1], (L, D_MODEL), 0.02)
    w_in = nrm(ks[2], (L, D_MODEL, IN_COLS), D_MODEL ** -0.5)
    rw_mu = jax.random.uniform(ks[3], (L, RW_SHIFT_COLS), f32)
    rw_w0 = jax.random.uniform(ks[4], (L, RW_WIDTH), f32, -5.0, 1.0)
    rw_w2 = nrm(ks[5], (L, RW_DECAY_RANK, RW_WIDTH), 0.1)
    rw_a0 = nrm(ks[6], (L, RW_WIDTH), 0.1)
    rw_a2 = nrm(ks[7], (L, RW_AAA_RANK, RW_WIDTH), 0.1)
    rw_k_k = 0.85 + nrm(ks[8], (L, RW_WIDTH), 0.02)
    rw_k_a = 1.0 + nrm(ks[9], (L, RW_WIDTH), 0.02)
    rw_r_k = nrm(ks[10], (L, RW_HEADS, RW_HEAD), 0.1)
    rw_gn_w = 1.0 + nrm(ks[11], (L, RW_WIDTH), 0.02)
    rw_gn_b = nrm(ks[12], (L, RW_WIDTH), 0.02)
    gd_conv_w = nrm(ks[13], (L, GD_CONV, 3 * GD_WIDTH), GD_CONV ** -0.5)
    gd_A_log = jnp.log(jax.random.uniform(ks[14], (L, GD_HEADS), f32, 1.0, 16.0))
    dt = jnp.exp(jax.random.uniform(ks[15], (L, GD_HEADS), f32, math.log(1e-3), math.log(1e-1)))
    gd_dt_bias = dt + jnp.log(-jnp.expm1(-dt))
    gd_o_norm_w = 1.0 + nrm(ks[16], (L, GD_HEAD), 0.02)
    w_branch_a = nrm(ks[17], (L, RW_WIDTH, D_MODEL), RW_WIDTH ** -0.5)
    w_branch_b = nrm(ks[18], (L, GD_WIDTH, D_MODEL), GD_WIDTH ** -0.5)
    w_out = nrm(ks[19], (L, D_MODEL, D_MODEL), D_MODEL ** -0.5)
    norm_out_w = 1.0 + nrm(ks[20], (D_MODEL,), 0.02)
    return {"x": x, "norm_in_w": norm_in_w, "w_in": w_in, "rw_mu": rw_mu, "rw_w0": rw_w0,
            "rw_w2": rw_w2, "rw_a0": rw_a0, "rw_a2": rw_a2, "rw_k_k": rw_k_k, "rw_k_a": rw_k_a,
            "rw_r_k": rw_r_k, "rw_gn_w": rw_gn_w, "rw_gn_b": rw_gn_b, "gd_conv_w": gd_conv_w,
            "gd_A_log": gd_A_log, "gd_dt_bias": gd_dt_bias, "gd_o_norm_w": gd_o_norm_w,
            "w_branch_a": w_branch_a, "w_branch_b": w_branch_b, "w_out": w_out,
            "norm_out_w": norm_out_w}


def reference(x, norm_in_w, w_in, rw_mu, rw_w0, rw_w2, rw_a0, rw_a2, rw_k_k, rw_k_a, rw_r_k,
              rw_gn_w, rw_gn_b, gd_conv_w, gd_A_log, gd_dt_bias, gd_o_norm_w,
              w_branch_a, w_branch_b, w_out, norm_out_w):
    for l in range(DEPTH):
        h = rms_norm(x, norm_in_w[l])
        p = h @ w_in[l]
        p_rw, z_rw, qkv_gd, z_gd, beta_gd, alpha_gd, gates = jnp.split(
            p, [SPLIT_RW_Z, SPLIT_GD_QKV, SPLIT_GD_Z, SPLIT_GD_BETA, SPLIT_GD_ALPHA, SPLIT_GATES],
            axis=-1)
        y_a = rwkv7_branch(p_rw, z_rw, rw_mu[l], rw_w0[l], rw_w2[l], rw_a0[l], rw_a2[l],
                           rw_k_k[l], rw_k_a[l], rw_r_k[l], rw_gn_w[l], rw_gn_b[l])
        y_b = gdn_branch(qkv_gd, z_gd, beta_gd, alpha_gd, gd_conv_w[l], gd_A_log[l],
                         gd_dt_bias[l], gd_o_norm_w[l])
        gate_a, gate_b = jnp.split(gates, 2, axis=-1)
        merged = (jax.nn.sigmoid(gate_a) * (y_a @ w_branch_a[l])
                  + jax.nn.sigmoid(gate_b) * (y_b @ w_branch_b[l]))
        x = x + merged @ w_out[l]
    return rms_norm(x, norm_out_w)
```

```python
import math
import numpy as np
import concourse.bass as bass
import concourse.mybir as mybir
from concourse.bass_utils import run_bass_kernel_spmd

F32 = mybir.dt.float32
BF16 = mybir.dt.bfloat16
ALU = mybir.AluOpType
AF = mybir.ActivationFunctionType


class Op:
    __slots__ = ("eng", "fn", "boxes_r", "boxes_w", "deps", "sig", "cnt", "idx",
                 "dsem", "dcnt", "dprev", "alldeps", "cost", "rows", "succ", "prio", "nin", "rt", "fin", "tag", "st", "alts", "wsz", "psum", "vc", "pos", "edeps")

    def __init__(self, eng, fn):
        self.eng = eng
        self.fn = fn
        self.deps = set()
        self.alldeps = set()
        self.cost = 0.3
        self.rows = None
        self.alts = None
        self.sig = False
        self.cnt = 0
        self.dsem = -1
        self.dcnt = 0
        self.dprev = None


def _box(ap):
    t = ap.tensor
    name = t.name
    pat = ap.ap
    off = ap.offset
    sp = str(ap.space)
    if "PSUM" in sp.upper():
        return (name, 0, 128, 0, 1 << 40)
    if "SB" in sp.upper():
        shp = t.shape
        F = 1
        for s in shp[1:]:
            F *= s
        p0 = off // F
        f0 = off % F
        npart = pat[0][1]
        ext = 1
        for st, c in pat[1:]:
            ext += (c - 1) * abs(st)
        return (name, p0, p0 + npart, f0, f0 + ext)
    ext = 1
    for st, c in pat:
        ext += (c - 1) * abs(st)
    return (name, 0, 1, off, off + ext)


def _fsize(ap):
    n = 1
    for st, c in ap.ap[1:]:
        n *= c
    return n


def _ovl(a, b):
    return a[1] < b[2] and b[1] < a[2] and a[3] < b[4] and b[3] < a[4]


def _covers(a, b):
    return a[1] <= b[1] and a[2] >= b[2] and a[3] <= b[3] and a[4] >= b[4]


PE_STANDALONE_WAITS = False
TRANSITIVE = True
LAST_ONLY = True
SCHED_EPS = 0.1


class Prog:
    NDMA = 16

    def __init__(self, nc):
        self.nc = nc
        self.ops = []
        self.hist = {}
        self.engs = {"pe": nc.tensor, "act": nc.scalar, "dve": nc.vector,
                     "pool": nc.gpsimd, "sp": nc.sync}

    def add(self, eng, fn, reads, writes, dma=False):
        alts = None
        if isinstance(eng, tuple):
            alts, eng = eng, eng[0]
        op = Op(eng, fn)
        op.alts = alts
        op.idx = len(self.ops)
        op.tag = getattr(self, 'tag', '')
        op.dsem = 0 if dma else -1
        br = [_box(a) for a in reads]
        bw = [_box(a) for a in writes]
        bw = bw + [b for b in br if b[4] == (1 << 40)]
        br = [b for b in br if b[4] != (1 << 40)]
        for b in br:
            for (hb, hop, hw) in self.hist.get(b[0], ()):
                if hw and _ovl(hb, b):
                    self._dep(op, hop, raw=True)
        for b in bw:
            for (hb, hop, hw) in self.hist.get(b[0], ()):
                if _ovl(hb, b):
                    self._dep(op, hop, raw=False)
        for b in bw:
            lst = self.hist.setdefault(b[0], [])
            lst[:] = [e for e in lst if not _covers(b, e[0])]
            lst.append((b, op, True))
        for b in br:
            self.hist.setdefault(b[0], []).append((b, op, False))
        self.ops.append(op)
        wsz = _fsize(writes[0]) if writes else 64
        psum = any(b[4] == (1 << 40) for b in bw)
        op.wsz = wsz
        op.psum = psum
        if dma:
            op.cost = 2.0 + 0.004 * wsz
        else:
            op.cost = self._ecost(eng, wsz, psum, op.cost)
        return op

    @staticmethod
    def _ecost(eng, wsz, psum, default):
        if eng == "act":
            return 0.18 + 0.0007 * wsz
        if eng == "dve":
            return (0.2 if psum else 0.12) + 0.00065 * wsz
        if eng == "pool":
            return 0.12 + 0.0021 * wsz
        return default

    def schedule(self):
        ops = self.ops
        for o in ops:
            o.succ = []
        for o in ops:
            for d in o.alldeps:
                d.succ.append(o)
            o.nin = len(o.alldeps)
        for o in reversed(ops):
            p = 0.0
            for q in o.succ:
                if q.prio > p:
                    p = q.prio
            o.prio = p + o.cost
        free = {e: 0.0 for e in self.engs}
        ready = {e: [] for e in self.engs}
        def push(o):
            for e in (o.alts or (o.eng,)):
                ready[e].append(o)

        for o in ops:
            if o.nin == 0:
                o.rt = 0.0
                push(o)
        order = []
        n = len(ops)
        pe_rows = None

        def pe_pen(o):
            if o.eng != "pe" or o.rows is None or pe_rows is None or o.rows == pe_rows:
                return 0.0
            if pe_rows[1] <= o.rows[0] or o.rows[1] <= pe_rows[0]:
                return 0.3
            return 0.11

        while len(order) < n:
            best = None
            for e, lst in ready.items():
                if not lst:
                    continue
                t = free[e]
                cand = None
                for o in lst:
                    st = (o.rt if o.rt > t else t) + pe_pen(o)
                    if o.alts:
                        ce = self._ecost(e, o.wsz, o.psum, o.cost)
                        st += ce - min(self._ecost(a, o.wsz, o.psum, o.cost) for a in o.alts)
                    key = (int(st / SCHED_EPS), -o.prio, o.idx, st)
                    if cand is None or key < cand[0]:
                        cand = (key, o, e)
                if best is None or cand[0] < best[0]:
                    best = cand
            key, o, e = best
            if o.alts:
                for a in o.alts:
                    ready[a].remove(o)
                t = free[e]
                o.eng = e
                o.cost = self._ecost(e, o.wsz, o.psum, o.cost)
                st = o.rt if o.rt > t else t
            else:
                st = key[3]
                ready[o.eng].remove(o)
            if o.eng == "pe" and o.rows is not None:
                pe_rows = o.rows
            if o.dsem >= 0:
                free[o.eng] = st + 0.06
            else:
                free[o.eng] = st + o.cost
            o.fin = st + o.cost
            o.st = st
            order.append(o)
            for q in o.succ:
                q.nin -= 1
                if q.nin == 0:
                    rt = 0.0
                    for d in q.alldeps:
                        lat = d.fin + (0.05 if d.eng == q.eng else 0.25)
                        if lat > rt:
                            rt = lat
                    q.rt = rt
                    push(q)
        self.ops = order
        self.est = max(o.fin for o in order)

    def _dep(self, op, src, raw):
        if src is op:
            return
        op.alldeps.add(src)
        if src.eng == op.eng and src.dsem < 0:
            if op.eng == "pe":
                return
        op.deps.add(src)

    def emit(self, final_wait=True):
        nc = self.nc
        names = ["pe", "act", "dve", "pool"]
        sems = {e: nc.alloc_semaphore("s_" + e) for e in names}
        dsems = [nc.alloc_semaphore("s_dma%d" % i) for i in range(self.NDMA)]
        last = None
        for op in self.ops:
            if op.eng == "pe" and op.rows is not None:
                if last is not None and (last.rows[1] <= op.rows[0] or op.rows[1] <= last.rows[0]):
                    op.deps.add(last)
                last = op
        for i, op in enumerate(self.ops):
            op.pos = i
        for op in self.ops:
            last = {}
            eff = []
            for d in op.deps:
                if d.dsem >= 0 or not LAST_ONLY:
                    eff.append(d)
                elif d.eng not in last or d.pos > last[d.eng].pos:
                    last[d.eng] = d
            eff.extend(last.values())
            op.edeps = eff
            for d in eff:
                d.sig = True
        cnt = {e: 0 for e in names}
        dcnt = [0] * self.NDMA
        dlast = [None] * self.NDMA
        ndma = 0
        for op in self.ops:
            if op.dsem >= 0:
                s = ndma % self.NDMA
                ndma += 1
                op.dsem = s
                dcnt[s] += 16
                op.dcnt = dcnt[s]
                op.dprev = dlast[s]
                dlast[s] = op
            elif op.sig:
                cnt[op.eng] += 1
                op.cnt = cnt[op.eng]
        waited = {e: {} for e in list(self.engs)}
        nwait = 0
        for op in self.ops:
            need = {}
            deps = list(op.edeps)
            if op.dprev is not None:
                deps.append(op.dprev)
            for d in deps:
                if d.dsem >= 0:
                    key = ("d", d.dsem)
                    val = d.dcnt
                else:
                    key = ("e", d.eng)
                    val = d.cnt
                if val > need.get(key, (0, None))[0]:
                    need[key] = (val, d)
            eng = self.engs[op.eng]
            w = waited[op.eng]
            todo = []
            for key, (val, d) in sorted(need.items(), key=lambda kv: -kv[1][1].idx if TRANSITIVE else 0):
                if val > w.get(key, 0):
                    w[key] = val
                    if TRANSITIVE:
                        for k2, v2 in d.vc.items():
                            if v2 > w.get(k2, 0):
                                w[k2] = v2
                    sem = dsems[key[1]] if key[0] == "d" else sems[key[1]]
                    todo.append((sem, val))
            if TRANSITIVE and (op.dsem >= 0 or op.sig):
                op.vc = dict(w)
                if op.dsem >= 0:
                    op.vc[("d", op.dsem)] = op.dcnt
                else:
                    op.vc[("e", op.eng)] = op.cnt
            standalone = todo if (op.eng == "pe" and PE_STANDALONE_WAITS) else todo[1:]
            for sem, val in standalone:
                eng.wait_ge(sem, val)
                nwait += 1
            ins = op.fn(eng)
            if todo and standalone is not todo:
                ins._wait_ge(todo[0][0], todo[0][1])
                nwait += 1
            if op.dsem >= 0:
                ins.then_inc(dsems[op.dsem], 16)
            elif op.sig:
                ins.then_inc(sems[op.eng], 1)
        if final_wait:
            sp = self.engs["sp"]
            for s in range(self.NDMA):
                if dcnt[s] > 0:
                    sp.wait_ge(dsems[s], dcnt[s])
        self.stats = dict(nops=len(self.ops), nwait=nwait,
                          nsig=sum(cnt.values()), ndma=ndma)

    def dma(self, out, in_, eng="sp"):
        return self.add(eng, lambda e, o=out, i=in_: e.dma_start(out=o, in_=i),
                        [in_], [out], dma=True)

    def _pe_rows(self, op, lhsT):
        b0 = lhsT.base_partition()
        op.rows = (b0, b0 + lhsT.ap[0][1])
        return op

    def mm(self, out, lhsT, rhs, start=True, stop=True):
        op = self.add("pe", lambda e, o=out, l=lhsT, r=rhs, s=start, t=stop:
                      e.matmul(o, l, r, start=s, stop=t), [lhsT, rhs], [out])
        n = _fsize(rhs)
        op.cost = (0.02 + 0.00085 * max(n, 64)) * (4.0 if rhs.dtype == F32 else 1.0)
        return self._pe_rows(op, lhsT)

    def tr(self, out, in_, ident):
        op = self.add("pe", lambda e, o=out, i=in_, d=ident: e.transpose(o, i, d),
                      [in_, ident], [out])
        op.cost = 0.09
        return self._pe_rows(op, in_)

    def act(self, out, in_, func, bias=None, scale=None, accum=None, eng="act"):
        reads = [in_]
        kw = {}
        if bias is not None:
            kw["bias"] = bias
            if not isinstance(bias, (int, float)):
                reads.append(bias)
        if scale is not None:
            kw["scale"] = scale
            if not isinstance(scale, (int, float)):
                reads.append(scale)
        writes = [out]
        if accum is not None:
            kw["accum_out"] = accum
            writes.append(accum)
        return self.add(eng, lambda e, o=out, i=in_, f=func, k=kw: e.activation(o, i, f, **k),
                        reads, writes)

    def tt(self, eng, out, in0, in1, op):
        return self.add(eng, lambda e, o=out, a=in0, b=in1, p=op: e.tensor_tensor(o, a, b, p),
                        [in0, in1], [out])

    def ts(self, eng, out, in0, s1, op0, s2=None, op1=None, accum=None):
        reads = [in0]
        if not isinstance(s1, (int, float)):
            reads.append(s1)
        if s2 is not None and not isinstance(s2, (int, float)):
            reads.append(s2)
        kw = {}
        if op1 is not None:
            kw["op1"] = op1
        writes = [out]
        if accum is not None:
            kw["accum_out"] = accum
            writes.append(accum)
        return self.add(eng, lambda e, o=out, a=in0, x=s1, y=s2, p=op0, k=kw:
                        e.tensor_scalar(o, a, x, y, p, **k), reads, writes)

    def stt(self, eng, out, in0, scalar, in1, op0, op1):
        reads = [in0, in1]
        if not isinstance(scalar, (int, float)):
            reads.append(scalar)
        return self.add(eng, lambda e, o=out, a=in0, s=scalar, b=in1, p=op0, q=op1:
                        e.scalar_tensor_tensor(o, a, s, b, p, q), reads, [out])

    def copy(self, eng, out, in_):
        def fn(e, o=out, i=in_):
            return e.copy(o, i) if e is self.engs["act"] else e.tensor_copy(o, i)
        return self.add(eng, fn, [in_], [out])

    def memset(self, eng, out, val):
        return self.add(eng, lambda e, o=out, v=val: e.memset(o, v), [], [out])


NCORES = 8
SEQ = 2048
DM = 1024
NSEQ = 2
NTOK = NSEQ * SEQ
TB = 128
NBLK = NTOK // TB
BPS = SEQ // TB
C = 64
INC = 6280
CDEC = math.exp(-0.5)

CV_G, CV_MU, CV_W0, CV_A0, CV_KK, CV_KA, CV_RK, CV_GNW, CV_GNB, CV_CONV, CV_ONW = \
    0, 8, 21, 25, 29, 33, 37, 41, 45, 49, 97
NCV = 98
PD = ("pool", "dve")
DP = ("dve", "pool")
AD = ("act", "dve")
DA = ("dve", "act")
SCHED = True
HORD = (0, 2, 4, 6, 1, 3, 5, 7)


def v3(ap, a, b):
    return ap.rearrange("p (a b) -> p a b", a=a, b=b)


class Arena:
    def __init__(self, nc, name, n, dtype, ap2d=None):
        self.t = ap2d if ap2d is not None else nc.alloc_sbuf_tensor(name, [128, n], dtype)
        self.n = n
        self.off = 0
        self.name = name

    def room(self):
        return self.n - self.off

    def reset(self, off=0):
        self.off = off

    def take(self, *shape, parts=128):
        n = 1
        for s in shape:
            n *= s
        assert self.off + n <= self.n, (self.name, self.off, n, self.n)
        ap = self.t[0:parts, self.off:self.off + n]
        self.off += n
        if len(shape) == 2:
            ap = ap.rearrange("p (a b) -> p a b", a=shape[0], b=shape[1])
        elif len(shape) == 3:
            ap = ap.rearrange("p (a b c) -> p a b c", a=shape[0], b=shape[1], c=shape[2])
        return ap


class Multi:
    def __init__(self, *arenas):
        self.arenas = arenas

    def take(self, *shape, parts=128):
        n = 1
        for s_ in shape:
            n *= s_
        for a in self.arenas:
            if a.room() >= n:
                return a.take(*shape, parts=parts)
        raise AssertionError(("out of scratch", shape, [a.room() for a in self.arenas]))


def build_nc(phases=(1, 2, 3), nblk=NBLK, dbg=None, trunc=None):
    nc = bass.Bass("TRN2", target_bir_lowering=False)
    P = Prog(nc)
    dt = lambda name, shape, kind="ExternalInput": nc.dram_tensor(name, shape, F32, kind=kind).ap()
    x_d = dt("x", [NTOK, DM])
    win_d = dt("w_in", [DM, INC])
    wa_d = dt("w_a", [512, DM])
    wb_d = dt("w_b", [512, DM])
    wo_d = dt("w_o", [DM, DM])
    w2_d = dt("w2", [64, 512])
    a2_d = dt("a2", [64, 512])
    cv_d = dt("cv", [128, NCV])
    nwo_d = dt("nwo", [1, DM])
    gv_d = dt("gvec", [1, 8])
    id_d = dt("ident", [128, 128])
    bo_d = dt("bo", [128, 128])
    mk_d = dt("masks", [64, 3, 64])
    rm_d = dt("resetm", [128, 128])
    mk128_d = dt("masks128", [128, 3, 128])
    out_d = dt("out", [NTOK, DM], kind="ExternalOutput")
    dbg_d = {}
    if dbg:
        for k, shp in dbg.items():
            dbg_d[k] = dt("dbg_" + k, shp, kind="ExternalOutput")

    sb = lambda name, shape, dtype=F32: nc.alloc_sbuf_tensor("s_" + name, shape, dtype)
    ytok = NTOK if not dbg else max(nblk * TB, 1024)
    YA = sb("YA", [128, 4, ytok], BF16)
    YB = sb("YB", [128, 4, ytok], BF16)
    tapbuf = sb("tapbuf", [128, 1024]) if dbg else None
    tapped = set()

    def tap(name, ap, parts=128):
        if not dbg or name not in dbg_d or name in tapped:
            return
        tapped.add(name)
        n = dbg[name][1]
        P.copy(DA, tapbuf[0:parts, 0:n], ap)
        P.dma(dbg_d[name], tapbuf[0:parts, 0:n])

    xbuf = [sb("xbuf%d" % i, [128, DM]) for i in range(2)]
    hT = [sb("hT%d" % i, [128, 8, TB + 3], BF16) for i in range(2)]
    xn = sb("xn", [128, DM], BF16)
    ss = sb("ss", [128, 1])
    rstd = sb("rstd", [128, 1])
    ss2 = sb("ss2", [128, 1])
    ss3 = sb("ss3", [128, 1])
    ss23 = sb("ss23", [128, 1])
    rstd3 = sb("rstd3", [128, 1])
    identb = sb("identb", [128, 128], BF16)
    bob = sb("bob", [128, 128], BF16)
    bo64b = sb("bo64b", [128, 128], BF16)
    onesb = sb("onesb", [128, 128], BF16)
    ones128b = sb("ones128b", [128, 128], BF16)
    onesdb = sb("onesdb", [128, 128], BF16)
    mk128b = sb("mk128b", [128, 3, 128], BF16)
    maskSI128 = sb("maskSI128", [128, 256], BF16)
    resetm = sb("resetm", [128, 128])
    cv = sb("cv", [128, NCV])
    omm = sb("omm", [128, 13])
    negh = sb("negh", [128, 1])
    cvn = sb("cvn", [128, 8])
    gvb128 = sb("gvb128", [128, 8])
    negA128 = sb("negA128", [128, 4])
    Mst = sb("Mst", [128, 4, 128])
    Mb = sb("Mb", [128, 4, 128], BF16)
    Sst = sb("Sst", [128, 4, 128])
    Sb = sb("Sb", [128, 4, 128], BF16)
    WA = Arena(nc, "WA", 32768, BF16)
    AFa = Arena(nc, "AFa", 4500, F32)
    ABa = Arena(nc, "ABa", 18660, BF16)
    if dbg:
        YBa = Arena(nc, "XTR", 16384, BF16)
    else:
        YBa = Arena(nc, "YBa", 4 * ytok, BF16, ap2d=YB[:, :, :].rearrange("p a b -> p (a b)"))

    ptp = nc.alloc_psum_tensor("ptp", [128, 1024], BF16)
    pbig = [nc.alloc_psum_tensor("pbig%d" % i, [128, 512], F32) for i in range(2)]
    psm = [nc.alloc_psum_tensor("psm%d" % i, [128, 512], F32) for i in range(4)]
    plong = nc.alloc_psum_tensor("plong", [128, 512], F32)
    rr = {"big": 0, "small": 0, 0: 0, 1: 0}

    def big():
        rr["big"] += 1
        return pbig[rr["big"] % 2]

    def big3():
        return big()

    def small(c=None):
        if c is None:
            rr["small"] += 1
            return psm[rr["small"] % 4]
        rr[c] += 1
        return psm[2 * c + rr[c] % 2]

    engrr = {"n": 0}

    def anyeng(choices=("dve", "pool")):
        engrr["n"] += 1
        return choices[engrr["n"] % len(choices)]

    identf = xbuf[0][:, 0:128]
    bof = xbuf[0][:, 128:256]
    P.dma(identf, id_d)
    P.dma(bof, bo_d)
    P.dma(resetm[:], rm_d)
    P.dma(cv[:], cv_d)
    P.dma(gvb128[:], gv_d.partition_broadcast(128))
    P.copy(DA, identb[:], identf)
    P.copy(DA, bob[:], bof)
    P.ts("dve", bo64b[:], bof, 1.0 / 64, ALU.mult)
    P.memset("pool", onesb[:], 1.0)
    P.memset("pool", ones128b[:], 128.0)
    P.memset("pool", onesdb[:], 1.0 / 128)
    for q_ in range(3):
        P.dma(xbuf[1][:, q_ * 128:(q_ + 1) * 128], mk128_d[:, q_, :])
    P.copy(DA, mk128b[:, :, :], v3(xbuf[1][:, 0:384], 3, 128))
    P.copy(DA, maskSI128[:, 0:128], xbuf[1][:, 128:256])
    P.copy(DA, maskSI128[:, 128:256], xbuf[1][:, 0:128])
    P.ts("dve", omm[:], cv[:, CV_MU:CV_MU + 13], -1.0, ALU.mult, 1.0, ALU.add)
    P.memset("pool", negh[:], -0.5)
    P.ts("dve", cvn[:, 0:8], cv[:, CV_W0:CV_W0 + 8], -1.0, ALU.mult)
    P.act(negA128[:], gvb128[:, 0:4], AF.Exp)
    P.ts("dve", negA128[:], negA128[:], -1.0, ALU.mult)

    def cvc(col):
        return cv[:, col:col + 1]

    def cvnc(col):
        return cvn[:, col:col + 1]

    def rsqrt(out, in_, tmp, bias, scale=None, pool=False):
        if pool:
            P.act(tmp, in_, AF.Identity, bias=bias, scale=scale)
            P.tt("pool", out, tmp, negh[0:out.shape[0], 0:1].to_broadcast(list(out.shape)), ALU.pow)
        else:
            P.act(tmp, in_, AF.Ln, bias=bias, scale=scale)
            P.act(out, tmp, AF.Exp, scale=-0.5)

    def sigmoid(out, in_, tmp, scale=1.0, nbias=None):
        P.act(tmp, in_, AF.Exp, scale=-scale, bias=nbias)
        P.act(tmp, tmp, AF.Ln, bias=1.0)
        P.act(out, tmp, AF.Exp, scale=-1.0)

    castrr = {"n": 0}

    def load_w(dst3, src, c0, ncols, rowscale=None, p0=0, parts=128):
        ndc = dst3.shape[1]
        for dc in range(ndc):
            for q in range(0, ncols, 1024):
                n = min(1024, ncols - q)
                k = castrr["n"] % 6
                stg = xbuf[k] if k < 2 else AFa.t[:, (k - 2) * 1024:(k - 1) * 1024]
                castrr["n"] += 1
                P.dma(stg[p0:p0 + parts, 0:n], src[dc * parts:(dc + 1) * parts, c0 + q:c0 + q + n],
                      eng=("sp", "act")[castrr["n"] % 2])
                eng = ("dve", "act")[castrr["n"] % 2]
                o = dst3[:, dc, q:q + n]
                i = stg[p0:p0 + parts, 0:n]
                if rowscale is not None:
                    if eng == "act":
                        P.act(o, i, AF.Copy, scale=cvc(rowscale + dc))
                    else:
                        P.ts(eng, o, i, cvc(rowscale + dc), ALU.mult)
                else:
                    P.copy(eng, o, i)

    def stage0(blk, need_halo=True, pool=False, dest=None):
        tok0 = blk * TB
        xt = xbuf[blk % 2]
        cur = hT[blk % 2]
        prev = hT[(blk + 1) % 2]
        P.dma(xt[:], x_d[tok0:tok0 + TB, :])
        P.memset("pool", ss[:], 0.0)
        P.act(xn[:], xt[:], AF.Square, accum=ss[:])
        rsqrt(rstd[:], ss[:], ss2[:], 1e-6, scale=1.0 / DM, pool=pool)
        P.act(xn[:], xt[:], AF.Copy, scale=rstd[:, 0:1])
        for dc in range(8):
            P.tr(ptp[:, dc * 128:(dc + 1) * 128], xn[:, dc * 128:(dc + 1) * 128], identb[:])
        P.copy(DA, dest if dest is not None else cur[:, :, 3:3 + TB], v3(ptp[:, :], 8, 128))
        if need_halo:
            if blk % BPS == 0:
                P.memset("pool", cur[:, :, 0:3], 0.0)
            else:
                P.copy(PD, cur[:, :, 0:3], prev[:, :, TB:TB + 3])
        return xt, cur

    def inproj(W3, c0, cur, halo, ps):
        for dc in range(8):
            P.mm(ps[:, 0:TB + halo], W3[:, dc, c0:c0 + 128], cur[:, dc, 3 - halo:3 + TB],
                 start=(dc == 0), stop=(dc == 7))

    def neumann(nh, P0, Q0, bufs, c=None, n=64):
        Pn, Qn, Xn = bufs
        w = nh * n
        X = Xn[0]
        P.tt(DP, X, P0, identb[0:n, 0:n].unsqueeze(1).to_broadcast([n, nh, n]), ALU.add)
        Pc, Qc = P0, Q0
        for it in range(5):
            last = (it == 4)
            Pd, Qd, Xd = Pn[it % 2], Qn[it % 2], Xn[(it + 1) % 2]
            psQ = small(c)
            for h in range(nh):
                P.mm(psQ[0:n, h * n:(h + 1) * n], Pc[:, h, :], Qc[:, h, :])
            if not last:
                psP = small(c)
                for h in range(nh):
                    P.mm(psP[0:n, h * n:(h + 1) * n], Qc[:, h, :], Pc[:, h, :])
            P.copy(AD, Qd, v3(psQ[0:n, 0:w], nh, n))
            if not last:
                P.copy(DA, Pd, v3(psP[0:n, 0:w], nh, n))
            psX = small(c)
            for h in range(nh):
                P.mm(psX[0:n, h * n:(h + 1) * n], Qd[:, h, :], X[:, h, :])
            P.tt("dve", Xd, X, v3(psX[0:n, 0:w], nh, n), ALU.add)
            X = Xd
            Pc, Qc = Pd, Qd
        return X

    def phase1():
        WA.reset(); AFa.reset(); ABa.reset()
        W1 = WA.take(8, 2176)
        w2a2 = WA.take(512)
        Brk = WA.take(4, 128)
        load_w(W1, win_d, 0, 2176, rowscale=CV_G)
        P.dma(xbuf[0][0:64, 0:512], w2_d)
        P.dma(xbuf[0][64:128, 0:512], a2_d)
        P.copy(DA, w2a2[:, :], xbuf[0][:, 0:512])
        for m in range(4):
            P.ts(PD, Brk[:, m, :], bob[:], cvc(CV_RK + m), ALU.mult)
        SC = Multi(ABa, WA, YBa)
        YBa.reset()
        f = lambda: AFa.take(TB)
        mtemps = [[f() for _ in range(15)] for _ in range(2)]
        wdad = f()
        eGCs = [AFa.take(4, 2) for _ in range(2)]
        o_msq = AFa.take(4, TB)
        o_t1 = SC.take(4, TB); o_yg = SC.take(4, TB); o_yb = SC.take(4, TB)
        twad = SC.take(TB)
        sqbs = [SC.take(TB) for _ in range(2)]
        yTb = SC.take(4, TB)
        sqy = SC.take(4, TB)
        bsets = []
        for _ in range(2):
            bsets.append(dict(
                AR=SC.take(4, 2, TB),
                BK=SC.take(4, 2, TB),
                KBh=SC.take(4, 2, TB),
                vfm=SC.take(4, TB), siluz=SC.take(4, TB), rkb=SC.take(4, TB),
                TM=SC.take(4, 512)))
        psets = []
        for _ in range(2):
            psets.append(dict(
                SA=SC.take(8, 256), SK=SC.take(8, 256), Xn=[SC.take(8, 128) for _ in range(2)],
                AVb=SC.take(8, 64), U0=SC.take(8, 64), WtT=SC.take(4, 256), Ub=SC.take(8, 64),
                Yb=SC.take(512), Y2s=SC.take(512)))
        Q0 = SC.take(8, 128)
        Pn = [[SC.take(4, 128) for _ in range(2)] for _ in range(2)]
        Qn = [[SC.take(4, 128) for _ in range(2)] for _ in range(2)]
        tmp, rr_, kk_, sg, aa, rs, kkn, t1, kp, bb, Gp, eG, eGn, Dp, eD = mtemps[0]
        sqb_ = sqbs[0]

        def shift_evac(ps, j, out):
            P.act(tmp, ps[:, 1:TB + 1], AF.Copy, scale=omm[:, j:j + 1])
            P.stt("dve", out, ps[:, 0:TB], cvc(CV_MU + j), tmp, ALU.mult, ALU.add)

        for blk in range(nblk):
            P.tag = "b%d.s0" % blk
            xt, cur = stage0(blk)
            bs = bsets[blk % 2]
            AR, BK, KBh, vfm, siluz, rkb, TM = bs["AR"], bs["BK"], bs["KBh"], bs["vfm"], bs["siluz"], bs["rkb"], bs["TM"]
            eGC = eGCs[blk % 2]
            tmp, rr_, kk_, sg, aa, rs, kkn, t1, kp, bb, Gp, eG, eGn, Dp, eD = mtemps[0]
            if blk % BPS == 0:
                P.memset("pool", Mst[:], 0.0)
                P.memset("pool", Mb[:], 0.0)
            ps = big()
            inproj(W1, 1536, cur, 1, ps)
            shift_evac(ps, 12, wdad)
            sigmoid(tmp[0:64, :], wdad[0:64, :], rs[0:64, :], scale=2.0)
            P.ts(PD, twad[0:64, :], tmp[0:64, :], 2.0, ALU.mult, -1.0, ALU.add)
            P.copy(AD, twad[64:128, :], wdad[64:128, :])
            for m in range(4):
                tmp, rr_, kk_, sg, aa, rs, kkn, t1, kp, bb, Gp, eG, eGn, Dp, eD = mtemps[m % 2]
                sqb_ = sqbs[m % 2]
                P.tag = "b%d.A%d" % (blk, m)
                ps = big(); inproj(W1, m * 128, cur, 1, ps); shift_evac(ps, m, rr_)
                ps = big(); inproj(W1, 512 + m * 128, cur, 1, ps); shift_evac(ps, 4 + m, kk_)
                ps = big(); inproj(W1, 1024 + m * 128, cur, 1, ps); shift_evac(ps, 8 + m, vfm[:, m, :])
                ps = big(); inproj(W1, 1664 + m * 128, cur, 0, ps)
                P.act(aa, ps[:, 0:TB], AF.Copy)
                sigmoid(sg, ps[:, 0:TB], rs)
                P.tt(PD, siluz[:, m, :], sg, aa, ALU.mult)
                ps = big()
                P.mm(ps[:, 0:TB], w2a2[0:64, m * 128:(m + 1) * 128], twad[0:64, :])
                sigmoid(sg, ps[:, 0:TB], rs, nbias=cvnc(m))
                ps = big()
                P.mm(ps[:, 0:TB], w2a2[64:128, m * 128:(m + 1) * 128], twad[64:128, :])
                sigmoid(aa, ps[:, 0:TB], rs, nbias=cvnc(4 + m))
                P.act(sqb_, kk_, AF.Square, scale=cvc(CV_KK + m))
                ps = big()
                P.mm(ps[:, 0:TB], bob[:], sqb_)
                rsqrt(rs, ps[:, 0:TB], t1, 1e-12)
                P.stt("dve", kkn, kk_, cvc(CV_KK + m), rs, ALU.mult, ALU.mult)
                P.ts(PD, t1, aa, -1.0, ALU.add, cvc(CV_KA + m), ALU.mult)
                P.stt("dve", kp, t1, 1.0, kk_, ALU.add, ALU.mult)
                P.tt(PD, bb, kkn, aa, ALU.mult)
                P.add("dve", lambda e, o=Gp, a=resetm[:], b=sg: e.tensor_tensor_scan(
                    o, a, b, 0.0, ALU.mult, ALU.add), [resetm[:], sg], [Gp])
                P.act(eG, Gp, AF.Exp, scale=-CDEC)
                P.act(eGn, Gp, AF.Exp, scale=CDEC)
                Gp3 = v3(Gp, 2, 64)
                P.tt(DP, v3(Dp, 2, 64), Gp3, Gp3[:, :, 63:64].to_broadcast([128, 2, 64]), ALU.subtract)
                P.act(eD, Dp, AF.Exp, scale=CDEC)
                eG3 = v3(eG, 2, 64)
                kk3 = v3(kkn, 2, 64)
                At3 = v3(AR[:, m, 0, :], 2, 64)
                P.stt("dve", At3[:, :, 1:64], kk3[:, :, 1:64], -1.0, eG3[:, :, 0:63], ALU.mult, ALU.mult)
                P.ts("dve", At3[:, :, 0:1], kk3[:, :, 0:1], -1.0, ALU.mult)
                P.tt(PD, AR[:, m, 1, :], rr_, eG, ALU.mult)
                P.tt(PD, BK[:, m, 0, :], bb, eGn, ALU.mult)
                P.tt(DP, BK[:, m, 1, :], kp, eGn, ALU.mult)
                P.tt(PD, KBh[:, m, 0, :], kp, eD, ALU.mult)
                P.tt(DP, KBh[:, m, 1, :], bb, eD, ALU.mult)
                P.tt(PD, rkb[:, m, :], rr_, kp, ALU.mult)
                P.copy(DA, eGC[:, m, :], eG3[:, :, 63])
                tap("t_r", rr_); tap("t_k", kk_); tap("t_v", vfm[:, m, :]); tap("t_sg", sg); tap("t_a", aa)
                tap("t_kkn", kkn); tap("t_kp", kp); tap("t_Gp", Gp); tap("t_eG", eG); tap("t_At", AR[:, m, 0, :])
                tap("t_wdad", wdad); tap("t_eD", eD)
            pYT = plong
            pp = psets[blk % 2]
            SA, SK, Xn, AVb, U0, WtT, Ub, Yb, Y2s = (pp[k] for k in ("SA", "SK", "Xn", "AVb", "U0", "WtT", "Ub", "Yb", "Y2s"))
            P.tag = "b%d.Bpar0" % blk
            srcs = [lambda m: AR[:, m, 0, :], lambda m: KBh[:, m, 0, :], lambda m: KBh[:, m, 1, :], lambda m: vfm[:, m, :]]
            for kind in range(4):
                ps = small()
                for m in range(4):
                    P.mm(ps[:, m * 128:(m + 1) * 128], srcs[kind](m), identb[:, :])
                P.copy(AD if kind % 2 == 0 else DA, TM[:, kind, :], ps[:, :])
            msk = maskSI128[:, :].unsqueeze(1).to_broadcast([128, 2, 256])
            for g in range(4):
                psA = small(); psK = small()
                for e in (0, 1):
                    pb = e * 64
                    P.mm(psA[:, e * 256:(e + 1) * 256], BK[pb:pb + 64, g, 0, :], AR[pb:pb + 64, g, :, :])
                    P.mm(psK[:, e * 256:(e + 1) * 256], BK[pb:pb + 64, g, 1, :], AR[pb:pb + 64, g, :, :])
                P.tt("dve", SA[:, 2 * g:2 * g + 2, :], v3(psA[:, :], 2, 256), msk, ALU.mult)
                P.tt("dve", SK[:, 2 * g:2 * g + 2, :], v3(psK[:, :], 2, 256), msk, ALU.mult)
            for half in range(2):
                psQ = small()
                for hl in (0, 2, 1, 3):
                    h = half * 4 + hl
                    m, pb = h // 2, (h % 2) * 64
                    P.mm(psQ[:, hl * 128:(hl + 1) * 128], AR[pb:pb + 64, m, 0, :], BK[pb:pb + 64, m, 0, :])
                P.tt("dve", Q0[:, half * 4:(half + 1) * 4, :], v3(psQ[:, :], 4, 128),
                     mk128b[:, 2, :].unsqueeze(1).to_broadcast([128, 4, 128]), ALU.mult)
            for half in range(2):
                hs = slice(half * 4, (half + 1) * 4)
                neumann(4, SA[:, hs, 0:128], Q0[:, hs, :], (Pn[half], Qn[half], [Xn[0][:, hs, :], Xn[1][:, hs, :]]), None, n=128)
            X = Xn[1]
            ps = small()
            for h in range(8):
                P.mm(ps[:, h * 64:(h + 1) * 64], SK[:, h, 0:128], TM[:, 3, h * 64:(h + 1) * 64])
            P.copy(AD, AVb, v3(ps[:, :], 8, 64))
            ps = small()
            for h in range(8):
                P.mm(ps[:, h * 64:(h + 1) * 64], X[:, h, :], AVb[:, h, :])
            P.copy(AD, U0, v3(ps[:, :], 8, 64))
            for q2 in range(2):
                ps = small()
                for ml in range(2):
                    m = q2 * 2 + ml
                    P.mm(ps[:, ml * 256:(ml + 1) * 256], TM[:, 0, m * 128:(m + 1) * 128], X[:, 2 * m:2 * m + 2, :])
                P.copy(DA, WtT[:, q2 * 2:q2 * 2 + 2, :], v3(ps[:, :], 2, 256))
            for c in range(2):
                pc = c * 64
                rows = slice(pc, pc + 64)
                P.tag = "b%d.Bseq%d" % (blk, c)
                psU = small()
                psY2 = small()
                for h in HORD:
                    m, pb, e = h // 2, (h % 2) * 64, h % 2
                    P.mm(psU[:, h * 64:(h + 1) * 64], WtT[pb:pb + 64, m, e * 128:(e + 1) * 128],
                         Mb[pb:pb + 64, m, e * 64:(e + 1) * 64])
                    P.mm(psY2[:, h * 64:(h + 1) * 64], AR[pb:pb + 64, m, 1, :], Mb[pb:pb + 64, m, e * 64:(e + 1) * 64])
                P.tt("dve", Ub[rows], v3(psU[rows, :], 8, 64), U0[rows], ALU.add)
                P.copy(AD, Y2s[rows, :], psY2[rows, :])
                psY = small()
                for h in range(8):
                    o = psY[:, h * 64:(h + 1) * 64]
                    P.mm(o, SK[rows, h, 128:256], TM[rows, 3, h * 64:(h + 1) * 64], start=True, stop=False)
                    P.mm(o, SA[rows, h, 128:256], Ub[rows, h, :], start=False, stop=True)
                P.tt("dve", Yb[rows, :], psY[rows, :], Y2s[rows, :], ALU.add)
                psM = small()
                for m in range(4):
                    o = psM[:, m * 128:(m + 1) * 128]
                    P.mm(o, TM[rows, 1, m * 128:(m + 1) * 128], TM[rows, 3, m * 128:(m + 1) * 128], start=True, stop=False)
                    P.mm(o, TM[rows, 2, m * 128:(m + 1) * 128], Ub[rows, 2 * m:2 * m + 2, :], start=False, stop=True)
                P.tt(PD, Mst[:], Mst[:], eGC[:, :, c:c + 1].to_broadcast([128, 4, 128]), ALU.mult)
                P.tt("dve", Mst[:], Mst[:], v3(psM[:, :], 4, 128), ALU.add)
                P.copy(AD, Mb[:], Mst[:])
                for m in range(4):
                    P.mm(pYT[:, m * 128 + c * 64:m * 128 + (c + 1) * 64], Yb[rows, m * 128:(m + 1) * 128], identb[rows, pc:pc + 64])
            P.tag = "b%d.C" % blk
            P.copy(DA, yTb, v3(pYT[:, :], 4, TB))
            P.act(sqy, yTb, AF.Square)
            psm_ = small(0); pse_ = small(1)
            for m in range(4):
                P.mm(psm_[:, m * 128:(m + 1) * 128], bo64b[:], yTb[:, m, :])
            for m in range(4):
                P.mm(pse_[:, m * 128:(m + 1) * 128], bo64b[:], sqy[:, m, :])
            P.tt("dve", o_t1, yTb, v3(psm_[:, :], 4, TB), ALU.subtract)
            P.act(o_msq, v3(psm_[:, :], 4, TB), AF.Square)
            P.tt("dve", o_msq, v3(pse_[:, :], 4, TB), o_msq, ALU.subtract)
            P.ts("dve", o_msq, o_msq, 0.0, ALU.max)
            rsqrt(o_msq, o_msq, o_msq, 64e-5)
            P.tt(PD, o_t1, o_t1, o_msq, ALU.mult)
            for m in range(4):
                P.act(o_yg[:, m, :], o_t1[:, m, :], AF.Identity, scale=cvc(CV_GNW + m), bias=cvc(CV_GNB + m))
            psb_ = small(0)
            for m in range(4):
                P.mm(psb_[:, m * 128:(m + 1) * 128], Brk[:, m, :], rkb[:, m, :])
            P.tt("dve", o_yb, v3(psb_[:, :], 4, TB), vfm, ALU.mult)
            tap("t_yg", o_yg.rearrange("p a b -> p (a b)")); tap("t_yb", o_yb.rearrange("p a b -> p (a b)"))
            tap("t_yT", yTb.rearrange("p a b -> p (a b)"))
            P.tt(PD, o_yg, o_yg, o_yb, ALU.add)
            P.tt(PD, YA[:, :, blk * TB:(blk + 1) * TB], o_yg, siluz, ALU.mult)

    def phase2():
        WA.reset(); AFa.reset(); ABa.reset()
        W2 = WA.take(8, 2056)
        load_w(W2, win_d, 2176, 2056, rowscale=CV_G)
        SC = Multi(ABa, WA)
        f = lambda: AFa.take(TB)
        jtemps = [[f() for _ in range(4)] for _ in range(2)]
        sqbs = [SC.take(TB) for _ in range(2)]
        gTri = AFa.take(4, 128)
        E0 = AFa.take(4, 128)
        EMt = AFa.take(4, 128)
        o_t = AFa.take(4, TB)
        o_r = AFa.take(4, TB)
        tba = AFa.take(4)
        gg = AFa.take(4)
        gcs = AFa.take(4)
        dgl = AFa.take(4)
        mk128f = AFa.take(3, 128)
        ones128f = AFa.take(128)
        for q_ in range(3):
            P.dma(mk128f[:, q_, :], mk128_d[:, q_, :])
        P.memset("pool", ones128f, 1.0)
        oTb = SC.take(4, TB)
        sqo = SC.take(4, TB)
        bsets = []
        for _ in range(2):
            bsets.append(dict(
                beta=AFa.take(4), nbeta=AFa.take(4), egc=AFa.take(4), ekt=AFa.take(4), egl=AFa.take(8),
                QKfm=SC.take(4, 2, TB),
                vfm=SC.take(4, TB), siluz=SC.take(4, TB), qdT=SC.take(4, TB),
                EM=SC.take(4, 2, 128),
                kg=SC.take(4, 128), kte=SC.take(4, 128), vtm=SC.take(4, 128), PQ=SC.take(4, 256),
                Xn=[SC.take(4, 128) for _ in range(2)], WnT=SC.take(4, 128), vnew=SC.take(4, 128), ob=SC.take(512)))
        Q0 = SC.take(4, 128)
        Pn = [SC.take(4, 128) for _ in range(2)]
        Qn = [SC.take(4, 128) for _ in range(2)]

        for blk in range(nblk):
            xt, cur = stage0(blk)
            bs = bsets[blk % 2]
            beta, nbeta, egc, ekt, egl, QKfm, vfm, siluz, qdT, EM, kg, kte, vtm, PQ, Xn, WnT, vnew, ob = (bs[k] for k in (
                "beta", "nbeta", "egc", "ekt", "egl", "QKfm", "vfm", "siluz", "qdT", "EM", "kg", "kte", "vtm", "PQ",
                "Xn", "WnT", "vnew", "ob"))
            if blk % BPS == 0:
                P.memset("pool", Sst[:], 0.0)
                P.memset("pool", Sb[:], 0.0)
            for j in range(12):
                accA, accB, sv, rs = jtemps[j % 2]
                sqb_ = sqbs[j % 2]
                ps = big()
                inproj(W2, j * 128, cur, 3, ps)
                cw = lambda i: cvc(CV_CONV + i * 12 + j)
                P.act(accA, ps[:, 3:TB + 3], AF.Copy, scale=cw(3))
                P.stt("dve", accB, ps[:, 2:TB + 2], cw(2), accA, ALU.mult, ALU.add)
                P.stt("dve", accA, ps[:, 1:TB + 1], cw(1), accB, ALU.mult, ALU.add)
                P.stt("dve", accB, ps[:, 0:TB], cw(0), accA, ALU.mult, ALU.add)
                h = j % 4
                sigmoid(accA, accB, accA)
                if j >= 8:
                    P.tt(PD, vfm[:, h, :], accA, accB, ALU.mult)
                else:
                    P.tt(PD, sv, accA, accB, ALU.mult)
                    P.act(sqb_, sv, AF.Square)
                    ps2 = big()
                    P.mm(ps2[:, 0:TB], (ones128b if j < 4 else onesb)[:], sqb_)
                    eps = 128e-12 if j < 4 else 1e-12
                    rsqrt(rs, ps2[:, 0:TB], accA, eps)
                    P.tt(PD, QKfm[:, h, 1 if j < 4 else 0, :], sv, rs, ALU.mult)
            for h in range(4):
                accA, accB, sv, rs = jtemps[h % 2]
                ps = big()
                inproj(W2, 1536 + h * 128, cur, 0, ps)
                P.act(accB, ps[:, 0:TB], AF.Copy)
                sigmoid(accA, ps[:, 0:TB], accA)
                P.tt(PD, siluz[:, h, :], accA, accB, ALU.mult)
            psba = small()
            for dc in range(8):
                P.mm(psba[:, 0:8], cur[:, dc, 3:3 + TB], W2[:, dc, 2048:2056], start=(dc == 0), stop=(dc == 7))
            sigmoid(beta, psba[:, 0:4], beta)
            P.tt("dve", tba, psba[:, 4:8], gvb128[:, 4:8], ALU.add)
            P.act(tba, tba, AF.Exp)
            P.act(tba, tba, AF.Ln, bias=1.0)
            P.tt("dve", gg, tba, negA128[:, :], ALU.mult)
            P.ts(PD, nbeta, beta, -1.0, ALU.mult)
            ps = small()
            P.mm(ps[:, 0:4], mk128f[:, 0, :], gg)
            P.act(egc, ps[:, 0:4], AF.Exp)
            P.copy(AD, gcs, ps[:, 0:4])
            ps = small()
            for c in range(2):
                P.mm(ps[:, c * 4:(c + 1) * 4], ones128f[c * 64:(c + 1) * 64, :], gg[c * 64:(c + 1) * 64, :])
            P.act(egl, ps[:, 0:8], AF.Exp)
            for c in range(2):
                rows = slice(c * 64, (c + 1) * 64)
                P.tt("dve", dgl[rows, :], ps[rows, c * 4:(c + 1) * 4], gcs[rows, :], ALU.subtract)
            P.act(ekt, dgl, AF.Exp)
            P.tt(DP, gTri, mk128f[:, 0, :].unsqueeze(1).to_broadcast([128, 4, 128]),
                 gg.unsqueeze(2).to_broadcast([128, 4, 128]), ALU.mult)
            ps = small()
            P.mm(ps[:, :], mk128f[:, 2, :], gTri.rearrange("p a b -> p (a b)"))
            P.act(E0, v3(ps[:, :], 4, 128), AF.Exp)
            P.tt(PD, EM[:, :, 1, :], E0, mk128f[:, 0, :].unsqueeze(1).to_broadcast([128, 4, 128]), ALU.mult)
            P.tt(DP, EMt, E0, nbeta.unsqueeze(2).to_broadcast([128, 4, 128]), ALU.mult)
            P.tt(PD, EM[:, :, 0, :], EMt, mk128f[:, 1, :].unsqueeze(1).to_broadcast([128, 4, 128]), ALU.mult)
            P.tt(DP, gTri, identb[:, :].unsqueeze(1).to_broadcast([128, 4, 128]),
                 egc.unsqueeze(2).to_broadcast([128, 4, 128]), ALU.mult)
            ps = small()
            P.mm(ps[:, :], ones128f, gTri.rearrange("p a b -> p (a b)"))
            P.tt("dve", qdT, QKfm[:, :, 1, :], v3(ps[:, :], 4, 128), ALU.mult)
            pOT = plong
            ps = small()
            for h in range(4):
                P.mm(ps[:, h * 128:(h + 1) * 128], QKfm[:, h, 0, :], identb[:, :])
            P.tt("dve", kg, v3(ps[:, :], 4, 128), egc.unsqueeze(2).to_broadcast([128, 4, 128]), ALU.mult)
            P.tt("dve", kte, v3(ps[:, :], 4, 128), ekt.unsqueeze(2).to_broadcast([128, 4, 128]), ALU.mult)
            ps = small()
            for h in range(4):
                P.mm(ps[:, h * 128:(h + 1) * 128], vfm[:, h, :], identb[:, :])
            P.copy(AD, vtm, v3(ps[:, :], 4, 128))
            for q2 in range(2):
                ps = small()
                for hl in range(2):
                    h = q2 * 2 + hl
                    P.mm(ps[:, hl * 256:(hl + 1) * 256], QKfm[:, h, 0, :], QKfm[:, h, :, :])
                P.tt("dve", PQ[:, q2 * 2:q2 * 2 + 2, :], v3(ps[:, :], 2, 256),
                     EM[:, q2 * 2:q2 * 2 + 2, :, :].rearrange("p a b c -> p a (b c)"), ALU.mult)
            ps = small()
            for h in range(4):
                P.mm(ps[:, h * 128:(h + 1) * 128], PQ[:, h, 0:128], identb[:, :])
            P.copy(AD, Q0, v3(ps[:, :], 4, 128))
            X = neumann(4, PQ[:, :, 0:128], Q0, (Pn, Qn, Xn), None, n=128)
            ps = small()
            for h in range(4):
                P.mm(ps[:, h * 128:(h + 1) * 128], kg[:, h, :], X[:, h, :])
            P.ts("dve", WnT, v3(ps[:, :], 4, 128), -1.0, ALU.mult)
            for c in range(2):
                pc = c * 64
                rows = slice(pc, pc + 64)
                psV = small()
                for h in range(4):
                    o = psV[:, h * 128:(h + 1) * 128]
                    P.mm(o, X[rows, h, :], vtm[rows, h, :], start=True, stop=False)
                    P.mm(o, WnT[:, h, :], Sb[:, h, :], start=False, stop=True)
                P.tt("dve", vnew[rows], v3(psV[rows, :], 4, 128), beta[rows, :].unsqueeze(2).to_broadcast([64, 4, 128]), ALU.mult)
                psO = small()
                for h in range(4):
                    o = psO[:, h * 128:(h + 1) * 128]
                    P.mm(o, qdT[:, h, :], Sb[:, h, :], start=True, stop=False)
                    P.mm(o, PQ[rows, h, 128:256], vnew[rows, h, :], start=False, stop=True)
                P.copy(AD, ob[rows, :], psO[rows, :])
                psS = small()
                for h in range(4):
                    P.mm(psS[:, h * 128:(h + 1) * 128], kte[rows, h, :], vnew[rows, h, :])
                P.tt(PD, Sst[:], Sst[:], egl[:, c * 4:(c + 1) * 4].unsqueeze(2).to_broadcast([128, 4, 128]), ALU.mult)
                P.tt("dve", Sst[:], Sst[:], v3(psS[:, :], 4, 128), ALU.add)
                P.copy(AD, Sb[:], Sst[:])
                for h in range(4):
                    P.mm(pOT[:, h * 128 + c * 64:h * 128 + (c + 1) * 64], ob[rows, h * 128:(h + 1) * 128], identb[rows, pc:pc + 64])
            P.copy(DA, oTb, v3(pOT[:, :], 4, TB))
            P.act(sqo, oTb, AF.Square)
            ps = small(0)
            for h in range(4):
                P.mm(ps[:, h * 128:(h + 1) * 128], onesdb[:], sqo[:, h, :])
            rsqrt(o_r, v3(ps[:, :], 4, TB), o_t, 1e-6)
            P.tt(PD, o_t, oTb, o_r, ALU.mult)
            P.stt("dve", YB[:, :, blk * TB:(blk + 1) * TB], o_t, cvc(CV_ONW), siluz, ALU.mult, ALU.mult)

    def phase3():
        WA.reset(); AFa.reset(); ABa.reset()
        Wg = WA.take(8, 2048)
        Wa = WA.take(4, 1024)
        Wb = WA.take(4, 1024)
        Wo = WA.take(8, 1024)
        load_w(Wg, win_d, 4232, 2048, rowscale=CV_G)
        load_w(Wa, wa_d, 0, 1024)
        load_w(Wb, wb_d, 0, 1024)
        load_w(Wo, wo_d, 0, 1024)
        T3 = 4 * TB
        xrs = [AFa.take(DM) for _ in range(2)]
        xres = AFa.take(DM)
        nwbc = AFa.take(DM)
        hT3 = [ABa.take(8, T3) for _ in range(2)]
        mgs = [ABa.take(8, T3) for _ in range(2)]
        sa = ABa.take(T3); sbb = ABa.take(T3)
        xn3 = ABa.take(2 * T3)
        t1 = xn3[:, 0:T3]; t2 = xn3[:, T3:2 * T3]
        P.dma(nwbc, nwo_d.partition_broadcast(128))
        for B in range(nblk // 4):
            h3 = hT3[B % 2]
            mg = mgs[B % 2]
            for t in range(4):
                stage0(B * 4 + t, need_halo=False, pool=True, dest=h3[:, :, t * TB:(t + 1) * TB])
            tok = slice(B * T3, (B + 1) * T3)
            for cc in range(8):
                psa = big3()
                for dc in range(8):
                    P.mm(psa[:, :], Wg[:, dc, cc * 128:(cc + 1) * 128], h3[:, dc, :], start=(dc == 0), stop=(dc == 7))
                psb = big3()
                for dc in range(8):
                    P.mm(psb[:, :], Wg[:, dc, 1024 + cc * 128:1024 + (cc + 1) * 128], h3[:, dc, :], start=(dc == 0), stop=(dc == 7))
                pya = small()
                for kc in range(4):
                    P.mm(pya[:, :], Wa[:, kc, cc * 128:(cc + 1) * 128], YA[:, kc, tok], start=(kc == 0), stop=(kc == 3))
                pyb = small()
                for kc in range(4):
                    P.mm(pyb[:, :], Wb[:, kc, cc * 128:(cc + 1) * 128], YB[:, kc, tok], start=(kc == 0), stop=(kc == 3))
                P.act(sa, psa[:, :], AF.Sigmoid)
                P.act(sbb, psb[:, :], AF.Sigmoid)
                P.tt("dve", t1, pya[:, :], sa, ALU.mult)
                P.tt("dve", t2, pyb[:, :], sbb, ALU.mult)
                P.tt(PD, mg[:, cc, :], t1, t2, ALU.add)
            for t in range(4):
                blk = B * 4 + t
                xr = xrs[blk % 2]
                rows = slice(blk * TB, (blk + 1) * TB)
                P.dma(xres, x_d[rows, :])
                for half in range(2):
                    pso = big3()
                    for kc in range(8):
                        P.mm(pso[:, :], mg[:, kc, t * TB:(t + 1) * TB], Wo[:, kc, half * 512:(half + 1) * 512],
                             start=(kc == 0), stop=(kc == 7))
                    P.tt("dve", xr[:, half * 512:(half + 1) * 512], pso[:, :], xres[:, half * 512:(half + 1) * 512], ALU.add)
                P.memset("pool", ss3[:], 0.0)
                P.act(xn3, xr, AF.Square, accum=ss3[:])
                rsqrt(rstd3[:], ss3[:], ss23[:], 1e-6, scale=1.0 / DM, pool=True)
                P.stt("dve", xr, xr, rstd3[:, 0:1], nwbc, ALU.mult, ALU.mult)
                P.dma(out_d[rows, :], xr)

    if 1 in phases:
        phase1()
    if 2 in phases:
        phase2()
    if 3 in phases:
        phase3()
    if dbg:
        if "YA" in dbg_d:
            stg = AFa.t
            for m in range(4):
                for q in range(0, dbg["YA"][2], 1024):
                    n = min(1024, dbg["YA"][2] - q)
                    P.copy(DA, stg[:, 0:n], YA[:, m, q:q + n])
                    P.dma(dbg_d["YA"][:, m, q:q + n], stg[:, 0:n])
        if "YB" in dbg_d:
            stg = AFa.t
            for m in range(4):
                for q in range(0, dbg["YB"][2], 1024):
                    n = min(1024, dbg["YB"][2] - q)
                    P.copy(DA, stg[:, 0:n], YB[:, m, q:q + n])
                    P.dma(dbg_d["YB"][:, m, q:q + n], stg[:, 0:n])
    if trunc:
        P.ops = P.ops[:trunc]
    if SCHED:
        P.schedule()
    P.emit()
    return nc, P


def host_inputs(inputs):
    f = lambda a: np.ascontiguousarray(np.asarray(a, dtype=np.float32))
    x = f(inputs["x"])
    vec4 = lambda v: f(v).reshape(-1, 128).T
    cw = f(inputs["gd_conv_w"])[0]
    cols = [vec4(inputs["norm_in_w"][0]), vec4(inputs["rw_mu"][0]), vec4(inputs["rw_w0"][0]),
            vec4(inputs["rw_a0"][0]), vec4(inputs["rw_k_k"][0]), vec4(inputs["rw_k_a"][0]),
            vec4(f(inputs["rw_r_k"])[0].reshape(-1)), vec4(inputs["rw_gn_w"][0]), vec4(inputs["rw_gn_b"][0])]
    cols += [vec4(cw[i]) for i in range(4)]
    cols += [f(inputs["gd_o_norm_w"])[0].reshape(128, 1)]
    cvh = np.ascontiguousarray(np.concatenate(cols, axis=1))
    assert cvh.shape == (128, NCV), cvh.shape
    p = np.arange(128)
    bo = (p[:, None] // 64 == p[None, :] // 64).astype(np.float32)
    q = np.arange(64)
    masks = np.stack([(q[:, None] <= q[None, :]), (q[:, None] < q[None, :]), (q[:, None] > q[None, :])],
                     axis=1).astype(np.float32)
    resetm = np.ones((128, 128), np.float32)
    resetm[:, 0] = 0.0
    resetm[:, 64] = 0.0
    shared = {
        "w_in": f(inputs["w_in"])[0], "w_a": f(inputs["w_branch_a"])[0], "w_b": f(inputs["w_branch_b"])[0],
        "w_o": f(inputs["w_out"])[0], "w2": f(inputs["rw_w2"])[0], "a2": f(inputs["rw_a2"])[0],
        "cv": cvh, "nwo": f(inputs["norm_out_w"]).reshape(1, DM),
        "gvec": np.concatenate([f(inputs["gd_A_log"])[0], f(inputs["gd_dt_bias"])[0]]).reshape(1, 8),
        "ident": np.eye(128, dtype=np.float32), "bo": bo, "masks": np.ascontiguousarray(masks), "resetm": resetm,
        "masks128": np.ascontiguousarray(np.kron(np.eye(2, dtype=np.float32)[:, None, :], masks).astype(np.float32)),
    }
    in_maps = []
    for c in range(NCORES):
        m = dict(shared)
        m["x"] = np.ascontiguousarray(x[NSEQ * c:NSEQ * (c + 1)].reshape(NTOK, DM))
        in_maps.append(m)
    return in_maps


def kernel(**inputs):
    in_maps = host_inputs(inputs)
    nc, _ = build_nc()
    res = run_bass_kernel_spmd(nc, in_maps, core_ids=list(range(NCORES)))
    outs = [np.asarray(r["out"], dtype=np.float32).reshape(NSEQ, SEQ, DM) for r in res.results]
    return np.concatenate(outs, axis=0)
```

```python
import math
import numpy as np
import concourse.bass as bass
import concourse.mybir as mybir
from concourse.bass_utils import run_bass_kernel_spmd

F32 = mybir.dt.float32
BF16 = mybir.dt.bfloat16
ALU = mybir.AluOpType
AF = mybir.ActivationFunctionType


class Op:
    __slots__ = ("eng", "fn", "boxes_r", "boxes_w", "deps", "sig", "cnt", "idx",
                 "dsem", "dcnt", "dprev", "alldeps", "cost", "rows", "succ", "prio", "nin", "rt", "fin", "tag", "st", "alts", "wsz", "psum", "vc", "pos", "edeps")

    def __init__(self, eng, fn):
        self.eng = eng
        self.fn = fn
        self.deps = set()
        self.alldeps = set()
        self.cost = 0.3
        self.rows = None
        self.alts = None
        self.sig = False
        self.cnt = 0
        self.dsem = -1
        self.dcnt = 0
        self.dprev = None


def _box(ap):
    t = ap.tensor
    name = t.name
    pat = ap.ap
    off = ap.offset
    sp = str(ap.space)
    if "PSUM" in sp.upper():
        return (name, 0, 128, 0, 1 << 40)
    if "SB" in sp.upper():
        shp = t.shape
        F = 1
        for s in shp[1:]:
            F *= s
        p0 = off // F
        f0 = off % F
        npart = pat[0][1]
        ext = 1
        for st, c in pat[1:]:
            ext += (c - 1) * abs(st)
        return (name, p0, p0 + npart, f0, f0 + ext)
    ext = 1
    for st, c in pat:
        ext += (c - 1) * abs(st)
    return (name, 0, 1, off, off + ext)


def _fsize(ap):
    n = 1
    for st, c in ap.ap[1:]:
        n *= c
    return n


def _ovl(a, b):
    return a[1] < b[2] and b[1] < a[2] and a[3] < b[4] and b[3] < a[4]


def _covers(a, b):
    return a[1] <= b[1] and a[2] >= b[2] and a[3] <= b[3] and a[4] >= b[4]


PE_STANDALONE_WAITS = False
TRANSITIVE = True
LAST_ONLY = True
SCHED_EPS = 0.1


class Prog:
    NDMA = 16

    def __init__(self, nc):
        self.nc = nc
        self.ops = []
        self.hist = {}
        self.engs = {"pe": nc.tensor, "act": nc.scalar, "dve": nc.vector,
                     "pool": nc.gpsimd, "sp": nc.sync}

    def add(self, eng, fn, reads, writes, dma=False):
        alts = None
        if isinstance(eng, tuple):
            alts, eng = eng, eng[0]
        op = Op(eng, fn)
        op.alts = alts
        op.idx = len(self.ops)
        op.tag = getattr(self, 'tag', '')
        op.dsem = 0 if dma else -1
        br = [_box(a) for a in reads]
        bw = [_box(a) for a in writes]
        bw = bw + [b for b in br if b[4] == (1 << 40)]
        br = [b for b in br if b[4] != (1 << 40)]
        for b in br:
            for (hb, hop, hw) in self.hist.get(b[0], ()):
                if hw and _ovl(hb, b):
                    self._dep(op, hop, raw=True)
        for b in bw:
            for (hb, hop, hw) in self.hist.get(b[0], ()):
                if _ovl(hb, b):
                    self._dep(op, hop, raw=False)
        for b in bw:
            lst = self.hist.setdefault(b[0], [])
            lst[:] = [e for e in lst if not _covers(b, e[0])]
            lst.append((b, op, True))
        for b in br:
            self.hist.setdefault(b[0], []).append((b, op, False))
        self.ops.append(op)
        wsz = _fsize(writes[0]) if writes else 64
        psum = any(b[4] == (1 << 40) for b in bw)
        op.wsz = wsz
        op.psum = psum
        if dma:
            op.cost = 2.0 + 0.004 * wsz
        else:
            op.cost = self._ecost(eng, wsz, psum, op.cost)
        return op

    @staticmethod
    def _ecost(eng, wsz, psum, default):
        if eng == "act":
            return 0.2 + 0.00085 * wsz
        if eng == "dve":
            return (0.1 if psum else 0.07) + 0.00105 * wsz
        if eng == "pool":
            return 0.12 + 0.0021 * wsz
        return default

    def schedule(self):
        ops = self.ops
        for o in ops:
            o.succ = []
        for o in ops:
            for d in o.alldeps:
                d.succ.append(o)
            o.nin = len(o.alldeps)
        for o in reversed(ops):
            p = 0.0
            for q in o.succ:
                if q.prio > p:
                    p = q.prio
            o.prio = p + o.cost
        free = {e: 0.0 for e in self.engs}
        ready = {e: [] for e in self.engs}
        def push(o):
            for e in (o.alts or (o.eng,)):
                ready[e].append(o)

        for o in ops:
            if o.nin == 0:
                o.rt = 0.0
                push(o)
        order = []
        n = len(ops)
        pe_rows = None

        def pe_pen(o):
            if o.eng != "pe" or o.rows is None or pe_rows is None or o.rows == pe_rows:
                return 0.0
            if pe_rows[1] <= o.rows[0] or o.rows[1] <= pe_rows[0]:
                return 0.45
            return 0.2

        while len(order) < n:
            best = None
            for e, lst in ready.items():
                if not lst:
                    continue
                t = free[e]
                cand = None
                for o in lst:
                    st = (o.rt if o.rt > t else t) + pe_pen(o)
                    if o.alts:
                        ce = self._ecost(e, o.wsz, o.psum, o.cost)
                        st += ce - min(self._ecost(a, o.wsz, o.psum, o.cost) for a in o.alts)
                    key = (int(st / SCHED_EPS), -o.prio, o.idx, st)
                    if cand is None or key < cand[0]:
                        cand = (key, o, e)
                if best is None or cand[0] < best[0]:
                    best = cand
            key, o, e = best
            if o.alts:
                for a in o.alts:
                    ready[a].remove(o)
                t = free[e]
                o.eng = e
                o.cost = self._ecost(e, o.wsz, o.psum, o.cost)
                st = o.rt if o.rt > t else t
            else:
                st = key[3]
                ready[o.eng].remove(o)
            if o.eng == "pe" and o.rows is not None:
                pe_rows = o.rows
            if o.dsem >= 0:
                free[o.eng] = st + 0.06
            else:
                free[o.eng] = st + o.cost
            o.fin = st + o.cost
            o.st = st
            order.append(o)
            for q in o.succ:
                q.nin -= 1
                if q.nin == 0:
                    rt = 0.0
                    for d in q.alldeps:
                        lat = d.fin + (0.05 if d.eng == q.eng else 0.25)
                        if lat > rt:
                            rt = lat
                    q.rt = rt
                    push(q)
        self.ops = order
        self.est = max(o.fin for o in order)

    def _dep(self, op, src, raw):
        if src is op:
            return
        op.alldeps.add(src)
        if src.eng == op.eng and src.dsem < 0:
            if op.eng == "pe":
                return
        op.deps.add(src)

    def emit(self, final_wait=True):
        nc = self.nc
        names = ["pe", "act", "dve", "pool"]
        sems = {e: nc.alloc_semaphore("s_" + e) for e in names}
        dsems = [nc.alloc_semaphore("s_dma%d" % i) for i in range(self.NDMA)]
        last = None
        for op in self.ops:
            if op.eng == "pe" and op.rows is not None:
                if last is not None and (last.rows[1] <= op.rows[0] or op.rows[1] <= last.rows[0]):
                    op.deps.add(last)
                last = op
        for i, op in enumerate(self.ops):
            op.pos = i
        for op in self.ops:
            last = {}
            eff = []
            for d in op.deps:
                if d.dsem >= 0 or not LAST_ONLY:
                    eff.append(d)
                elif d.eng not in last or d.pos > last[d.eng].pos:
                    last[d.eng] = d
            eff.extend(last.values())
            op.edeps = eff
            for d in eff:
                d.sig = True
        cnt = {e: 0 for e in names}
        dcnt = [0] * self.NDMA
        dlast = [None] * self.NDMA
        ndma = 0
        for op in self.ops:
            if op.dsem >= 0:
                s = ndma % self.NDMA
                ndma += 1
                op.dsem = s
                dcnt[s] += 16
                op.dcnt = dcnt[s]
                op.dprev = dlast[s]
                dlast[s] = op
            elif op.sig:
                cnt[op.eng] += 1
                op.cnt = cnt[op.eng]
        waited = {e: {} for e in list(self.engs)}
        nwait = 0
        for op in self.ops:
            need = {}
            deps = list(op.edeps)
            if op.dprev is not None:
                deps.append(op.dprev)
            for d in deps:
                if d.dsem >= 0:
                    key = ("d", d.dsem)
                    val = d.dcnt
                else:
                    key = ("e", d.eng)
                    val = d.cnt
                if val > need.get(key, (0, None))[0]:
                    need[key] = (val, d)
            eng = self.engs[op.eng]
            w = waited[op.eng]
            todo = []
            for key, (val, d) in sorted(need.items(), key=lambda kv: -kv[1][1].idx if TRANSITIVE else 0):
                if val > w.get(key, 0):
                    w[key] = val
                    if TRANSITIVE:
                        for k2, v2 in d.vc.items():
                            if v2 > w.get(k2, 0):
                                w[k2] = v2
                    sem = dsems[key[1]] if key[0] == "d" else sems[key[1]]
                    todo.append((sem, val))
            if TRANSITIVE and (op.dsem >= 0 or op.sig):
                op.vc = dict(w)
                if op.dsem >= 0:
                    op.vc[("d", op.dsem)] = op.dcnt
                else:
                    op.vc[("e", op.eng)] = op.cnt
            standalone = todo if (op.eng == "pe" and PE_STANDALONE_WAITS) else todo[1:]
            for sem, val in standalone:
                eng.wait_ge(sem, val)
                nwait += 1
            ins = op.fn(eng)
            if todo and standalone is not todo:
                ins._wait_ge(todo[0][0], todo[0][1])
                nwait += 1
            if op.dsem >= 0:
                ins.then_inc(dsems[op.dsem], 16)
            elif op.sig:
                ins.then_inc(sems[op.eng], 1)
        if final_wait:
            sp = self.engs["sp"]
            for s in range(self.NDMA):
                if dcnt[s] > 0:
                    sp.wait_ge(dsems[s], dcnt[s])
        self.stats = dict(nops=len(self.ops), nwait=nwait,
                          nsig=sum(cnt.values()), ndma=ndma)

    def dma(self, out, in_, eng="sp"):
        return self.add(eng, lambda e, o=out, i=in_: e.dma_start(out=o, in_=i),
                        [in_], [out], dma=True)

    def _pe_rows(self, op, lhsT):
        b0 = lhsT.base_partition()
        op.rows = (b0, b0 + lhsT.ap[0][1])
        return op

    def mm(self, out, lhsT, rhs, start=True, stop=True):
        op = self.add("pe", lambda e, o=out, l=lhsT, r=rhs, s=start, t=stop:
                      e.matmul(o, l, r, start=s, stop=t), [lhsT, rhs], [out])
        n = _fsize(rhs)
        op.cost = (0.02 + 0.00085 * max(n, 64)) * (4.0 if rhs.dtype == F32 else 1.0)
        return self._pe_rows(op, lhsT)

    def tr(self, out, in_, ident):
        op = self.add("pe", lambda e, o=out, i=in_, d=ident: e.transpose(o, i, d),
                      [in_, ident], [out])
        op.cost = 0.09
        return self._pe_rows(op, in_)

    def act(self, out, in_, func, bias=None, scale=None, accum=None, eng="act"):
        reads = [in_]
        kw = {}
        if bias is not None:
            kw["bias"] = bias
            if not isinstance(bias, (int, float)):
                reads.append(bias)
        if scale is not None:
            kw["scale"] = scale
            if not isinstance(scale, (int, float)):
                reads.append(scale)
        writes = [out]
        if accum is not None:
            kw["accum_out"] = accum
            writes.append(accum)
        return self.add(eng, lambda e, o=out, i=in_, f=func, k=kw: e.activation(o, i, f, **k),
                        reads, writes)

    def tt(self, eng, out, in0, in1, op):
        return self.add(eng, lambda e, o=out, a=in0, b=in1, p=op: e.tensor_tensor(o, a, b, p),
                        [in0, in1], [out])

    def ts(self, eng, out, in0, s1, op0, s2=None, op1=None, accum=None):
        reads = [in0]
        if not isinstance(s1, (int, float)):
            reads.append(s1)
        if s2 is not None and not isinstance(s2, (int, float)):
            reads.append(s2)
        kw = {}
        if op1 is not None:
            kw["op1"] = op1
        writes = [out]
        if accum is not None:
            kw["accum_out"] = accum
            writes.append(accum)
        return self.add(eng, lambda e, o=out, a=in0, x=s1, y=s2, p=op0, k=kw:
                        e.tensor_scalar(o, a, x, y, p, **k), reads, writes)

    def stt(self, eng, out, in0, scalar, in1, op0, op1):
        reads = [in0, in1]
        if not isinstance(scalar, (int, float)):
            reads.append(scalar)
        return self.add(eng, lambda e, o=out, a=in0, s=scalar, b=in1, p=op0, q=op1:
                        e.scalar_tensor_tensor(o, a, s, b, p, q), reads, [out])

    def copy(self, eng, out, in_):
        def fn(e, o=out, i=in_):
            return e.copy(o, i) if e is self.engs["act"] else e.tensor_copy(o, i)
        return self.add(eng, fn, [in_], [out])

    def memset(self, eng, out, val):
        return self.add(eng, lambda e, o=out, v=val: e.memset(o, v), [], [out])


NCORES = 8
SEQ = 2048
DM = 1024
NSEQ = 2
NTOK = NSEQ * SEQ
TB = 128
NBLK = NTOK // TB
BPS = SEQ // TB
C = 64
INC = 6280
CDEC = math.exp(-0.5)

CV_G, CV_MU, CV_W0, CV_A0, CV_KK, CV_KA, CV_RK, CV_GNW, CV_GNB, CV_CONV, CV_ONW = \
    0, 8, 21, 25, 29, 33, 37, 41, 45, 49, 97
NCV = 98
PD = ("pool", "dve")
DP = ("dve", "pool")
AD = ("act", "dve")
DA = ("dve", "act")
SCHED = True
HORD = (0, 2, 4, 6, 1, 3, 5, 7)


def v3(ap, a, b):
    return ap.rearrange("p (a b) -> p a b", a=a, b=b)


class Arena:
    def __init__(self, nc, name, n, dtype, ap2d=None):
        self.t = ap2d if ap2d is not None else nc.alloc_sbuf_tensor(name, [128, n], dtype)
        self.n = n
        self.off = 0
        self.name = name

    def room(self):
        return self.n - self.off

    def reset(self, off=0):
        self.off = off

    def take(self, *shape, parts=128):
        n = 1
        for s in shape:
            n *= s
        assert self.off + n <= self.n, (self.name, self.off, n, self.n)
        ap = self.t[0:parts, self.off:self.off + n]
        self.off += n
        if len(shape) == 2:
            ap = ap.rearrange("p (a b) -> p a b", a=shape[0], b=shape[1])
        elif len(shape) == 3:
            ap = ap.rearrange("p (a b c) -> p a b c", a=shape[0], b=shape[1], c=shape[2])
        return ap


class Multi:
    def __init__(self, *arenas):
        self.arenas = arenas

    def take(self, *shape, parts=128):
        n = 1
        for s_ in shape:
            n *= s_
        for a in self.arenas:
            if a.room() >= n:
                return a.take(*shape, parts=parts)
        raise AssertionError(("out of scratch", shape, [a.room() for a in self.arenas]))


def build_nc(phases=(1, 2, 3), nblk=NBLK, dbg=None, trunc=None):
    nc = bass.Bass("TRN2", target_bir_lowering=False)
    P = Prog(nc)
    dt = lambda name, shape, kind="ExternalInput": nc.dram_tensor(name, shape, F32, kind=kind).ap()
    x_d = dt("x", [NTOK, DM])
    win_d = dt("w_in", [DM, INC])
    wa_d = dt("w_a", [512, DM])
    wb_d = dt("w_b", [512, DM])
    wo_d = dt("w_o", [DM, DM])
    w2_d = dt("w2", [64, 512])
    a2_d = dt("a2", [64, 512])
    cv_d = dt("cv", [128, NCV])
    nwo_d = dt("nwo", [1, DM])
    gv_d = dt("gvec", [1, 8])
    id_d = dt("ident", [128, 128])
    bo_d = dt("bo", [128, 128])
    mk_d = dt("masks", [64, 3, 64])
    rm_d = dt("resetm", [128, 128])
    mk128_d = dt("masks128", [128, 3, 128])
    out_d = dt("out", [NTOK, DM], kind="ExternalOutput")
    dbg_d = {}
    if dbg:
        for k, shp in dbg.items():
            dbg_d[k] = dt("dbg_" + k, shp, kind="ExternalOutput")

    sb = lambda name, shape, dtype=F32: nc.alloc_sbuf_tensor("s_" + name, shape, dtype)
    ytok = NTOK if not dbg else max(nblk * TB, 1024)
    YA = sb("YA", [128, 4, ytok], BF16)
    YB = sb("YB", [128, 4, ytok], BF16)
    tapbuf = sb("tapbuf", [128, 1024]) if dbg else None
    tapped = set()

    def tap(name, ap, parts=128):
        if not dbg or name not in dbg_d or name in tapped:
            return
        tapped.add(name)
        n = dbg[name][1]
        P.copy(DA, tapbuf[0:parts, 0:n], ap)
        P.dma(dbg_d[name], tapbuf[0:parts, 0:n])

    xbuf = [sb("xbuf%d" % i, [128, DM]) for i in range(2)]
    hT = [sb("hT%d" % i, [128, 8, TB + 3], BF16) for i in range(2)]
    xn = sb("xn", [128, DM], BF16)
    ss = sb("ss", [128, 1])
    rstd = sb("rstd", [128, 1])
    ss2 = sb("ss2", [128, 1])
    ss3 = sb("ss3", [128, 1])
    ss23 = sb("ss23", [128, 1])
    rstd3 = sb("rstd3", [128, 1])
    identb = sb("identb", [128, 128], BF16)
    bob = sb("bob", [128, 128], BF16)
    bo64b = sb("bo64b", [128, 128], BF16)
    onesb = sb("onesb", [128, 128], BF16)
    ones128b = sb("ones128b", [128, 128], BF16)
    onesdb = sb("onesdb", [128, 128], BF16)
    mk128b = sb("mk128b", [128, 3, 128], BF16)
    maskSI128 = sb("maskSI128", [128, 256], BF16)
    resetm = sb("resetm", [128, 128])
    cv = sb("cv", [128, NCV])
    omm = sb("omm", [128, 13])
    negh = sb("negh", [128, 1])
    cvn = sb("cvn", [128, 8])
    gvb128 = sb("gvb128", [128, 8])
    negA128 = sb("negA128", [128, 4])
    Mst = sb("Mst", [128, 4, 128])
    Mb = sb("Mb", [128, 4, 128], BF16)
    Sst = sb("Sst", [128, 4, 128])
    Sb = sb("Sb", [128, 4, 128], BF16)
    WA = Arena(nc, "WA", 32768, BF16)
    AFa = Arena(nc, "AFa", 4500, F32)
    ABa = Arena(nc, "ABa", 18660, BF16)
    if dbg:
        YBa = Arena(nc, "XTR", 16384, BF16)
    else:
        YBa = Arena(nc, "YBa", 4 * ytok, BF16, ap2d=YB[:, :, :].rearrange("p a b -> p (a b)"))

    ptp = nc.alloc_psum_tensor("ptp", [128, 1024], BF16)
    pbig = [nc.alloc_psum_tensor("pbig%d" % i, [128, 512], F32) for i in range(2)]
    psm = [nc.alloc_psum_tensor("psm%d" % i, [128, 512], F32) for i in range(4)]
    plong = nc.alloc_psum_tensor("plong", [128, 512], F32)
    rr = {"big": 0, "small": 0, 0: 0, 1: 0}

    def big():
        rr["big"] += 1
        return pbig[rr["big"] % 2]

    def big3():
        return big()

    def small(c=None):
        if c is None:
            rr["small"] += 1
            return psm[rr["small"] % 4]
        rr[c] += 1
        return psm[2 * c + rr[c] % 2]

    engrr = {"n": 0}

    def anyeng(choices=("dve", "pool")):
        engrr["n"] += 1
        return choices[engrr["n"] % len(choices)]

    identf = xbuf[0][:, 0:128]
    bof = xbuf[0][:, 128:256]
    P.dma(identf, id_d)
    P.dma(bof, bo_d)
    P.dma(resetm[:], rm_d)
    P.dma(cv[:], cv_d)
    P.dma(gvb128[:], gv_d.partition_broadcast(128))
    P.copy(DA, identb[:], identf)
    P.copy(DA, bob[:], bof)
    P.ts("dve", bo64b[:], bof, 1.0 / 64, ALU.mult)
    P.memset("pool", onesb[:], 1.0)
    P.memset("pool", ones128b[:], 128.0)
    P.memset("pool", onesdb[:], 1.0 / 128)
    for q_ in range(3):
        P.dma(xbuf[1][:, q_ * 128:(q_ + 1) * 128], mk128_d[:, q_, :])
    P.copy(DA, mk128b[:, :, :], v3(xbuf[1][:, 0:384], 3, 128))
    P.copy(DA, maskSI128[:, 0:128], xbuf[1][:, 128:256])
    P.copy(DA, maskSI128[:, 128:256], xbuf[1][:, 0:128])
    P.ts("dve", omm[:], cv[:, CV_MU:CV_MU + 13], -1.0, ALU.mult, 1.0, ALU.add)
    P.memset("pool", negh[:], -0.5)
    P.ts("dve", cvn[:, 0:8], cv[:, CV_W0:CV_W0 + 8], -1.0, ALU.mult)
    P.act(negA128[:], gvb128[:, 0:4], AF.Exp)
    P.ts("dve", negA128[:], negA128[:], -1.0, ALU.mult)

    def cvc(col):
        return cv[:, col:col + 1]

    def cvnc(col):
        return cvn[:, col:col + 1]

    def rsqrt(out, in_, tmp, bias, scale=None, pool=False):
        if pool:
            P.act(tmp, in_, AF.Identity, bias=bias, scale=scale)
            P.tt("pool", out, tmp, negh[0:out.shape[0], 0:1].to_broadcast(list(out.shape)), ALU.pow)
        else:
            P.act(tmp, in_, AF.Ln, bias=bias, scale=scale)
            P.act(out, tmp, AF.Exp, scale=-0.5)

    def sigmoid(out, in_, tmp, scale=1.0, nbias=None):
        P.act(tmp, in_, AF.Exp, scale=-scale, bias=nbias)
        P.act(tmp, tmp, AF.Ln, bias=1.0)
        P.act(out, tmp, AF.Exp, scale=-1.0)

    castrr = {"n": 0}

    def load_w(dst3, src, c0, ncols, rowscale=None, p0=0, parts=128):
        ndc = dst3.shape[1]
        for dc in range(ndc):
            for q in range(0, ncols, 1024):
                n = min(1024, ncols - q)
                k = castrr["n"] % 6
                stg = xbuf[k] if k < 2 else AFa.t[:, (k - 2) * 1024:(k - 1) * 1024]
                castrr["n"] += 1
                P.dma(stg[p0:p0 + parts, 0:n], src[dc * parts:(dc + 1) * parts, c0 + q:c0 + q + n],
                      eng=("sp", "act")[castrr["n"] % 2])
                eng = ("dve", "act")[castrr["n"] % 2]
                o = dst3[:, dc, q:q + n]
                i = stg[p0:p0 + parts, 0:n]
                if rowscale is not None:
                    if eng == "act":
                        P.act(o, i, AF.Copy, scale=cvc(rowscale + dc))
                    else:
                        P.ts(eng, o, i, cvc(rowscale + dc), ALU.mult)
                else:
                    P.copy(eng, o, i)

    def stage0(blk, need_halo=True, pool=False, dest=None):
        tok0 = blk * TB
        xt = xbuf[blk % 2]
        cur = hT[blk % 2]
        prev = hT[(blk + 1) % 2]
        P.dma(xt[:], x_d[tok0:tok0 + TB, :])
        P.memset("pool", ss[:], 0.0)
        P.act(xn[:], xt[:], AF.Square, accum=ss[:])
        rsqrt(rstd[:], ss[:], ss2[:], 1e-6, scale=1.0 / DM, pool=pool)
        P.act(xn[:], xt[:], AF.Copy, scale=rstd[:, 0:1])
        for dc in range(8):
            P.tr(ptp[:, dc * 128:(dc + 1) * 128], xn[:, dc * 128:(dc + 1) * 128], identb[:])
        P.copy(DA, dest if dest is not None else cur[:, :, 3:3 + TB], v3(ptp[:, :], 8, 128))
        if need_halo:
            if blk % BPS == 0:
                P.memset("pool", cur[:, :, 0:3], 0.0)
            else:
                P.copy(PD, cur[:, :, 0:3], prev[:, :, TB:TB + 3])
        return xt, cur

    def inproj(W3, c0, cur, halo, ps):
        for dc in range(8):
            P.mm(ps[:, 0:TB + halo], W3[:, dc, c0:c0 + 128], cur[:, dc, 3 - halo:3 + TB],
                 start=(dc == 0), stop=(dc == 7))

    def neumann(nh, P0, Q0, bufs, c=None, n=64):
        Pn, Qn, Xn = bufs
        w = nh * n
        X = Xn[0]
        P.tt(DP, X, P0, identb[0:n, 0:n].unsqueeze(1).to_broadcast([n, nh, n]), ALU.add)
        Pc, Qc = P0, Q0
        for it in range(5):
            last = (it == 4)
            Pd, Qd, Xd = Pn[it % 2], Qn[it % 2], Xn[(it + 1) % 2]
            psQ = small(c)
            for h in range(nh):
                P.mm(psQ[0:n, h * n:(h + 1) * n], Pc[:, h, :], Qc[:, h, :])
            if not last:
                psP = small(c)
                for h in range(nh):
                    P.mm(psP[0:n, h * n:(h + 1) * n], Qc[:, h, :], Pc[:, h, :])
            P.copy(AD, Qd, v3(psQ[0:n, 0:w], nh, n))
            if not last:
                P.copy(DA, Pd, v3(psP[0:n, 0:w], nh, n))
            psX = small(c)
            for h in range(nh):
                P.mm(psX[0:n, h * n:(h + 1) * n], Qd[:, h, :], X[:, h, :])
            P.tt("dve", Xd, X, v3(psX[0:n, 0:w], nh, n), ALU.add)
            X = Xd
            Pc, Qc = Pd, Qd
        return X

    def phase1():
        WA.reset(); AFa.reset(); ABa.reset()
        W1 = WA.take(8, 2176)
        w2a2 = WA.take(512)
        Brk = WA.take(4, 128)
        load_w(W1, win_d, 0, 2176, rowscale=CV_G)
        P.dma(xbuf[0][0:64, 0:512], w2_d)
        P.dma(xbuf[0][64:128, 0:512], a2_d)
        P.copy(DA, w2a2[:, :], xbuf[0][:, 0:512])
        for m in range(4):
            P.ts(PD, Brk[:, m, :], bob[:], cvc(CV_RK + m), ALU.mult)
        SC = Multi(ABa, WA, YBa)
        YBa.reset()
        f = lambda: AFa.take(TB)
        mtemps = [[f() for _ in range(15)] for _ in range(2)]
        wdad = f()
        eGCs = [AFa.take(4, 2) for _ in range(2)]
        o_msq = AFa.take(4, TB)
        o_t1 = SC.take(4, TB); o_yg = SC.take(4, TB); o_yb = SC.take(4, TB)
        twad = SC.take(TB)
        sqbs = [SC.take(TB) for _ in range(2)]
        yTb = SC.take(4, TB)
        sqy = SC.take(4, TB)
        bsets = []
        for _ in range(2):
            bsets.append(dict(
                AR=SC.take(4, 2, TB),
                BK=SC.take(4, 2, TB),
                KBh=SC.take(4, 2, TB),
                vfm=SC.take(4, TB), siluz=SC.take(4, TB), rkb=SC.take(4, TB),
                TM=SC.take(4, 512)))
        psets = []
        for _ in range(2):
            psets.append(dict(
                SA=SC.take(8, 256), SK=SC.take(8, 256), Xn=[SC.take(8, 128) for _ in range(2)],
                AVb=SC.take(8, 64), U0=SC.take(8, 64), WtT=SC.take(4, 256), Ub=SC.take(8, 64),
                Yb=SC.take(512), Y2s=SC.take(512)))
        Q0 = SC.take(8, 128)
        Pn = [[SC.take(4, 128) for _ in range(2)] for _ in range(2)]
        Qn = [[SC.take(4, 128) for _ in range(2)] for _ in range(2)]
        tmp, rr_, kk_, sg, aa, rs, kkn, t1, kp, bb, Gp, eG, eGn, Dp, eD = mtemps[0]
        sqb_ = sqbs[0]

        def shift_evac(ps, j, out):
            P.act(tmp, ps[:, 1:TB + 1], AF.Copy, scale=omm[:, j:j + 1])
            P.stt("dve", out, ps[:, 0:TB], cvc(CV_MU + j), tmp, ALU.mult, ALU.add)

        for blk in range(nblk):
            P.tag = "b%d.s0" % blk
            xt, cur = stage0(blk)
            bs = bsets[blk % 2]
            AR, BK, KBh, vfm, siluz, rkb, TM = bs["AR"], bs["BK"], bs["KBh"], bs["vfm"], bs["siluz"], bs["rkb"], bs["TM"]
            eGC = eGCs[blk % 2]
            tmp, rr_, kk_, sg, aa, rs, kkn, t1, kp, bb, Gp, eG, eGn, Dp, eD = mtemps[0]
            if blk % BPS == 0:
                P.memset("pool", Mst[:], 0.0)
                P.memset("pool", Mb[:], 0.0)
            ps = big()
            inproj(W1, 1536, cur, 1, ps)
            shift_evac(ps, 12, wdad)
            sigmoid(tmp[0:64, :], wdad[0:64, :], rs[0:64, :], scale=2.0)
            P.ts(PD, twad[0:64, :], tmp[0:64, :], 2.0, ALU.mult, -1.0, ALU.add)
            P.copy(AD, twad[64:128, :], wdad[64:128, :])
            for m in range(4):
                tmp, rr_, kk_, sg, aa, rs, kkn, t1, kp, bb, Gp, eG, eGn, Dp, eD = mtemps[m % 2]
                sqb_ = sqbs[m % 2]
                P.tag = "b%d.A%d" % (blk, m)
                ps = big(); inproj(W1, m * 128, cur, 1, ps); shift_evac(ps, m, rr_)
                ps = big(); inproj(W1, 512 + m * 128, cur, 1, ps); shift_evac(ps, 4 + m, kk_)
                ps = big(); inproj(W1, 1024 + m * 128, cur, 1, ps); shift_evac(ps, 8 + m, vfm[:, m, :])
                ps = big(); inproj(W1, 1664 + m * 128, cur, 0, ps)
                P.act(aa, ps[:, 0:TB], AF.Copy)
                sigmoid(sg, ps[:, 0:TB], rs)
                P.tt(PD, siluz[:, m, :], sg, aa, ALU.mult)
                ps = big()
                P.mm(ps[:, 0:TB], w2a2[0:64, m * 128:(m + 1) * 128], twad[0:64, :])
                sigmoid(sg, ps[:, 0:TB], rs, nbias=cvnc(m))
                ps = big()
                P.mm(ps[:, 0:TB], w2a2[64:128, m * 128:(m + 1) * 128], twad[64:128, :])
                sigmoid(aa, ps[:, 0:TB], rs, nbias=cvnc(4 + m))
                P.act(sqb_, kk_, AF.Square, scale=cvc(CV_KK + m))
                ps = big()
                P.mm(ps[:, 0:TB], bob[:], sqb_)
                rsqrt(rs, ps[:, 0:TB], t1, 1e-12)
                P.stt("dve", kkn, kk_, cvc(CV_KK + m), rs, ALU.mult, ALU.mult)
                P.ts(PD, t1, aa, -1.0, ALU.add, cvc(CV_KA + m), ALU.mult)
                P.stt("dve", kp, t1, 1.0, kk_, ALU.add, ALU.mult)
                P.tt(PD, bb, kkn, aa, ALU.mult)
                P.add("dve", lambda e, o=Gp, a=resetm[:], b=sg: e.tensor_tensor_scan(
                    o, a, b, 0.0, ALU.mult, ALU.add), [resetm[:], sg], [Gp])
                P.act(eG, Gp, AF.Exp, scale=-CDEC)
                P.act(eGn, Gp, AF.Exp, scale=CDEC)
                Gp3 = v3(Gp, 2, 64)
                P.tt(DP, v3(Dp, 2, 64), Gp3, Gp3[:, :, 63:64].to_broadcast([128, 2, 64]), ALU.subtract)
                P.act(eD, Dp, AF.Exp, scale=CDEC)
                eG3 = v3(eG, 2, 64)
                kk3 = v3(kkn, 2, 64)
                At3 = v3(AR[:, m, 0, :], 2, 64)
                P.stt("dve", At3[:, :, 1:64], kk3[:, :, 1:64], -1.0, eG3[:, :, 0:63], ALU.mult, ALU.mult)
                P.ts("dve", At3[:, :, 0:1], kk3[:, :, 0:1], -1.0, ALU.mult)
                P.tt(PD, AR[:, m, 1, :], rr_, eG, ALU.mult)
                P.tt(PD, BK[:, m, 0, :], bb, eGn, ALU.mult)
                P.tt(DP, BK[:, m, 1, :], kp, eGn, ALU.mult)
                P.tt(PD, KBh[:, m, 0, :], kp, eD, ALU.mult)
                P.tt(DP, KBh[:, m, 1, :], bb, eD, ALU.mult)
                P.tt(PD, rkb[:, m, :], rr_, kp, ALU.mult)
                P.copy(DA, eGC[:, m, :], eG3[:, :, 63])
                tap("t_r", rr_); tap("t_k", kk_); tap("t_v", vfm[:, m, :]); tap("t_sg", sg); tap("t_a", aa)
                tap("t_kkn", kkn); tap("t_kp", kp); tap("t_Gp", Gp); tap("t_eG", eG); tap("t_At", AR[:, m, 0, :])
                tap("t_wdad", wdad); tap("t_eD", eD)
            pYT = plong
            pp = psets[blk % 2]
            SA, SK, Xn, AVb, U0, WtT, Ub, Yb, Y2s = (pp[k] for k in ("SA", "SK", "Xn", "AVb", "U0", "WtT", "Ub", "Yb", "Y2s"))
            P.tag = "b%d.Bpar0" % blk
            srcs = [lambda m: AR[:, m, 0, :], lambda m: KBh[:, m, 0, :], lambda m: KBh[:, m, 1, :], lambda m: vfm[:, m, :]]
            for kind in range(4):
                ps = small()
                for m in range(4):
                    P.mm(ps[:, m * 128:(m + 1) * 128], srcs[kind](m), identb[:, :])
                P.copy(AD if kind % 2 == 0 else DA, TM[:, kind, :], ps[:, :])
            msk = maskSI128[:, :].unsqueeze(1).to_broadcast([128, 2, 256])
            for g in range(4):
                psA = small(); psK = small()
                for e in (0, 1):
                    pb = e * 64
                    P.mm(psA[:, e * 256:(e + 1) * 256], BK[pb:pb + 64, g, 0, :], AR[pb:pb + 64, g, :, :])
                    P.mm(psK[:, e * 256:(e + 1) * 256], BK[pb:pb + 64, g, 1, :], AR[pb:pb + 64, g, :, :])
                P.tt("dve", SA[:, 2 * g:2 * g + 2, :], v3(psA[:, :], 2, 256), msk, ALU.mult)
                P.tt("dve", SK[:, 2 * g:2 * g + 2, :], v3(psK[:, :], 2, 256), msk, ALU.mult)
            for half in range(2):
                psQ = small()
                for hl in (0, 2, 1, 3):
                    h = half * 4 + hl
                    m, pb = h // 2, (h % 2) * 64
                    P.mm(psQ[:, hl * 128:(hl + 1) * 128], AR[pb:pb + 64, m, 0, :], BK[pb:pb + 64, m, 0, :])
                P.tt("dve", Q0[:, half * 4:(half + 1) * 4, :], v3(psQ[:, :], 4, 128),
                     mk128b[:, 2, :].unsqueeze(1).to_broadcast([128, 4, 128]), ALU.mult)
            for half in range(2):
                hs = slice(half * 4, (half + 1) * 4)
                neumann(4, SA[:, hs, 0:128], Q0[:, hs, :], (Pn[half], Qn[half], [Xn[0][:, hs, :], Xn[1][:, hs, :]]), None, n=128)
            X = Xn[1]
            ps = small()
            for h in range(8):
                P.mm(ps[:, h * 64:(h + 1) * 64], SK[:, h, 0:128], TM[:, 3, h * 64:(h + 1) * 64])
            P.copy(AD, AVb, v3(ps[:, :], 8, 64))
            ps = small()
            for h in range(8):
                P.mm(ps[:, h * 64:(h + 1) * 64], X[:, h, :], AVb[:, h, :])
            P.copy(AD, U0, v3(ps[:, :], 8, 64))
            for q2 in range(2):
                ps = small()
                for ml in range(2):
                    m = q2 * 2 + ml
                    P.mm(ps[:, ml * 256:(ml + 1) * 256], TM[:, 0, m * 128:(m + 1) * 128], X[:, 2 * m:2 * m + 2, :])
                P.copy(DA, WtT[:, q2 * 2:q2 * 2 + 2, :], v3(ps[:, :], 2, 256))
            for c in range(2):
                pc = c * 64
                rows = slice(pc, pc + 64)
                P.tag = "b%d.Bseq%d" % (blk, c)
                psU = small()
                psY2 = small()
                for h in HORD:
                    m, pb, e = h // 2, (h % 2) * 64, h % 2
                    P.mm(psU[:, h * 64:(h + 1) * 64], WtT[pb:pb + 64, m, e * 128:(e + 1) * 128],
                         Mb[pb:pb + 64, m, e * 64:(e + 1) * 64])
                    P.mm(psY2[:, h * 64:(h + 1) * 64], AR[pb:pb + 64, m, 1, :], Mb[pb:pb + 64, m, e * 64:(e + 1) * 64])
                P.tt("dve", Ub[rows], v3(psU[rows, :], 8, 64), U0[rows], ALU.add)
                P.copy(AD, Y2s[rows, :], psY2[rows, :])
                psY = small()
                for h in range(8):
                    o = psY[:, h * 64:(h + 1) * 64]
                    P.mm(o, SK[rows, h, 128:256], TM[rows, 3, h * 64:(h + 1) * 64], start=True, stop=False)
                    P.mm(o, SA[rows, h, 128:256], Ub[rows, h, :], start=False, stop=True)
                P.tt("dve", Yb[rows, :], psY[rows, :], Y2s[rows, :], ALU.add)
                psM = small()
                for m in range(4):
                    o = psM[:, m * 128:(m + 1) * 128]
                    P.mm(o, TM[rows, 1, m * 128:(m + 1) * 128], TM[rows, 3, m * 128:(m + 1) * 128], start=True, stop=False)
                    P.mm(o, TM[rows, 2, m * 128:(m + 1) * 128], Ub[rows, 2 * m:2 * m + 2, :], start=False, stop=True)
                P.tt(PD, Mst[:], Mst[:], eGC[:, :, c:c + 1].to_broadcast([128, 4, 128]), ALU.mult)
                P.tt("dve", Mst[:], Mst[:], v3(psM[:, :], 4, 128), ALU.add)
                P.copy(AD, Mb[:], Mst[:])
                for m in range(4):
                    P.mm(pYT[:, m * 128 + c * 64:m * 128 + (c + 1) * 64], Yb[rows, m * 128:(m + 1) * 128], identb[rows, pc:pc + 64])
            P.tag = "b%d.C" % blk
            P.copy(DA, yTb, v3(pYT[:, :], 4, TB))
            P.act(sqy, yTb, AF.Square)
            psm_ = small(0); pse_ = small(1)
            for m in range(4):
                P.mm(psm_[:, m * 128:(m + 1) * 128], bo64b[:], yTb[:, m, :])
            for m in range(4):
                P.mm(pse_[:, m * 128:(m + 1) * 128], bo64b[:], sqy[:, m, :])
            P.tt("dve", o_t1, yTb, v3(psm_[:, :], 4, TB), ALU.subtract)
            P.act(o_msq, v3(psm_[:, :], 4, TB), AF.Square)
            P.tt("dve", o_msq, v3(pse_[:, :], 4, TB), o_msq, ALU.subtract)
            P.ts("dve", o_msq, o_msq, 0.0, ALU.max)
            rsqrt(o_msq, o_msq, o_msq, 64e-5)
            P.tt(PD, o_t1, o_t1, o_msq, ALU.mult)
            for m in range(4):
                P.act(o_yg[:, m, :], o_t1[:, m, :], AF.Identity, scale=cvc(CV_GNW + m), bias=cvc(CV_GNB + m))
            psb_ = small(0)
            for m in range(4):
                P.mm(psb_[:, m * 128:(m + 1) * 128], Brk[:, m, :], rkb[:, m, :])
            P.tt("dve", o_yb, v3(psb_[:, :], 4, TB), vfm, ALU.mult)
            tap("t_yg", o_yg.rearrange("p a b -> p (a b)")); tap("t_yb", o_yb.rearrange("p a b -> p (a b)"))
            tap("t_yT", yTb.rearrange("p a b -> p (a b)"))
            P.tt(PD, o_yg, o_yg, o_yb, ALU.add)
            P.tt(PD, YA[:, :, blk * TB:(blk + 1) * TB], o_yg, siluz, ALU.mult)

    def phase2():
        WA.reset(); AFa.reset(); ABa.reset()
        W2 = WA.take(8, 2056)
        load_w(W2, win_d, 2176, 2056, rowscale=CV_G)
        SC = Multi(ABa, WA)
        f = lambda: AFa.take(TB)
        jtemps = [[f() for _ in range(4)] for _ in range(2)]
        sqbs = [SC.take(TB) for _ in range(2)]
        gTri = AFa.take(4, 128)
        E0 = AFa.take(4, 128)
        EMt = AFa.take(4, 128)
        o_t = AFa.take(4, TB)
        o_r = AFa.take(4, TB)
        tba = AFa.take(4)
        gg = AFa.take(4)
        gcs = AFa.take(4)
        dgl = AFa.take(4)
        mk128f = AFa.take(3, 128)
        ones128f = AFa.take(128)
        for q_ in range(3):
            P.dma(mk128f[:, q_, :], mk128_d[:, q_, :])
        P.memset("pool", ones128f, 1.0)
        oTb = SC.take(4, TB)
        sqo = SC.take(4, TB)
        bsets = []
        for _ in range(2):
            bsets.append(dict(
                beta=AFa.take(4), nbeta=AFa.take(4), egc=AFa.take(4), ekt=AFa.take(4), egl=AFa.take(8),
                QKfm=SC.take(4, 2, TB),
                vfm=SC.take(4, TB), siluz=SC.take(4, TB), qdT=SC.take(4, TB),
                EM=SC.take(4, 2, 128),
                kg=SC.take(4, 128), kte=SC.take(4, 128), vtm=SC.take(4, 128), PQ=SC.take(4, 256),
                Xn=[SC.take(4, 128) for _ in range(2)], WnT=SC.take(4, 128), vnew=SC.take(4, 128), ob=SC.take(512)))
        Q0 = SC.take(4, 128)
        Pn = [SC.take(4, 128) for _ in range(2)]
        Qn = [SC.take(4, 128) for _ in range(2)]

        for blk in range(nblk):
            xt, cur = stage0(blk)
            bs = bsets[blk % 2]
            beta, nbeta, egc, ekt, egl, QKfm, vfm, siluz, qdT, EM, kg, kte, vtm, PQ, Xn, WnT, vnew, ob = (bs[k] for k in (
                "beta", "nbeta", "egc", "ekt", "egl", "QKfm", "vfm", "siluz", "qdT", "EM", "kg", "kte", "vtm", "PQ",
                "Xn", "WnT", "vnew", "ob"))
            if blk % BPS == 0:
                P.memset("pool", Sst[:], 0.0)
                P.memset("pool", Sb[:], 0.0)
            for j in range(12):
                accA, accB, sv, rs = jtemps[j % 2]
                sqb_ = sqbs[j % 2]
                ps = big()
                inproj(W2, j * 128, cur, 3, ps)
                cw = lambda i: cvc(CV_CONV + i * 12 + j)
                P.act(accA, ps[:, 3:TB + 3], AF.Copy, scale=cw(3))
                P.stt("dve", accB, ps[:, 2:TB + 2], cw(2), accA, ALU.mult, ALU.add)
                P.stt("dve", accA, ps[:, 1:TB + 1], cw(1), accB, ALU.mult, ALU.add)
                P.stt("dve", accB, ps[:, 0:TB], cw(0), accA, ALU.mult, ALU.add)
                h = j % 4
                sigmoid(accA, accB, accA)
                if j >= 8:
                    P.tt(PD, vfm[:, h, :], accA, accB, ALU.mult)
                else:
                    P.tt(PD, sv, accA, accB, ALU.mult)
                    P.act(sqb_, sv, AF.Square)
                    ps2 = big()
                    P.mm(ps2[:, 0:TB], (ones128b if j < 4 else onesb)[:], sqb_)
                    eps = 128e-12 if j < 4 else 1e-12
                    rsqrt(rs, ps2[:, 0:TB], accA, eps)
                    P.tt(PD, QKfm[:, h, 1 if j < 4 else 0, :], sv, rs, ALU.mult)
            for h in range(4):
                accA, accB, sv, rs = jtemps[h % 2]
                ps = big()
                inproj(W2, 1536 + h * 128, cur, 0, ps)
                P.act(accB, ps[:, 0:TB], AF.Copy)
                sigmoid(accA, ps[:, 0:TB], accA)
                P.tt(PD, siluz[:, h, :], accA, accB, ALU.mult)
            psba = small()
            for dc in range(8):
                P.mm(psba[:, 0:8], cur[:, dc, 3:3 + TB], W2[:, dc, 2048:2056], start=(dc == 0), stop=(dc == 7))
            sigmoid(beta, psba[:, 0:4], beta)
            P.tt("dve", tba, psba[:, 4:8], gvb128[:, 4:8], ALU.add)
            P.act(tba, tba, AF.Exp)
            P.act(tba, tba, AF.Ln, bias=1.0)
            P.tt("dve", gg, tba, negA128[:, :], ALU.mult)
            P.ts(PD, nbeta, beta, -1.0, ALU.mult)
            ps = small()
            P.mm(ps[:, 0:4], mk128f[:, 0, :], gg)
            P.act(egc, ps[:, 0:4], AF.Exp)
            P.copy(AD, gcs, ps[:, 0:4])
            ps = small()
            for c in range(2):
                P.mm(ps[:, c * 4:(c + 1) * 4], ones128f[c * 64:(c + 1) * 64, :], gg[c * 64:(c + 1) * 64, :])
            P.act(egl, ps[:, 0:8], AF.Exp)
            for c in range(2):
                rows = slice(c * 64, (c + 1) * 64)
                P.tt("dve", dgl[rows, :], ps[rows, c * 4:(c + 1) * 4], gcs[rows, :], ALU.subtract)
            P.act(ekt, dgl, AF.Exp)
            P.tt(DP, gTri, mk128f[:, 0, :].unsqueeze(1).to_broadcast([128, 4, 128]),
                 gg.unsqueeze(2).to_broadcast([128, 4, 128]), ALU.mult)
            ps = small()
            P.mm(ps[:, :], mk128f[:, 2, :], gTri.rearrange("p a b -> p (a b)"))
            P.act(E0, v3(ps[:, :], 4, 128), AF.Exp)
            P.tt(PD, EM[:, :, 1, :], E0, mk128f[:, 0, :].unsqueeze(1).to_broadcast([128, 4, 128]), ALU.mult)
            P.tt(DP, EMt, E0, nbeta.unsqueeze(2).to_broadcast([128, 4, 128]), ALU.mult)
            P.tt(PD, EM[:, :, 0, :], EMt, mk128f[:, 1, :].unsqueeze(1).to_broadcast([128, 4, 128]), ALU.mult)
            P.tt(DP, gTri, identb[:, :].unsqueeze(1).to_broadcast([128, 4, 128]),
                 egc.unsqueeze(2).to_broadcast([128, 4, 128]), ALU.mult)
            ps = small()
            P.mm(ps[:, :], ones128f, gTri.rearrange("p a b -> p (a b)"))
            P.tt("dve", qdT, QKfm[:, :, 1, :], v3(ps[:, :], 4, 128), ALU.mult)
            pOT = plong
            ps = small()
            for h in range(4):
                P.mm(ps[:, h * 128:(h + 1) * 128], QKfm[:, h, 0, :], identb[:, :])
            P.tt("dve", kg, v3(ps[:, :], 4, 128), egc.unsqueeze(2).to_broadcast([128, 4, 128]), ALU.mult)
            P.tt("dve", kte, v3(ps[:, :], 4, 128), ekt.unsqueeze(2).to_broadcast([128, 4, 128]), ALU.mult)
            ps = small()
            for h in range(4):
                P.mm(ps[:, h * 128:(h + 1) * 128], vfm[:, h, :], identb[:, :])
            P.copy(AD, vtm, v3(ps[:, :], 4, 128))
            for q2 in range(2):
                ps = small()
                for hl in range(2):
                    h = q2 * 2 + hl
                    P.mm(ps[:, hl * 256:(hl + 1) * 256], QKfm[:, h, 0, :], QKfm[:, h, :, :])
                P.tt("dve", PQ[:, q2 * 2:q2 * 2 + 2, :], v3(ps[:, :], 2, 256),
                     EM[:, q2 * 2:q2 * 2 + 2, :, :].rearrange("p a b c -> p a (b c)"), ALU.mult)
            ps = small()
            for h in range(4):
                P.mm(ps[:, h * 128:(h + 1) * 128], PQ[:, h, 0:128], identb[:, :])
            P.copy(AD, Q0, v3(ps[:, :], 4, 128))
            X = neumann(4, PQ[:, :, 0:128], Q0, (Pn, Qn, Xn), None, n=128)
            ps = small()
            for h in range(4):
                P.mm(ps[:, h * 128:(h + 1) * 128], kg[:, h, :], X[:, h, :])
            P.ts("dve", WnT, v3(ps[:, :], 4, 128), -1.0, ALU.mult)
            for c in range(2):
                pc = c * 64
                rows = slice(pc, pc + 64)
                psV = small()
                for h in range(4):
                    o = psV[:, h * 128:(h + 1) * 128]
                    P.mm(o, X[rows, h, :], vtm[rows, h, :], start=True, stop=False)
                    P.mm(o, WnT[:, h, :], Sb[:, h, :], start=False, stop=True)
                P.tt("dve", vnew[rows], v3(psV[rows, :], 4, 128), beta[rows, :].unsqueeze(2).to_broadcast([64, 4, 128]), ALU.mult)
                psO = small()
                for h in range(4):
                    o = psO[:, h * 128:(h + 1) * 128]
                    P.mm(o, qdT[:, h, :], Sb[:, h, :], start=True, stop=False)
                    P.mm(o, PQ[rows, h, 128:256], vnew[rows, h, :], start=False, stop=True)
                P.copy(AD, ob[rows, :], psO[rows, :])
                psS = small()
                for h in range(4):
                    P.mm(psS[:, h * 128:(h + 1) * 128], kte[rows, h, :], vnew[rows, h, :])
                P.tt(PD, Sst[:], Sst[:], egl[:, c * 4:(c + 1) * 4].unsqueeze(2).to_broadcast([128, 4, 128]), ALU.mult)
                P.tt("dve", Sst[:], Sst[:], v3(psS[:, :], 4, 128), ALU.add)
                P.copy(AD, Sb[:], Sst[:])
                for h in range(4):
                    P.mm(pOT[:, h * 128 + c * 64:h * 128 + (c + 1) * 64], ob[rows, h * 128:(h + 1) * 128], identb[rows, pc:pc + 64])
            P.copy(DA, oTb, v3(pOT[:, :], 4, TB))
            P.act(sqo, oTb, AF.Square)
            ps = small(0)
            for h in range(4):
                P.mm(ps[:, h * 128:(h + 1) * 128], onesdb[:], sqo[:, h, :])
            rsqrt(o_r, v3(ps[:, :], 4, TB), o_t, 1e-6)
            P.tt(PD, o_t, oTb, o_r, ALU.mult)
            P.stt("dve", YB[:, :, blk * TB:(blk + 1) * TB], o_t, cvc(CV_ONW), siluz, ALU.mult, ALU.mult)

    def phase3():
        WA.reset(); AFa.reset(); ABa.reset()
        Wg = WA.take(8, 2048)
        Wa = WA.take(4, 1024)
        Wb = WA.take(4, 1024)
        Wo = WA.take(8, 1024)
        load_w(Wg, win_d, 4232, 2048, rowscale=CV_G)
        load_w(Wa, wa_d, 0, 1024)
        load_w(Wb, wb_d, 0, 1024)
        load_w(Wo, wo_d, 0, 1024)
        T3 = 4 * TB
        xrs = [AFa.take(DM) for _ in range(2)]
        xres = AFa.take(DM)
        nwbc = AFa.take(DM)
        hT3 = [ABa.take(8, T3) for _ in range(2)]
        mgs = [ABa.take(8, T3) for _ in range(2)]
        sa = ABa.take(T3); sbb = ABa.take(T3)
        xn3 = ABa.take(2 * T3)
        t1 = xn3[:, 0:T3]; t2 = xn3[:, T3:2 * T3]
        P.dma(nwbc, nwo_d.partition_broadcast(128))
        for B in range(nblk // 4):
            h3 = hT3[B % 2]
            mg = mgs[B % 2]
            for t in range(4):
                stage0(B * 4 + t, need_halo=False, pool=True, dest=h3[:, :, t * TB:(t + 1) * TB])
            tok = slice(B * T3, (B + 1) * T3)
            for cc in range(8):
                psa = big3()
                for dc in range(8):
                    P.mm(psa[:, :], Wg[:, dc, cc * 128:(cc + 1) * 128], h3[:, dc, :], start=(dc == 0), stop=(dc == 7))
                psb = big3()
                for dc in range(8):
                    P.mm(psb[:, :], Wg[:, dc, 1024 + cc * 128:1024 + (cc + 1) * 128], h3[:, dc, :], start=(dc == 0), stop=(dc == 7))
                pya = small()
                for kc in range(4):
                    P.mm(pya[:, :], Wa[:, kc, cc * 128:(cc + 1) * 128], YA[:, kc, tok], start=(kc == 0), stop=(kc == 3))
                pyb = small()
                for kc in range(4):
                    P.mm(pyb[:, :], Wb[:, kc, cc * 128:(cc + 1) * 128], YB[:, kc, tok], start=(kc == 0), stop=(kc == 3))
                P.act(sa, psa[:, :], AF.Sigmoid)
                P.act(sbb, psb[:, :], AF.Sigmoid)
                P.tt("dve", t1, pya[:, :], sa, ALU.mult)
                P.tt("dve", t2, pyb[:, :], sbb, ALU.mult)
                P.tt(PD, mg[:, cc, :], t1, t2, ALU.add)
            for t in range(4):
                blk = B * 4 + t
                xr = xrs[blk % 2]
                rows = slice(blk * TB, (blk + 1) * TB)
                P.dma(xres, x_d[rows, :])
                for half in range(2):
                    pso = big3()
                    for kc in range(8):
                        P.mm(pso[:, :], mg[:, kc, t * TB:(t + 1) * TB], Wo[:, kc, half * 512:(half + 1) * 512],
                             start=(kc == 0), stop=(kc == 7))
                    P.tt("dve", xr[:, half * 512:(half + 1) * 512], pso[:, :], xres[:, half * 512:(half + 1) * 512], ALU.add)
                P.memset("pool", ss3[:], 0.0)
                P.act(xn3, xr, AF.Square, accum=ss3[:])
                rsqrt(rstd3[:], ss3[:], ss23[:], 1e-6, scale=1.0 / DM, pool=True)
                P.stt("dve", xr, xr, rstd3[:, 0:1], nwbc, ALU.mult, ALU.mult)
                P.dma(out_d[rows, :], xr)

    if 1 in phases:
        phase1()
    if 2 in phases:
        phase2()
    if 3 in phases:
        phase3()
    if dbg:
        if "YA" in dbg_d:
            stg = AFa.t
            for m in range(4):
                for q in range(0, dbg["YA"][2], 1024):
                    n = min(1024, dbg["YA"][2] - q)
                    P.copy(DA, stg[:, 0:n], YA[:, m, q:q + n])
                    P.dma(dbg_d["YA"][:, m, q:q + n], stg[:, 0:n])
        if "YB" in dbg_d:
            stg = AFa.t
            for m in range(4):
                for q in range(0, dbg["YB"][2], 1024):
                    n = min(1024, dbg["YB"][2] - q)
                    P.copy(DA, stg[:, 0:n], YB[:, m, q:q + n])
                    P.dma(dbg_d["YB"][:, m, q:q + n], stg[:, 0:n])
    if trunc:
        P.ops = P.ops[:trunc]
    if SCHED:
        P.schedule()
    P.emit()
    return nc, P


def host_inputs(inputs):
    f = lambda a: np.ascontiguousarray(np.asarray(a, dtype=np.float32))
    x = f(inputs["x"])
    vec4 = lambda v: f(v).reshape(-1, 128).T
    cw = f(inputs["gd_conv_w"])[0]
    cols = [vec4(inputs["norm_in_w"][0]), vec4(inputs["rw_mu"][0]), vec4(inputs["rw_w0"][0]),
            vec4(inputs["rw_a0"][0]), vec4(inputs["rw_k_k"][0]), vec4(inputs["rw_k_a"][0]),
            vec4(f(inputs["rw_r_k"])[0].reshape(-1)), vec4(inputs["rw_gn_w"][0]), vec4(inputs["rw_gn_b"][0])]
    cols += [vec4(cw[i]) for i in range(4)]
    cols += [f(inputs["gd_o_norm_w"])[0].reshape(128, 1)]
    cvh = np.ascontiguousarray(np.concatenate(cols, axis=1))
    assert cvh.shape == (128, NCV), cvh.shape
    p = np.arange(128)
    bo = (p[:, None] // 64 == p[None, :] // 64).astype(np.float32)
    q = np.arange(64)
    masks = np.stack([(q[:, None] <= q[None, :]), (q[:, None] < q[None, :]), (q[:, None] > q[None, :])],
                     axis=1).astype(np.float32)
    resetm = np.ones((128, 128), np.float32)
    resetm[:, 0] = 0.0
    resetm[:, 64] = 0.0
    shared = {
        "w_in": f(inputs["w_in"])[0], "w_a": f(inputs["w_branch_a"])[0], "w_b": f(inputs["w_branch_b"])[0],
        "w_o": f(inputs["w_out"])[0], "w2": f(inputs["rw_w2"])[0], "a2": f(inputs["rw_a2"])[0],
        "cv": cvh, "nwo": f(inputs["norm_out_w"]).reshape(1, DM),
        "gvec": np.concatenate([f(inputs["gd_A_log"])[0], f(inputs["gd_dt_bias"])[0]]).reshape(1, 8),
        "ident": np.eye(128, dtype=np.float32), "bo": bo, "masks": np.ascontiguousarray(masks), "resetm": resetm,
        "masks128": np.ascontiguousarray(np.kron(np.eye(2, dtype=np.float32)[:, None, :], masks).astype(np.float32)),
    }
    in_maps = []
    for c in range(NCORES):
        m = dict(shared)
        m["x"] = np.ascontiguousarray(x[NSEQ * c:NSEQ * (c + 1)].reshape(NTOK, DM))
        in_maps.append(m)
    return in_maps


def kernel(**inputs):
    in_maps = host_inputs(inputs)
    nc, _ = build_nc()
    res = run_bass_kernel_spmd(nc, in_maps, core_ids=list(range(NCORES)))
    outs = [np.asarray(r["out"], dtype=np.float32).reshape(NSEQ, SEQ, DM) for r in res.results]
    return np.concatenate(outs, axis=0)
```

```python
import math
import numpy as np
import concourse.bass as bass
import concourse.mybir as mybir
from concourse.bass_utils import run_bass_kernel_spmd

F32 = mybir.dt.float32
BF16 = mybir.dt.bfloat16
ALU = mybir.AluOpType
AF = mybir.ActivationFunctionType


class Op:
    __slots__ = ("eng", "fn", "boxes_r", "boxes_w", "deps", "sig", "cnt", "idx",
                 "dsem", "dcnt", "dprev", "alldeps", "cost", "rows", "succ", "prio", "nin", "rt", "fin", "tag", "st", "alts", "wsz", "psum", "vc", "pos", "edeps")

    def __init__(self, eng, fn):
        self.eng = eng
        self.fn = fn
        self.deps = set()
        self.alldeps = set()
        self.cost = 0.3
        self.rows = None
        self.alts = None
        self.sig = False
        self.cnt = 0
        self.dsem = -1
        self.dcnt = 0
        self.dprev = None


def _box(ap):
    t = ap.tensor
    name = t.name
    pat = ap.ap
    off = ap.offset
    sp = str(ap.space)
    if "PSUM" in sp.upper():
        return (name, 0, 128, 0, 1 << 40)
    if "SB" in sp.upper():
        shp = t.shape
        F = 1
        for s in shp[1:]:
            F *= s
        p0 = off // F
        f0 = off % F
        npart = pat[0][1]
        ext = 1
        for st, c in pat[1:]:
            ext += (c - 1) * abs(st)
        return (name, p0, p0 + npart, f0, f0 + ext)
    ext = 1
    for st, c in pat:
        ext += (c - 1) * abs(st)
    return (name, 0, 1, off, off + ext)


def _fsize(ap):
    n = 1
    for st, c in ap.ap[1:]:
        n *= c
    return n


def _ovl(a, b):
    return a[1] < b[2] and b[1] < a[2] and a[3] < b[4] and b[3] < a[4]


def _covers(a, b):
    return a[1] <= b[1] and a[2] >= b[2] and a[3] <= b[3] and a[4] >= b[4]


PE_STANDALONE_WAITS = False
TRANSITIVE = True
LAST_ONLY = True
SCHED_EPS = 0.1


class Prog:
    NDMA = 16

    def __init__(self, nc):
        self.nc = nc
        self.ops = []
        self.hist = {}
        self.engs = {"pe": nc.tensor, "act": nc.scalar, "dve": nc.vector,
                     "pool": nc.gpsimd, "sp": nc.sync}

    def add(self, eng, fn, reads, writes, dma=False):
        alts = None
        if isinstance(eng, tuple):
            alts, eng = eng, eng[0]
        op = Op(eng, fn)
        op.alts = alts
        op.idx = len(self.ops)
        op.tag = getattr(self, 'tag', '')
        op.dsem = 0 if dma else -1
        br = [_box(a) for a in reads]
        bw = [_box(a) for a in writes]
        bw = bw + [b for b in br if b[4] == (1 << 40)]
        br = [b for b in br if b[4] != (1 << 40)]
        for b in br:
            for (hb, hop, hw) in self.hist.get(b[0], ()):
                if hw and _ovl(hb, b):
                    self._dep(op, hop, raw=True)
        for b in bw:
            for (hb, hop, hw) in self.hist.get(b[0], ()):
                if _ovl(hb, b):
                    self._dep(op, hop, raw=False)
        for b in bw:
            lst = self.hist.setdefault(b[0], [])
            lst[:] = [e for e in lst if not _covers(b, e[0])]
            lst.append((b, op, True))
        for b in br:
            self.hist.setdefault(b[0], []).append((b, op, False))
        self.ops.append(op)
        wsz = _fsize(writes[0]) if writes else 64
        psum = any(b[4] == (1 << 40) for b in bw)
        op.wsz = wsz
        op.psum = psum
        if dma:
            op.cost = 2.0 + 0.004 * wsz
        else:
            op.cost = self._ecost(eng, wsz, psum, op.cost)
        return op

    @staticmethod
    def _ecost(eng, wsz, psum, default):
        if eng == "act":
            return 0.2 + 0.00085 * wsz
        if eng == "dve":
            return (0.1 if psum else 0.07) + 0.00105 * wsz
        if eng == "pool":
            return 0.12 + 0.0021 * wsz
        return default

    def schedule(self):
        ops = self.ops
        for o in ops:
            o.succ = []
        for o in ops:
            for d in o.alldeps:
                d.succ.append(o)
            o.nin = len(o.alldeps)
        for o in reversed(ops):
            p = 0.0
            for q in o.succ:
                if q.prio > p:
                    p = q.prio
            o.prio = p + o.cost
        free = {e: 0.0 for e in self.engs}
        ready = {e: [] for e in self.engs}
        def push(o):
            for e in (o.alts or (o.eng,)):
                ready[e].append(o)

        for o in ops:
            if o.nin == 0:
                o.rt = 0.0
                push(o)
        order = []
        n = len(ops)
        pe_rows = None

        def pe_pen(o):
            if o.eng != "pe" or o.rows is None or pe_rows is None or o.rows == pe_rows:
                return 0.0
            if pe_rows[1] <= o.rows[0] or o.rows[1] <= pe_rows[0]:
                return 0.3
            return 0.11

        while len(order) < n:
            best = None
            for e, lst in ready.items():
                if not lst:
                    continue
                t = free[e]
                cand = None
                for o in lst:
                    st = (o.rt if o.rt > t else t) + pe_pen(o)
                    if o.alts:
                        ce = self._ecost(e, o.wsz, o.psum, o.cost)
                        st += ce - min(self._ecost(a, o.wsz, o.psum, o.cost) for a in o.alts)
                    key = (int(st / SCHED_EPS), -o.prio, o.idx, st)
                    if cand is None or key < cand[0]:
                        cand = (key, o, e)
                if best is None or cand[0] < best[0]:
                    best = cand
            key, o, e = best
            if o.alts:
                for a in o.alts:
                    ready[a].remove(o)
                t = free[e]
                o.eng = e
                o.cost = self._ecost(e, o.wsz, o.psum, o.cost)
                st = o.rt if o.rt > t else t
            else:
                st = key[3]
                ready[o.eng].remove(o)
            if o.eng == "pe" and o.rows is not None:
                pe_rows = o.rows
            if o.dsem >= 0:
                free[o.eng] = st + 0.06
            else:
                free[o.eng] = st + o.cost
            o.fin = st + o.cost
            o.st = st
            order.append(o)
            for q in o.succ:
                q.nin -= 1
                if q.nin == 0:
                    rt = 0.0
                    for d in q.alldeps:
                        lat = d.fin + (0.12 if d.eng == q.eng else 0.25)
                        if lat > rt:
                            rt = lat
                    q.rt = rt
                    push(q)
        self.ops = order
        self.est = max(o.fin for o in order)

    def _dep(self, op, src, raw):
        if src is op:
            return
        op.alldeps.add(src)
        if src.eng == op.eng and src.dsem < 0:
            if op.eng == "pe":
                return
        op.deps.add(src)

    def emit(self, final_wait=True):
        nc = self.nc
        names = ["pe", "act", "dve", "pool"]
        sems = {e: nc.alloc_semaphore("s_" + e) for e in names}
        dsems = [nc.alloc_semaphore("s_dma%d" % i) for i in range(self.NDMA)]
        last = None
        for op in self.ops:
            if op.eng == "pe" and op.rows is not None:
                if last is not None and (last.rows[1] <= op.rows[0] or op.rows[1] <= last.rows[0]):
                    op.deps.add(last)
                last = op
        for i, op in enumerate(self.ops):
            op.pos = i
        for op in self.ops:
            last = {}
            eff = []
            for d in op.deps:
                if d.dsem >= 0 or not LAST_ONLY:
                    eff.append(d)
                elif d.eng not in last or d.pos > last[d.eng].pos:
                    last[d.eng] = d
            eff.extend(last.values())
            op.edeps = eff
            for d in eff:
                d.sig = True
        cnt = {e: 0 for e in names}
        dcnt = [0] * self.NDMA
        dlast = [None] * self.NDMA
        ndma = 0
        for op in self.ops:
            if op.dsem >= 0:
                s = ndma % self.NDMA
                ndma += 1
                op.dsem = s
                dcnt[s] += 16
                op.dcnt = dcnt[s]
                op.dprev = dlast[s]
                dlast[s] = op
            elif op.sig:
                cnt[op.eng] += 1
                op.cnt = cnt[op.eng]
        waited = {e: {} for e in list(self.engs)}
        nwait = 0
        for op in self.ops:
            need = {}
            deps = list(op.edeps)
            if op.dprev is not None:
                deps.append(op.dprev)
            for d in deps:
                if d.dsem >= 0:
                    key = ("d", d.dsem)
                    val = d.dcnt
                else:
                    key = ("e", d.eng)
                    val = d.cnt
                if val > need.get(key, (0, None))[0]:
                    need[key] = (val, d)
            eng = self.engs[op.eng]
            w = waited[op.eng]
            todo = []
            for key, (val, d) in sorted(need.items(), key=lambda kv: -kv[1][1].idx if TRANSITIVE else 0):
                if val > w.get(key, 0):
                    w[key] = val
                    if TRANSITIVE:
                        for k2, v2 in d.vc.items():
                            if v2 > w.get(k2, 0):
                                w[k2] = v2
                    sem = dsems[key[1]] if key[0] == "d" else sems[key[1]]
                    todo.append((sem, val))
            if TRANSITIVE and (op.dsem >= 0 or op.sig):
                op.vc = dict(w)
                if op.dsem >= 0:
                    op.vc[("d", op.dsem)] = op.dcnt
                else:
                    op.vc[("e", op.eng)] = op.cnt
            standalone = todo if (op.eng == "pe" and PE_STANDALONE_WAITS) else todo[1:]
            for sem, val in standalone:
                eng.wait_ge(sem, val)
                nwait += 1
            ins = op.fn(eng)
            if todo and standalone is not todo:
                ins._wait_ge(todo[0][0], todo[0][1])
                nwait += 1
            if op.dsem >= 0:
                ins.then_inc(dsems[op.dsem], 16)
            elif op.sig:
                ins.then_inc(sems[op.eng], 1)
        if final_wait:
            sp = self.engs["sp"]
            for s in range(self.NDMA):
                if dcnt[s] > 0:
                    sp.wait_ge(dsems[s], dcnt[s])
        self.stats = dict(nops=len(self.ops), nwait=nwait,
                          nsig=sum(cnt.values()), ndma=ndma)

    def dma(self, out, in_, eng="sp"):
        return self.add(eng, lambda e, o=out, i=in_: e.dma_start(out=o, in_=i),
                        [in_], [out], dma=True)

    def _pe_rows(self, op, lhsT):
        b0 = lhsT.base_partition()
        op.rows = (b0, b0 + lhsT.ap[0][1])
        return op

    def mm(self, out, lhsT, rhs, start=True, stop=True):
        op = self.add("pe", lambda e, o=out, l=lhsT, r=rhs, s=start, t=stop:
                      e.matmul(o, l, r, start=s, stop=t), [lhsT, rhs], [out])
        n = _fsize(rhs)
        op.cost = (0.02 + 0.00085 * max(n, 64)) * (4.0 if rhs.dtype == F32 else 1.0)
        return self._pe_rows(op, lhsT)

    def tr(self, out, in_, ident):
        op = self.add("pe", lambda e, o=out, i=in_, d=ident: e.transpose(o, i, d),
                      [in_, ident], [out])
        op.cost = 0.09
        return self._pe_rows(op, in_)

    def act(self, out, in_, func, bias=None, scale=None, accum=None, eng="act"):
        reads = [in_]
        kw = {}
        if bias is not None:
            kw["bias"] = bias
            if not isinstance(bias, (int, float)):
                reads.append(bias)
        if scale is not None:
            kw["scale"] = scale
            if not isinstance(scale, (int, float)):
                reads.append(scale)
        writes = [out]
        if accum is not None:
            kw["accum_out"] = accum
            writes.append(accum)
        return self.add(eng, lambda e, o=out, i=in_, f=func, k=kw: e.activation(o, i, f, **k),
                        reads, writes)

    def tt(self, eng, out, in0, in1, op):
        return self.add(eng, lambda e, o=out, a=in0, b=in1, p=op: e.tensor_tensor(o, a, b, p),
                        [in0, in1], [out])

    def ts(self, eng, out, in0, s1, op0, s2=None, op1=None, accum=None):
        reads = [in0]
        if not isinstance(s1, (int, float)):
            reads.append(s1)
        if s2 is not None and not isinstance(s2, (int, float)):
            reads.append(s2)
        kw = {}
        if op1 is not None:
            kw["op1"] = op1
        writes = [out]
        if accum is not None:
            kw["accum_out"] = accum
            writes.append(accum)
        return self.add(eng, lambda e, o=out, a=in0, x=s1, y=s2, p=op0, k=kw:
                        e.tensor_scalar(o, a, x, y, p, **k), reads, writes)

    def stt(self, eng, out, in0, scalar, in1, op0, op1):
        reads = [in0, in1]
        if not isinstance(scalar, (int, float)):
            reads.append(scalar)
        return self.add(eng, lambda e, o=out, a=in0, s=scalar, b=in1, p=op0, q=op1:
                        e.scalar_tensor_tensor(o, a, s, b, p, q), reads, [out])

    def copy(self, eng, out, in_):
        def fn(e, o=out, i=in_):
            return e.copy(o, i) if e is self.engs["act"] else e.tensor_copy(o, i)
        return self.add(eng, fn, [in_], [out])

    def memset(self, eng, out, val):
        return self.add(eng, lambda e, o=out, v=val: e.memset(o, v), [], [out])


NCORES = 8
SEQ = 2048
DM = 1024
NSEQ = 2
NTOK = NSEQ * SEQ
TB = 128
NBLK = NTOK // TB
BPS = SEQ // TB
C = 64
INC = 6280
CDEC = math.exp(-0.5)

CV_G, CV_MU, CV_W0, CV_A0, CV_KK, CV_KA, CV_RK, CV_GNW, CV_GNB, CV_CONV, CV_ONW = \
    0, 8, 21, 25, 29, 33, 37, 41, 45, 49, 97
NCV = 98
PD = ("pool", "dve")
DP = ("dve", "pool")
AD = ("act", "dve")
DA = ("dve", "act")
SCHED = True
HORD = (0, 2, 4, 6, 1, 3, 5, 7)


def v3(ap, a, b):
    return ap.rearrange("p (a b) -> p a b", a=a, b=b)


class Arena:
    def __init__(self, nc, name, n, dtype, ap2d=None):
        self.t = ap2d if ap2d is not None else nc.alloc_sbuf_tensor(name, [128, n], dtype)
        self.n = n
        self.off = 0
        self.name = name

    def room(self):
        return self.n - self.off

    def reset(self, off=0):
        self.off = off

    def take(self, *shape, parts=128):
        n = 1
        for s in shape:
            n *= s
        assert self.off + n <= self.n, (self.name, self.off, n, self.n)
        ap = self.t[0:parts, self.off:self.off + n]
        self.off += n
        if len(shape) == 2:
            ap = ap.rearrange("p (a b) -> p a b", a=shape[0], b=shape[1])
        elif len(shape) == 3:
            ap = ap.rearrange("p (a b c) -> p a b c", a=shape[0], b=shape[1], c=shape[2])
        return ap


class Multi:
    def __init__(self, *arenas):
        self.arenas = arenas

    def take(self, *shape, parts=128):
        n = 1
        for s_ in shape:
            n *= s_
        for a in self.arenas:
            if a.room() >= n:
                return a.take(*shape, parts=parts)
        raise AssertionError(("out of scratch", shape, [a.room() for a in self.arenas]))


def build_nc(phases=(1, 2, 3), nblk=NBLK, dbg=None, trunc=None):
    nc = bass.Bass("TRN2", target_bir_lowering=False)
    P = Prog(nc)
    dt = lambda name, shape, kind="ExternalInput": nc.dram_tensor(name, shape, F32, kind=kind).ap()
    x_d = dt("x", [NTOK, DM])
    win_d = dt("w_in", [DM, INC])
    wa_d = dt("w_a", [512, DM])
    wb_d = dt("w_b", [512, DM])
    wo_d = dt("w_o", [DM, DM])
    w2_d = dt("w2", [64, 512])
    a2_d = dt("a2", [64, 512])
    cv_d = dt("cv", [128, NCV])
    nwo_d = dt("nwo", [1, DM])
    gv_d = dt("gvec", [1, 8])
    id_d = dt("ident", [128, 128])
    bo_d = dt("bo", [128, 128])
    mk_d = dt("masks", [64, 3, 64])
    rm_d = dt("resetm", [128, 128])
    mk128_d = dt("masks128", [128, 3, 128])
    out_d = dt("out", [NTOK, DM], kind="ExternalOutput")
    dbg_d = {}
    if dbg:
        for k, shp in dbg.items():
            dbg_d[k] = dt("dbg_" + k, shp, kind="ExternalOutput")

    sb = lambda name, shape, dtype=F32: nc.alloc_sbuf_tensor("s_" + name, shape, dtype)
    ytok = NTOK if not dbg else max(nblk * TB, 1024)
    YA = sb("YA", [128, 4, ytok], BF16)
    YB = sb("YB", [128, 4, ytok], BF16)
    tapbuf = sb("tapbuf", [128, 1024]) if dbg else None
    tapped = set()

    def tap(name, ap, parts=128):
        if not dbg or name not in dbg_d or name in tapped:
            return
        tapped.add(name)
        n = dbg[name][1]
        P.copy(DA, tapbuf[0:parts, 0:n], ap)
        P.dma(dbg_d[name], tapbuf[0:parts, 0:n])

    xbuf = [sb("xbuf%d" % i, [128, DM]) for i in range(2)]
    hT = [sb("hT%d" % i, [128, 8, TB + 3], BF16) for i in range(2)]
    xn = sb("xn", [128, DM], BF16)
    ss = sb("ss", [128, 1])
    rstd = sb("rstd", [128, 1])
    ss2 = sb("ss2", [128, 1])
    ss3 = sb("ss3", [128, 1])
    ss23 = sb("ss23", [128, 1])
    rstd3 = sb("rstd3", [128, 1])
    identb = sb("identb", [128, 128], BF16)
    bob = sb("bob", [128, 128], BF16)
    bo64b = sb("bo64b", [128, 128], BF16)
    onesb = sb("onesb", [128, 128], BF16)
    ones128b = sb("ones128b", [128, 128], BF16)
    onesdb = sb("onesdb", [128, 128], BF16)
    mk128b = sb("mk128b", [128, 3, 128], BF16)
    maskSI128 = sb("maskSI128", [128, 256], BF16)
    resetm = sb("resetm", [128, 128])
    cv = sb("cv", [128, NCV])
    omm = sb("omm", [128, 13])
    negh = sb("negh", [128, 1])
    cvn = sb("cvn", [128, 8])
    gvb128 = sb("gvb128", [128, 8])
    negA128 = sb("negA128", [128, 4])
    Mst = sb("Mst", [128, 4, 128])
    Mb = sb("Mb", [128, 4, 128], BF16)
    Sst = sb("Sst", [128, 4, 128])
    Sb = sb("Sb", [128, 4, 128], BF16)
    WA = Arena(nc, "WA", 32768, BF16)
    AFa = Arena(nc, "AFa", 4500, F32)
    ABa = Arena(nc, "ABa", 18660, BF16)
    if dbg:
        YBa = Arena(nc, "XTR", 16384, BF16)
    else:
        YBa = Arena(nc, "YBa", 4 * ytok, BF16, ap2d=YB[:, :, :].rearrange("p a b -> p (a b)"))

    ptp = nc.alloc_psum_tensor("ptp", [128, 1024], BF16)
    pbig = [nc.alloc_psum_tensor("pbig%d" % i, [128, 512], F32) for i in range(2)]
    psm = [nc.alloc_psum_tensor("psm%d" % i, [128, 512], F32) for i in range(4)]
    plong = nc.alloc_psum_tensor("plong", [128, 512], F32)
    rr = {"big": 0, "small": 0, 0: 0, 1: 0}

    def big():
        rr["big"] += 1
        return pbig[rr["big"] % 2]

    def big3():
        return big()

    def small(c=None):
        if c is None:
            rr["small"] += 1
            return psm[rr["small"] % 4]
        rr[c] += 1
        return psm[2 * c + rr[c] % 2]

    engrr = {"n": 0}

    def anyeng(choices=("dve", "pool")):
        engrr["n"] += 1
        return choices[engrr["n"] % len(choices)]

    identf = xbuf[0][:, 0:128]
    bof = xbuf[0][:, 128:256]
    P.dma(identf, id_d)
    P.dma(bof, bo_d)
    P.dma(resetm[:], rm_d)
    P.dma(cv[:], cv_d)
    P.dma(gvb128[:], gv_d.partition_broadcast(128))
    P.copy(DA, identb[:], identf)
    P.copy(DA, bob[:], bof)
    P.ts("dve", bo64b[:], bof, 1.0 / 64, ALU.mult)
    P.memset("pool", onesb[:], 1.0)
    P.memset("pool", ones128b[:], 128.0)
    P.memset("pool", onesdb[:], 1.0 / 128)
    for q_ in range(3):
        P.dma(xbuf[1][:, q_ * 128:(q_ + 1) * 128], mk128_d[:, q_, :])
    P.copy(DA, mk128b[:, :, :], v3(xbuf[1][:, 0:384], 3, 128))
    P.copy(DA, maskSI128[:, 0:128], xbuf[1][:, 128:256])
    P.copy(DA, maskSI128[:, 128:256], xbuf[1][:, 0:128])
    P.ts("dve", omm[:], cv[:, CV_MU:CV_MU + 13], -1.0, ALU.mult, 1.0, ALU.add)
    P.memset("pool", negh[:], -0.5)
    P.ts("dve", cvn[:, 0:8], cv[:, CV_W0:CV_W0 + 8], -1.0, ALU.mult)
    P.act(negA128[:], gvb128[:, 0:4], AF.Exp)
    P.ts("dve", negA128[:], negA128[:], -1.0, ALU.mult)

    def cvc(col):
        return cv[:, col:col + 1]

    def cvnc(col):
        return cvn[:, col:col + 1]

    def rsqrt(out, in_, tmp, bias, scale=None, pool=False):
        if pool:
            P.act(tmp, in_, AF.Identity, bias=bias, scale=scale)
            P.tt("pool", out, tmp, negh[0:out.shape[0], 0:1].to_broadcast(list(out.shape)), ALU.pow)
        else:
            P.act(tmp, in_, AF.Ln, bias=bias, scale=scale)
            P.act(out, tmp, AF.Exp, scale=-0.5)

    def sigmoid(out, in_, tmp, scale=1.0, nbias=None):
        P.act(tmp, in_, AF.Exp, scale=-scale, bias=nbias)
        P.act(tmp, tmp, AF.Ln, bias=1.0)
        P.act(out, tmp, AF.Exp, scale=-1.0)

    castrr = {"n": 0}

    def load_w(dst3, src, c0, ncols, rowscale=None, p0=0, parts=128):
        ndc = dst3.shape[1]
        for dc in range(ndc):
            for q in range(0, ncols, 1024):
                n = min(1024, ncols - q)
                k = castrr["n"] % 6
                stg = xbuf[k] if k < 2 else AFa.t[:, (k - 2) * 1024:(k - 1) * 1024]
                castrr["n"] += 1
                P.dma(stg[p0:p0 + parts, 0:n], src[dc * parts:(dc + 1) * parts, c0 + q:c0 + q + n],
                      eng=("sp", "act")[castrr["n"] % 2])
                eng = ("dve", "act")[castrr["n"] % 2]
                o = dst3[:, dc, q:q + n]
                i = stg[p0:p0 + parts, 0:n]
                if rowscale is not None:
                    if eng == "act":
                        P.act(o, i, AF.Copy, scale=cvc(rowscale + dc))
                    else:
                        P.ts(eng, o, i, cvc(rowscale + dc), ALU.mult)
                else:
                    P.copy(eng, o, i)

    def stage0(blk, need_halo=True, pool=False, dest=None):
        tok0 = blk * TB
        xt = xbuf[blk % 2]
        cur = hT[blk % 2]
        prev = hT[(blk + 1) % 2]
        P.dma(xt[:], x_d[tok0:tok0 + TB, :])
        P.memset("pool", ss[:], 0.0)
        P.act(xn[:], xt[:], AF.Square, accum=ss[:])
        rsqrt(rstd[:], ss[:], ss2[:], 1e-6, scale=1.0 / DM, pool=pool)
        P.act(xn[:], xt[:], AF.Copy, scale=rstd[:, 0:1])
        for dc in range(8):
            P.tr(ptp[:, dc * 128:(dc + 1) * 128], xn[:, dc * 128:(dc + 1) * 128], identb[:])
        P.copy(DA, dest if dest is not None else cur[:, :, 3:3 + TB], v3(ptp[:, :], 8, 128))
        if need_halo:
            if blk % BPS == 0:
                P.memset("pool", cur[:, :, 0:3], 0.0)
            else:
                P.copy(PD, cur[:, :, 0:3], prev[:, :, TB:TB + 3])
        return xt, cur

    def inproj(W3, c0, cur, halo, ps):
        for dc in range(8):
            P.mm(ps[:, 0:TB + halo], W3[:, dc, c0:c0 + 128], cur[:, dc, 3 - halo:3 + TB],
                 start=(dc == 0), stop=(dc == 7))

    def neumann(nh, P0, Q0, bufs, c=None, n=64):
        Pn, Qn, Xn = bufs
        w = nh * n
        X = Xn[0]
        P.tt(DP, X, P0, identb[0:n, 0:n].unsqueeze(1).to_broadcast([n, nh, n]), ALU.add)
        Pc, Qc = P0, Q0
        for it in range(5):
            last = (it == 4)
            Pd, Qd, Xd = Pn[it % 2], Qn[it % 2], Xn[(it + 1) % 2]
            psQ = small(c)
            for h in range(nh):
                P.mm(psQ[0:n, h * n:(h + 1) * n], Pc[:, h, :], Qc[:, h, :])
            if not last:
                psP = small(c)
                for h in range(nh):
                    P.mm(psP[0:n, h * n:(h + 1) * n], Qc[:, h, :], Pc[:, h, :])
            P.copy(AD, Qd, v3(psQ[0:n, 0:w], nh, n))
            if not last:
                P.copy(DA, Pd, v3(psP[0:n, 0:w], nh, n))
            psX = small(c)
            for h in range(nh):
                P.mm(psX[0:n, h * n:(h + 1) * n], Qd[:, h, :], X[:, h, :])
            P.tt("dve", Xd, X, v3(psX[0:n, 0:w], nh, n), ALU.add)
            X = Xd
            Pc, Qc = Pd, Qd
        return X

    def phase1():
        WA.reset(); AFa.reset(); ABa.reset()
        W1 = WA.take(8, 2176)
        w2a2 = WA.take(512)
        Brk = WA.take(4, 128)
        load_w(W1, win_d, 0, 2176, rowscale=CV_G)
        P.dma(xbuf[0][0:64, 0:512], w2_d)
        P.dma(xbuf[0][64:128, 0:512], a2_d)
        P.copy(DA, w2a2[:, :], xbuf[0][:, 0:512])
        for m in range(4):
            P.ts(PD, Brk[:, m, :], bob[:], cvc(CV_RK + m), ALU.mult)
        SC = Multi(ABa, WA, YBa)
        YBa.reset()
        f = lambda: AFa.take(TB)
        mtemps = [[f() for _ in range(15)] for _ in range(2)]
        wdad = f()
        eGCs = [AFa.take(4, 2) for _ in range(2)]
        o_msq = AFa.take(4, TB)
        o_t1 = SC.take(4, TB); o_yg = SC.take(4, TB); o_yb = SC.take(4, TB)
        twad = SC.take(TB)
        sqbs = [SC.take(TB) for _ in range(2)]
        yTb = SC.take(4, TB)
        sqy = SC.take(4, TB)
        bsets = []
        for _ in range(2):
            bsets.append(dict(
                AR=SC.take(4, 2, TB),
                BK=SC.take(4, 2, TB),
                KBh=SC.take(4, 2, TB),
                vfm=SC.take(4, TB), siluz=SC.take(4, TB), rkb=SC.take(4, TB),
                TM=SC.take(4, 512)))
        psets = []
        for _ in range(2):
            psets.append(dict(
                SA=SC.take(8, 256), SK=SC.take(8, 256), Xn=[SC.take(8, 128) for _ in range(2)],
                AVb=SC.take(8, 64), U0=SC.take(8, 64), WtT=SC.take(4, 256), Ub=SC.take(8, 64),
                Yb=SC.take(512), Y2s=SC.take(512)))
        Q0 = SC.take(8, 128)
        Pn = [[SC.take(4, 128) for _ in range(2)] for _ in range(2)]
        Qn = [[SC.take(4, 128) for _ in range(2)] for _ in range(2)]
        tmp, rr_, kk_, sg, aa, rs, kkn, t1, kp, bb, Gp, eG, eGn, Dp, eD = mtemps[0]
        sqb_ = sqbs[0]

        def shift_evac(ps, j, out):
            P.act(tmp, ps[:, 1:TB + 1], AF.Copy, scale=omm[:, j:j + 1])
            P.stt("dve", out, ps[:, 0:TB], cvc(CV_MU + j), tmp, ALU.mult, ALU.add)

        for blk in range(nblk):
            P.tag = "b%d.s0" % blk
            xt, cur = stage0(blk)
            bs = bsets[blk % 2]
            AR, BK, KBh, vfm, siluz, rkb, TM = bs["AR"], bs["BK"], bs["KBh"], bs["vfm"], bs["siluz"], bs["rkb"], bs["TM"]
            eGC = eGCs[blk % 2]
            tmp, rr_, kk_, sg, aa, rs, kkn, t1, kp, bb, Gp, eG, eGn, Dp, eD = mtemps[0]
            if blk % BPS == 0:
                P.memset("pool", Mst[:], 0.0)
                P.memset("pool", Mb[:], 0.0)
            ps = big()
            inproj(W1, 1536, cur, 1, ps)
            shift_evac(ps, 12, wdad)
            sigmoid(tmp[0:64, :], wdad[0:64, :], rs[0:64, :], scale=2.0)
            P.ts(PD, twad[0:64, :], tmp[0:64, :], 2.0, ALU.mult, -1.0, ALU.add)
            P.copy(AD, twad[64:128, :], wdad[64:128, :])
            for m in range(4):
                tmp, rr_, kk_, sg, aa, rs, kkn, t1, kp, bb, Gp, eG, eGn, Dp, eD = mtemps[m % 2]
                sqb_ = sqbs[m % 2]
                P.tag = "b%d.A%d" % (blk, m)
                ps = big(); inproj(W1, m * 128, cur, 1, ps); shift_evac(ps, m, rr_)
                ps = big(); inproj(W1, 512 + m * 128, cur, 1, ps); shift_evac(ps, 4 + m, kk_)
                ps = big(); inproj(W1, 1024 + m * 128, cur, 1, ps); shift_evac(ps, 8 + m, vfm[:, m, :])
                ps = big(); inproj(W1, 1664 + m * 128, cur, 0, ps)
                P.act(aa, ps[:, 0:TB], AF.Copy)
                sigmoid(sg, ps[:, 0:TB], rs)
                P.tt(PD, siluz[:, m, :], sg, aa, ALU.mult)
                ps = big()
                P.mm(ps[:, 0:TB], w2a2[0:64, m * 128:(m + 1) * 128], twad[0:64, :])
                sigmoid(sg, ps[:, 0:TB], rs, nbias=cvnc(m))
                ps = big()
                P.mm(ps[:, 0:TB], w2a2[64:128, m * 128:(m + 1) * 128], twad[64:128, :])
                sigmoid(aa, ps[:, 0:TB], rs, nbias=cvnc(4 + m))
                P.act(sqb_, kk_, AF.Square, scale=cvc(CV_KK + m))
                ps = big()
                P.mm(ps[:, 0:TB], bob[:], sqb_)
                rsqrt(rs, ps[:, 0:TB], t1, 1e-12)
                P.stt("dve", kkn, kk_, cvc(CV_KK + m), rs, ALU.mult, ALU.mult)
                P.ts(PD, t1, aa, -1.0, ALU.add, cvc(CV_KA + m), ALU.mult)
                P.stt("dve", kp, t1, 1.0, kk_, ALU.add, ALU.mult)
                P.tt(PD, bb, kkn, aa, ALU.mult)
                P.add("dve", lambda e, o=Gp, a=resetm[:], b=sg: e.tensor_tensor_scan(
                    o, a, b, 0.0, ALU.mult, ALU.add), [resetm[:], sg], [Gp])
                P.act(eG, Gp, AF.Exp, scale=-CDEC)
                P.act(eGn, Gp, AF.Exp, scale=CDEC)
                Gp3 = v3(Gp, 2, 64)
                P.tt(DP, v3(Dp, 2, 64), Gp3, Gp3[:, :, 63:64].to_broadcast([128, 2, 64]), ALU.subtract)
                P.act(eD, Dp, AF.Exp, scale=CDEC)
                eG3 = v3(eG, 2, 64)
                kk3 = v3(kkn, 2, 64)
                At3 = v3(AR[:, m, 0, :], 2, 64)
                P.stt("dve", At3[:, :, 1:64], kk3[:, :, 1:64], -1.0, eG3[:, :, 0:63], ALU.mult, ALU.mult)
                P.ts("dve", At3[:, :, 0:1], kk3[:, :, 0:1], -1.0, ALU.mult)
                P.tt(PD, AR[:, m, 1, :], rr_, eG, ALU.mult)
                P.tt(PD, BK[:, m, 0, :], bb, eGn, ALU.mult)
                P.tt(DP, BK[:, m, 1, :], kp, eGn, ALU.mult)
                P.tt(PD, KBh[:, m, 0, :], kp, eD, ALU.mult)
                P.tt(DP, KBh[:, m, 1, :], bb, eD, ALU.mult)
                P.tt(PD, rkb[:, m, :], rr_, kp, ALU.mult)
                P.copy(DA, eGC[:, m, :], eG3[:, :, 63])
                tap("t_r", rr_); tap("t_k", kk_); tap("t_v", vfm[:, m, :]); tap("t_sg", sg); tap("t_a", aa)
                tap("t_kkn", kkn); tap("t_kp", kp); tap("t_Gp", Gp); tap("t_eG", eG); tap("t_At", AR[:, m, 0, :])
                tap("t_wdad", wdad); tap("t_eD", eD)
            pYT = plong
            pp = psets[blk % 2]
            SA, SK, Xn, AVb, U0, WtT, Ub, Yb, Y2s = (pp[k] for k in ("SA", "SK", "Xn", "AVb", "U0", "WtT", "Ub", "Yb", "Y2s"))
            P.tag = "b%d.Bpar0" % blk
            srcs = [lambda m: AR[:, m, 0, :], lambda m: KBh[:, m, 0, :], lambda m: KBh[:, m, 1, :], lambda m: vfm[:, m, :]]
            for kind in range(4):
                ps = small()
                for m in range(4):
                    P.mm(ps[:, m * 128:(m + 1) * 128], srcs[kind](m), identb[:, :])
                P.copy(AD if kind % 2 == 0 else DA, TM[:, kind, :], ps[:, :])
            msk = maskSI128[:, :].unsqueeze(1).to_broadcast([128, 2, 256])
            for g in range(4):
                psA = small(); psK = small()
                for e in (0, 1):
                    pb = e * 64
                    P.mm(psA[:, e * 256:(e + 1) * 256], BK[pb:pb + 64, g, 0, :], AR[pb:pb + 64, g, :, :])
                    P.mm(psK[:, e * 256:(e + 1) * 256], BK[pb:pb + 64, g, 1, :], AR[pb:pb + 64, g, :, :])
                P.tt("dve", SA[:, 2 * g:2 * g + 2, :], v3(psA[:, :], 2, 256), msk, ALU.mult)
                P.tt("dve", SK[:, 2 * g:2 * g + 2, :], v3(psK[:, :], 2, 256), msk, ALU.mult)
            for half in range(2):
                psQ = small()
                for hl in (0, 2, 1, 3):
                    h = half * 4 + hl
                    m, pb = h // 2, (h % 2) * 64
                    P.mm(psQ[:, hl * 128:(hl + 1) * 128], AR[pb:pb + 64, m, 0, :], BK[pb:pb + 64, m, 0, :])
                P.tt("dve", Q0[:, half * 4:(half + 1) * 4, :], v3(psQ[:, :], 4, 128),
                     mk128b[:, 2, :].unsqueeze(1).to_broadcast([128, 4, 128]), ALU.mult)
            for half in range(2):
                hs = slice(half * 4, (half + 1) * 4)
                neumann(4, SA[:, hs, 0:128], Q0[:, hs, :], (Pn[half], Qn[half], [Xn[0][:, hs, :], Xn[1][:, hs, :]]), None, n=128)
            X = Xn[1]
            ps = small()
            for h in range(8):
                P.mm(ps[:, h * 64:(h + 1) * 64], SK[:, h, 0:128], TM[:, 3, h * 64:(h + 1) * 64])
            P.copy(AD, AVb, v3(ps[:, :], 8, 64))
            ps = small()
            for h in range(8):
                P.mm(ps[:, h * 64:(h + 1) * 64], X[:, h, :], AVb[:, h, :])
            P.copy(AD, U0, v3(ps[:, :], 8, 64))
            for q2 in range(2):
                ps = small()
                for ml in range(2):
                    m = q2 * 2 + ml
                    P.mm(ps[:, ml * 256:(ml + 1) * 256], TM[:, 0, m * 128:(m + 1) * 128], X[:, 2 * m:2 * m + 2, :])
                P.copy(DA, WtT[:, q2 * 2:q2 * 2 + 2, :], v3(ps[:, :], 2, 256))
            for c in range(2):
                pc = c * 64
                rows = slice(pc, pc + 64)
                P.tag = "b%d.Bseq%d" % (blk, c)
                psU = small()
                psY2 = small()
                for h in HORD:
                    m, pb, e = h // 2, (h % 2) * 64, h % 2
                    P.mm(psU[:, h * 64:(h + 1) * 64], WtT[pb:pb + 64, m, e * 128:(e + 1) * 128],
                         Mb[pb:pb + 64, m, e * 64:(e + 1) * 64])
                    P.mm(psY2[:, h * 64:(h + 1) * 64], AR[pb:pb + 64, m, 1, :], Mb[pb:pb + 64, m, e * 64:(e + 1) * 64])
                P.tt("dve", Ub[rows], v3(psU[rows, :], 8, 64), U0[rows], ALU.add)
                P.copy(AD, Y2s[rows, :], psY2[rows, :])
                psY = small()
                for h in range(8):
                    o = psY[:, h * 64:(h + 1) * 64]
                    P.mm(o, SK[rows, h, 128:256], TM[rows, 3, h * 64:(h + 1) * 64], start=True, stop=False)
                    P.mm(o, SA[rows, h, 128:256], Ub[rows, h, :], start=False, stop=True)
                P.tt("dve", Yb[rows, :], psY[rows, :], Y2s[rows, :], ALU.add)
                psM = small()
                for m in range(4):
                    o = psM[:, m * 128:(m + 1) * 128]
                    P.mm(o, TM[rows, 1, m * 128:(m + 1) * 128], TM[rows, 3, m * 128:(m + 1) * 128], start=True, stop=False)
                    P.mm(o, TM[rows, 2, m * 128:(m + 1) * 128], Ub[rows, 2 * m:2 * m + 2, :], start=False, stop=True)
                P.tt(PD, Mst[:], Mst[:], eGC[:, :, c:c + 1].to_broadcast([128, 4, 128]), ALU.mult)
                P.tt("dve", Mst[:], Mst[:], v3(psM[:, :], 4, 128), ALU.add)
                P.copy(AD, Mb[:], Mst[:])
                for m in range(4):
                    P.mm(pYT[:, m * 128 + c * 64:m * 128 + (c + 1) * 64], Yb[rows, m * 128:(m + 1) * 128], identb[rows, pc:pc + 64])
            P.tag = "b%d.C" % blk
            P.copy(DA, yTb, v3(pYT[:, :], 4, TB))
            P.act(sqy, yTb, AF.Square)
            psm_ = small(0); pse_ = small(1)
            for m in range(4):
                P.mm(psm_[:, m * 128:(m + 1) * 128], bo64b[:], yTb[:, m, :])
            for m in range(4):
                P.mm(pse_[:, m * 128:(m + 1) * 128], bo64b[:], sqy[:, m, :])
            P.tt("dve", o_t1, yTb, v3(psm_[:, :], 4, TB), ALU.subtract)
            P.act(o_msq, v3(psm_[:, :], 4, TB), AF.Square)
            P.tt("dve", o_msq, v3(pse_[:, :], 4, TB), o_msq, ALU.subtract)
            P.ts("dve", o_msq, o_msq, 0.0, ALU.max)
            rsqrt(o_msq, o_msq, o_msq, 64e-5)
            P.tt(PD, o_t1, o_t1, o_msq, ALU.mult)
            for m in range(4):
                P.act(o_yg[:, m, :], o_t1[:, m, :], AF.Identity, scale=cvc(CV_GNW + m), bias=cvc(CV_GNB + m))
            psb_ = small(0)
            for m in range(4):
                P.mm(psb_[:, m * 128:(m + 1) * 128], Brk[:, m, :], rkb[:, m, :])
            P.tt("dve", o_yb, v3(psb_[:, :], 4, TB), vfm, ALU.mult)
            tap("t_yg", o_yg.rearrange("p a b -> p (a b)")); tap("t_yb", o_yb.rearrange("p a b -> p (a b)"))
            tap("t_yT", yTb.rearrange("p a b -> p (a b)"))
            P.tt(PD, o_yg, o_yg, o_yb, ALU.add)
            P.tt(PD, YA[:, :, blk * TB:(blk + 1) * TB], o_yg, siluz, ALU.mult)

    def phase2():
        WA.reset(); AFa.reset(); ABa.reset()
        W2 = WA.take(8, 2056)
        load_w(W2, win_d, 2176, 2056, rowscale=CV_G)
        SC = Multi(ABa, WA)
        f = lambda: AFa.take(TB)
        jtemps = [[f() for _ in range(4)] for _ in range(2)]
        sqbs = [SC.take(TB) for _ in range(2)]
        gTri = AFa.take(4, 128)
        E0 = AFa.take(4, 128)
        EMt = AFa.take(4, 128)
        o_t = AFa.take(4, TB)
        o_r = AFa.take(4, TB)
        tba = AFa.take(4)
        gg = AFa.take(4)
        gcs = AFa.take(4)
        dgl = AFa.take(4)
        mk128f = AFa.take(3, 128)
        ones128f = AFa.take(128)
        for q_ in range(3):
            P.dma(mk128f[:, q_, :], mk128_d[:, q_, :])
        P.memset("pool", ones128f, 1.0)
        oTb = SC.take(4, TB)
        sqo = SC.take(4, TB)
        bsets = []
        for _ in range(2):
            bsets.append(dict(
                beta=AFa.take(4), nbeta=AFa.take(4), egc=AFa.take(4), ekt=AFa.take(4), egl=AFa.take(8),
                QKfm=SC.take(4, 2, TB),
                vfm=SC.take(4, TB), siluz=SC.take(4, TB), qdT=SC.take(4, TB),
                EM=SC.take(4, 2, 128),
                kg=SC.take(4, 128), kte=SC.take(4, 128), vtm=SC.take(4, 128), PQ=SC.take(4, 256),
                Xn=[SC.take(4, 128) for _ in range(2)], WnT=SC.take(4, 128), vnew=SC.take(4, 128), ob=SC.take(512)))
        Q0 = SC.take(4, 128)
        Pn = [SC.take(4, 128) for _ in range(2)]
        Qn = [SC.take(4, 128) for _ in range(2)]

        for blk in range(nblk):
            xt, cur = stage0(blk)
            bs = bsets[blk % 2]
            beta, nbeta, egc, ekt, egl, QKfm, vfm, siluz, qdT, EM, kg, kte, vtm, PQ, Xn, WnT, vnew, ob = (bs[k] for k in (
                "beta", "nbeta", "egc", "ekt", "egl", "QKfm", "vfm", "siluz", "qdT", "EM", "kg", "kte", "vtm", "PQ",
                "Xn", "WnT", "vnew", "ob"))
            if blk % BPS == 0:
                P.memset("pool", Sst[:], 0.0)
                P.memset("pool", Sb[:], 0.0)
            for j in range(12):
                accA, accB, sv, rs = jtemps[j % 2]
                sqb_ = sqbs[j % 2]
                ps = big()
                inproj(W2, j * 128, cur, 3, ps)
                cw = lambda i: cvc(CV_CONV + i * 12 + j)
                P.act(accA, ps[:, 3:TB + 3], AF.Copy, scale=cw(3))
                P.stt("dve", accB, ps[:, 2:TB + 2], cw(2), accA, ALU.mult, ALU.add)
                P.stt("dve", accA, ps[:, 1:TB + 1], cw(1), accB, ALU.mult, ALU.add)
                P.stt("dve", accB, ps[:, 0:TB], cw(0), accA, ALU.mult, ALU.add)
                h = j % 4
                sigmoid(accA, accB, accA)
                if j >= 8:
                    P.tt(PD, vfm[:, h, :], accA, accB, ALU.mult)
                else:
                    P.tt(PD, sv, accA, accB, ALU.mult)
                    P.act(sqb_, sv, AF.Square)
                    ps2 = big()
                    P.mm(ps2[:, 0:TB], (ones128b if j < 4 else onesb)[:], sqb_)
                    eps = 128e-12 if j < 4 else 1e-12
                    rsqrt(rs, ps2[:, 0:TB], accA, eps)
                    P.tt(PD, QKfm[:, h, 1 if j < 4 else 0, :], sv, rs, ALU.mult)
            for h in range(4):
                accA, accB, sv, rs = jtemps[h % 2]
                ps = big()
                inproj(W2, 1536 + h * 128, cur, 0, ps)
                P.act(accB, ps[:, 0:TB], AF.Copy)
                sigmoid(accA, ps[:, 0:TB], accA)
                P.tt(PD, siluz[:, h, :], accA, accB, ALU.mult)
            psba = small()
            for dc in range(8):
                P.mm(psba[:, 0:8], cur[:, dc, 3:3 + TB], W2[:, dc, 2048:2056], start=(dc == 0), stop=(dc == 7))
            sigmoid(beta, psba[:, 0:4], beta)
            P.tt("dve", tba, psba[:, 4:8], gvb128[:, 4:8], ALU.add)
            P.act(tba, tba, AF.Exp)
            P.act(tba, tba, AF.Ln, bias=1.0)
            P.tt("dve", gg, tba, negA128[:, :], ALU.mult)
            P.ts(PD, nbeta, beta, -1.0, ALU.mult)
            ps = small()
            P.mm(ps[:, 0:4], mk128f[:, 0, :], gg)
            P.act(egc, ps[:, 0:4], AF.Exp)
            P.copy(AD, gcs, ps[:, 0:4])
            ps = small()
            for c in range(2):
                P.mm(ps[:, c * 4:(c + 1) * 4], ones128f[c * 64:(c + 1) * 64, :], gg[c * 64:(c + 1) * 64, :])
            P.act(egl, ps[:, 0:8], AF.Exp)
            for c in range(2):
                rows = slice(c * 64, (c + 1) * 64)
                P.tt("dve", dgl[rows, :], ps[rows, c * 4:(c + 1) * 4], gcs[rows, :], ALU.subtract)
            P.act(ekt, dgl, AF.Exp)
            P.tt(DP, gTri, mk128f[:, 0, :].unsqueeze(1).to_broadcast([128, 4, 128]),
                 gg.unsqueeze(2).to_broadcast([128, 4, 128]), ALU.mult)
            ps = small()
            P.mm(ps[:, :], mk128f[:, 2, :], gTri.rearrange("p a b -> p (a b)"))
            P.act(E0, v3(ps[:, :], 4, 128), AF.Exp)
            P.tt(PD, EM[:, :, 1, :], E0, mk128f[:, 0, :].unsqueeze(1).to_broadcast([128, 4, 128]), ALU.mult)
            P.tt(DP, EMt, E0, nbeta.unsqueeze(2).to_broadcast([128, 4, 128]), ALU.mult)
            P.tt(PD, EM[:, :, 0, :], EMt, mk128f[:, 1, :].unsqueeze(1).to_broadcast([128, 4, 128]), ALU.mult)
            P.tt(DP, gTri, identb[:, :].unsqueeze(1).to_broadcast([128, 4, 128]),
                 egc.unsqueeze(2).to_broadcast([128, 4, 128]), ALU.mult)
            ps = small()
            P.mm(ps[:, :], ones128f, gTri.rearrange("p a b -> p (a b)"))
            P.tt("dve", qdT, QKfm[:, :, 1, :], v3(ps[:, :], 4, 128), ALU.mult)
            pOT = plong
            ps = small()
            for h in range(4):
                P.mm(ps[:, h * 128:(h + 1) * 128], QKfm[:, h, 0, :], identb[:, :])
            P.tt("dve", kg, v3(ps[:, :], 4, 128), egc.unsqueeze(2).to_broadcast([128, 4, 128]), ALU.mult)
            P.tt("dve", kte, v3(ps[:, :], 4, 128), ekt.unsqueeze(2).to_broadcast([128, 4, 128]), ALU.mult)
            ps = small()
            for h in range(4):
                P.mm(ps[:, h * 128:(h + 1) * 128], vfm[:, h, :], identb[:, :])
            P.copy(AD, vtm, v3(ps[:, :], 4, 128))
            for q2 in range(2):
                ps = small()
                for hl in range(2):
                    h = q2 * 2 + hl
                    P.mm(ps[:, hl * 256:(hl + 1) * 256], QKfm[:, h, 0, :], QKfm[:, h, :, :])
                P.tt("dve", PQ[:, q2 * 2:q2 * 2 + 2, :], v3(ps[:, :], 2, 256),
                     EM[:, q2 * 2:q2 * 2 + 2, :, :].rearrange("p a b c -> p a (b c)"), ALU.mult)
            ps = small()
            for h in range(4):
                P.mm(ps[:, h * 128:(h + 1) * 128], PQ[:, h, 0:128], identb[:, :])
            P.copy(AD, Q0, v3(ps[:, :], 4, 128))
            X = neumann(4, PQ[:, :, 0:128], Q0, (Pn, Qn, Xn), None, n=128)
            ps = small()
            for h in range(4):
                P.mm(ps[:, h * 128:(h + 1) * 128], kg[:, h, :], X[:, h, :])
            P.ts("dve", WnT, v3(ps[:, :], 4, 128), -1.0, ALU.mult)
            for c in range(2):
                pc = c * 64
                rows = slice(pc, pc + 64)
                psV = small()
                for h in range(4):
                    o = psV[:, h * 128:(h + 1) * 128]
                    P.mm(o, X[rows, h, :], vtm[rows, h, :], start=True, stop=False)
                    P.mm(o, WnT[:, h, :], Sb[:, h, :], start=False, stop=True)
                P.tt("dve", vnew[rows], v3(psV[rows, :], 4, 128), beta[rows, :].unsqueeze(2).to_broadcast([64, 4, 128]), ALU.mult)
                psO = small()
                for h in range(4):
                    o = psO[:, h * 128:(h + 1) * 128]
                    P.mm(o, qdT[:, h, :], Sb[:, h, :], start=True, stop=False)
                    P.mm(o, PQ[rows, h, 128:256], vnew[rows, h, :], start=False, stop=True)
                P.copy(AD, ob[rows, :], psO[rows, :])
                psS = small()
                for h in range(4):
                    P.mm(psS[:, h * 128:(h + 1) * 128], kte[rows, h, :], vnew[rows, h, :])
                P.tt(PD, Sst[:], Sst[:], egl[:, c * 4:(c + 1) * 4].unsqueeze(2).to_broadcast([128, 4, 128]), ALU.mult)
                P.tt("dve", Sst[:], Sst[:], v3(psS[:, :], 4, 128), ALU.add)
                P.copy(AD, Sb[:], Sst[:])
                for h in range(4):
                    P.mm(pOT[:, h * 128 + c * 64:h * 128 + (c + 1) * 64], ob[rows, h * 128:(h + 1) * 128], identb[rows, pc:pc + 64])
            P.copy(DA, oTb, v3(pOT[:, :], 4, TB))
            P.act(sqo, oTb, AF.Square)
            ps = small(0)
            for h in range(4):
                P.mm(ps[:, h * 128:(h + 1) * 128], onesdb[:], sqo[:, h, :])
            rsqrt(o_r, v3(ps[:, :], 4, TB), o_t, 1e-6)
            P.tt(PD, o_t, oTb, o_r, ALU.mult)
            P.stt("dve", YB[:, :, blk * TB:(blk + 1) * TB], o_t, cvc(CV_ONW), siluz, ALU.mult, ALU.mult)

    def phase3():
        WA.reset(); AFa.reset(); ABa.reset()
        Wg = WA.take(8, 2048)
        Wa = WA.take(4, 1024)
        Wb = WA.take(4, 1024)
        Wo = WA.take(8, 1024)
        load_w(Wg, win_d, 4232, 2048, rowscale=CV_G)
        load_w(Wa, wa_d, 0, 1024)
        load_w(Wb, wb_d, 0, 1024)
        load_w(Wo, wo_d, 0, 1024)
        T3 = 4 * TB
        xrs = [AFa.take(DM) for _ in range(2)]
        xres = AFa.take(DM)
        nwbc = AFa.take(DM)
        hT3 = [ABa.take(8, T3) for _ in range(2)]
        mgs = [ABa.take(8, T3) for _ in range(2)]
        sa = ABa.take(T3); sbb = ABa.take(T3)
        xn3 = ABa.take(2 * T3)
        t1 = xn3[:, 0:T3]; t2 = xn3[:, T3:2 * T3]
        P.dma(nwbc, nwo_d.partition_broadcast(128))
        for B in range(nblk // 4):
            h3 = hT3[B % 2]
            mg = mgs[B % 2]
            for t in range(4):
                stage0(B * 4 + t, need_halo=False, pool=True, dest=h3[:, :, t * TB:(t + 1) * TB])
            tok = slice(B * T3, (B + 1) * T3)
            for cc in range(8):
                psa = big3()
                for dc in range(8):
                    P.mm(psa[:, :], Wg[:, dc, cc * 128:(cc + 1) * 128], h3[:, dc, :], start=(dc == 0), stop=(dc == 7))
                psb = big3()
                for dc in range(8):
                    P.mm(psb[:, :], Wg[:, dc, 1024 + cc * 128:1024 + (cc + 1) * 128], h3[:, dc, :], start=(dc == 0), stop=(dc == 7))
                pya = small()
                for kc in range(4):
                    P.mm(pya[:, :], Wa[:, kc, cc * 128:(cc + 1) * 128], YA[:, kc, tok], start=(kc == 0), stop=(kc == 3))
                pyb = small()
                for kc in range(4):
                    P.mm(pyb[:, :], Wb[:, kc, cc * 128:(cc + 1) * 128], YB[:, kc, tok], start=(kc == 0), stop=(kc == 3))
                P.act(sa, psa[:, :], AF.Sigmoid)
                P.act(sbb, psb[:, :], AF.Sigmoid)
                P.tt("dve", t1, pya[:, :], sa, ALU.mult)
                P.tt("dve", t2, pyb[:, :], sbb, ALU.mult)
                P.tt(PD, mg[:, cc, :], t1, t2, ALU.add)
            for t in range(4):
                blk = B * 4 + t
                xr = xrs[blk % 2]
                rows = slice(blk * TB, (blk + 1) * TB)
                P.dma(xres, x_d[rows, :])
                for half in range(2):
                    pso = big3()
                    for kc in range(8):
                        P.mm(pso[:, :], mg[:, kc, t * TB:(t + 1) * TB], Wo[:, kc, half * 512:(half + 1) * 512],
                             start=(kc == 0), stop=(kc == 7))
                    P.tt("dve", xr[:, half * 512:(half + 1) * 512], pso[:, :], xres[:, half * 512:(half + 1) * 512], ALU.add)
                P.memset("pool", ss3[:], 0.0)
                P.act(xn3, xr, AF.Square, accum=ss3[:])
                rsqrt(rstd3[:], ss3[:], ss23[:], 1e-6, scale=1.0 / DM, pool=True)
                P.stt("dve", xr, xr, rstd3[:, 0:1], nwbc, ALU.mult, ALU.mult)
                P.dma(out_d[rows, :], xr)

    if 1 in phases:
        phase1()
    if 2 in phases:
        phase2()
    if 3 in phases:
        phase3()
    if dbg:
        if "YA" in dbg_d:
            stg = AFa.t
            for m in range(4):
                for q in range(0, dbg["YA"][2], 1024):
                    n = min(1024, dbg["YA"][2] - q)
                    P.copy(DA, stg[:, 0:n], YA[:, m, q:q + n])
                    P.dma(dbg_d["YA"][:, m, q:q + n], stg[:, 0:n])
        if "YB" in dbg_d:
            stg = AFa.t
            for m in range(4):
                for q in range(0, dbg["YB"][2], 1024):
                    n = min(1024, dbg["YB"][2] - q)
                    P.copy(DA, stg[:, 0:n], YB[:, m, q:q + n])
                    P.dma(dbg_d["YB"][:, m, q:q + n], stg[:, 0:n])
    if trunc:
        P.ops = P.ops[:trunc]
    if SCHED:
        P.schedule()
    P.emit()
    return nc, P


def host_inputs(inputs):
    f = lambda a: np.ascontiguousarray(np.asarray(a, dtype=np.float32))
    x = f(inputs["x"])
    vec4 = lambda v: f(v).reshape(-1, 128).T
    cw = f(inputs["gd_conv_w"])[0]
    cols = [vec4(inputs["norm_in_w"][0]), vec4(inputs["rw_mu"][0]), vec4(inputs["rw_w0"][0]),
            vec4(inputs["rw_a0"][0]), vec4(inputs["rw_k_k"][0]), vec4(inputs["rw_k_a"][0]),
            vec4(f(inputs["rw_r_k"])[0].reshape(-1)), vec4(inputs["rw_gn_w"][0]), vec4(inputs["rw_gn_b"][0])]
    cols += [vec4(cw[i]) for i in range(4)]
    cols += [f(inputs["gd_o_norm_w"])[0].reshape(128, 1)]
    cvh = np.ascontiguousarray(np.concatenate(cols, axis=1))
    assert cvh.shape == (128, NCV), cvh.shape
    p = np.arange(128)
    bo = (p[:, None] // 64 == p[None, :] // 64).astype(np.float32)
    q = np.arange(64)
    masks = np.stack([(q[:, None] <= q[None, :]), (q[:, None] < q[None, :]), (q[:, None] > q[None, :])],
                     axis=1).astype(np.float32)
    resetm = np.ones((128, 128), np.float32)
    resetm[:, 0] = 0.0
    resetm[:, 64] = 0.0
    shared = {
        "w_in": f(inputs["w_in"])[0], "w_a": f(inputs["w_branch_a"])[0], "w_b": f(inputs["w_branch_b"])[0],
        "w_o": f(inputs["w_out"])[0], "w2": f(inputs["rw_w2"])[0], "a2": f(inputs["rw_a2"])[0],
        "cv": cvh, "nwo": f(inputs["norm_out_w"]).reshape(1, DM),
        "gvec": np.concatenate([f(inputs["gd_A_log"])[0], f(inputs["gd_dt_bias"])[0]]).reshape(1, 8),
        "ident": np.eye(128, dtype=np.float32), "bo": bo, "masks": np.ascontiguousarray(masks), "resetm": resetm,
        "masks128": np.ascontiguousarray(np.kron(np.eye(2, dtype=np.float32)[:, None, :], masks).astype(np.float32)),
    }
    in_maps = []
    for c in range(NCORES):
        m = dict(shared)
        m["x"] = np.ascontiguousarray(x[NSEQ * c:NSEQ * (c + 1)].reshape(NTOK, DM))
        in_maps.append(m)
    return in_maps


def kernel(**inputs):
    in_maps = host_inputs(inputs)
    nc, _ = build_nc()
    res = run_bass_kernel_spmd(nc, in_maps, core_ids=list(range(NCORES)))
    outs = [np.asarray(r["out"], dtype=np.float32).reshape(NSEQ, SEQ, DM) for r in res.results]
    return np.concatenate(outs, axis=0)
```

```python
import math
import numpy as np
import concourse.bass as bass
import concourse.mybir as mybir
from concourse.bass_utils import run_bass_kernel_spmd

F32 = mybir.dt.float32
BF16 = mybir.dt.bfloat16
ALU = mybir.AluOpType
AF = mybir.ActivationFunctionType


class Op:
    __slots__ = ("eng", "fn", "boxes_r", "boxes_w", "deps", "sig", "cnt", "idx",
                 "dsem", "dcnt", "dprev", "alldeps", "cost", "rows", "succ", "prio", "nin", "rt", "fin", "tag", "st", "alts", "wsz", "psum", "vc", "pos", "edeps")

    def __init__(self, eng, fn):
        self.eng = eng
        self.fn = fn
        self.deps = set()
        self.alldeps = set()
        self.cost = 0.3
        self.rows = None
        self.alts = None
        self.sig = False
        self.cnt = 0
        self.dsem = -1
        self.dcnt = 0
        self.dprev = None


def _box(ap):
    t = ap.tensor
    name = t.name
    pat = ap.ap
    off = ap.offset
    sp = str(ap.space)
    if "PSUM" in sp.upper():
        return (name, 0, 128, 0, 1 << 40)
    if "SB" in sp.upper():
        shp = t.shape
        F = 1
        for s in shp[1:]:
            F *= s
        p0 = off // F
        f0 = off % F
        npart = pat[0][1]
        ext = 1
        for st, c in pat[1:]:
            ext += (c - 1) * abs(st)
        return (name, p0, p0 + npart, f0, f0 + ext)
    ext = 1
    for st, c in pat:
        ext += (c - 1) * abs(st)
    return (name, 0, 1, off, off + ext)


def _fsize(ap):
    n = 1
    for st, c in ap.ap[1:]:
        n *= c
    return n


def _ovl(a, b):
    return a[1] < b[2] and b[1] < a[2] and a[3] < b[4] and b[3] < a[4]


def _covers(a, b):
    return a[1] <= b[1] and a[2] >= b[2] and a[3] <= b[3] and a[4] >= b[4]


PE_STANDALONE_WAITS = False
TRANSITIVE = True
LAST_ONLY = True
SCHED_EPS = 0.1


class Prog:
    NDMA = 16

    def __init__(self, nc):
        self.nc = nc
        self.ops = []
        self.hist = {}
        self.engs = {"pe": nc.tensor, "act": nc.scalar, "dve": nc.vector,
                     "pool": nc.gpsimd, "sp": nc.sync}

    def add(self, eng, fn, reads, writes, dma=False):
        alts = None
        if isinstance(eng, tuple):
            alts, eng = eng, eng[0]
        op = Op(eng, fn)
        op.alts = alts
        op.idx = len(self.ops)
        op.tag = getattr(self, 'tag', '')
        op.dsem = 0 if dma else -1
        br = [_box(a) for a in reads]
        bw = [_box(a) for a in writes]
        bw = bw + [b for b in br if b[4] == (1 << 40)]
        br = [b for b in br if b[4] != (1 << 40)]
        for b in br:
            for (hb, hop, hw) in self.hist.get(b[0], ()):
                if hw and _ovl(hb, b):
                    self._dep(op, hop, raw=True)
        for b in bw:
            for (hb, hop, hw) in self.hist.get(b[0], ()):
                if _ovl(hb, b):
                    self._dep(op, hop, raw=False)
        for b in bw:
            lst = self.hist.setdefault(b[0], [])
            lst[:] = [e for e in lst if not _covers(b, e[0])]
            lst.append((b, op, True))
        for b in br:
            self.hist.setdefault(b[0], []).append((b, op, False))
        self.ops.append(op)
        wsz = _fsize(writes[0]) if writes else 64
        psum = any(b[4] == (1 << 40) for b in bw)
        op.wsz = wsz
        op.psum = psum
        if dma:
            op.cost = 2.0 + 0.004 * wsz
        else:
            op.cost = self._ecost(eng, wsz, psum, op.cost)
        return op

    @staticmethod
    def _ecost(eng, wsz, psum, default):
        if eng == "act":
            return 0.2 + 0.00085 * wsz
        if eng == "dve":
            return (0.1 if psum else 0.07) + 0.00105 * wsz
        if eng == "pool":
            return 0.12 + 0.0021 * wsz
        return default

    def schedule(self):
        ops = self.ops
        for o in ops:
            o.succ = []
        for o in ops:
            for d in o.alldeps:
                d.succ.append(o)
            o.nin = len(o.alldeps)
        for o in reversed(ops):
            p = 0.0
            for q in o.succ:
                if q.prio > p:
                    p = q.prio
            o.prio = p + o.cost
        free = {e: 0.0 for e in self.engs}
        ready = {e: [] for e in self.engs}
        def push(o):
            for e in (o.alts or (o.eng,)):
                ready[e].append(o)

        for o in ops:
            if o.nin == 0:
                o.rt = 0.0
                push(o)
        order = []
        n = len(ops)
        pe_rows = None

        def pe_pen(o):
            if o.eng != "pe" or o.rows is None or pe_rows is None or o.rows == pe_rows:
                return 0.0
            if pe_rows[1] <= o.rows[0] or o.rows[1] <= pe_rows[0]:
                return 0.25
            return 0.07

        while len(order) < n:
            best = None
            for e, lst in ready.items():
                if not lst:
                    continue
                t = free[e]
                cand = None
                for o in lst:
                    st = (o.rt if o.rt > t else t) + pe_pen(o)
                    if o.alts:
                        ce = self._ecost(e, o.wsz, o.psum, o.cost)
                        st += ce - min(self._ecost(a, o.wsz, o.psum, o.cost) for a in o.alts)
                    key = (int(st / SCHED_EPS), -o.prio, o.idx, st)
                    if cand is None or key < cand[0]:
                        cand = (key, o, e)
                if best is None or cand[0] < best[0]:
                    best = cand
            key, o, e = best
            if o.alts:
                for a in o.alts:
                    ready[a].remove(o)
                t = free[e]
                o.eng = e
                o.cost = self._ecost(e, o.wsz, o.psum, o.cost)
                st = o.rt if o.rt > t else t
            else:
                st = key[3]
                ready[o.eng].remove(o)
            if o.eng == "pe" and o.rows is not None:
                pe_rows = o.rows
            if o.dsem >= 0:
                free[o.eng] = st + 0.06
            else:
                free[o.eng] = st + o.cost
            o.fin = st + o.cost
            o.st = st
            order.append(o)
            for q in o.succ:
                q.nin -= 1
                if q.nin == 0:
                    rt = 0.0
                    for d in q.alldeps:
                        lat = d.fin + (0.05 if d.eng == q.eng else 0.25)
                        if lat > rt:
                            rt = lat
                    q.rt = rt
                    push(q)
        self.ops = order
        self.est = max(o.fin for o in order)

    def _dep(self, op, src, raw):
        if src is op:
            return
        op.alldeps.add(src)
        if src.eng == op.eng and src.dsem < 0:
            if op.eng == "pe":
                return
        op.deps.add(src)

    def emit(self, final_wait=True):
        nc = self.nc
        names = ["pe", "act", "dve", "pool"]
        sems = {e: nc.alloc_semaphore("s_" + e) for e in names}
        dsems = [nc.alloc_semaphore("s_dma%d" % i) for i in range(self.NDMA)]
        last = None
        for op in self.ops:
            if op.eng == "pe" and op.rows is not None:
                if last is not None and (last.rows[1] <= op.rows[0] or op.rows[1] <= last.rows[0]):
                    op.deps.add(last)
                last = op
        for i, op in enumerate(self.ops):
            op.pos = i
        for op in self.ops:
            last = {}
            eff = []
            for d in op.deps:
                if d.dsem >= 0 or not LAST_ONLY:
                    eff.append(d)
                elif d.eng not in last or d.pos > last[d.eng].pos:
                    last[d.eng] = d
            eff.extend(last.values())
            op.edeps = eff
            for d in eff:
                d.sig = True
        cnt = {e: 0 for e in names}
        dcnt = [0] * self.NDMA
        dlast = [None] * self.NDMA
        ndma = 0
        for op in self.ops:
            if op.dsem >= 0:
                s = ndma % self.NDMA
                ndma += 1
                op.dsem = s
                dcnt[s] += 16
                op.dcnt = dcnt[s]
                op.dprev = dlast[s]
                dlast[s] = op
            elif op.sig:
                cnt[op.eng] += 1
                op.cnt = cnt[op.eng]
        waited = {e: {} for e in list(self.engs)}
        nwait = 0
        for op in self.ops:
            need = {}
            deps = list(op.edeps)
            if op.dprev is not None:
                deps.append(op.dprev)
            for d in deps:
                if d.dsem >= 0:
                    key = ("d", d.dsem)
                    val = d.dcnt
                else:
                    key = ("e", d.eng)
                    val = d.cnt
                if val > need.get(key, (0, None))[0]:
                    need[key] = (val, d)
            eng = self.engs[op.eng]
            w = waited[op.eng]
            todo = []
            for key, (val, d) in sorted(need.items(), key=lambda kv: -kv[1][1].idx if TRANSITIVE else 0):
                if val > w.get(key, 0):
                    w[key] = val
                    if TRANSITIVE:
                        for k2, v2 in d.vc.items():
                            if v2 > w.get(k2, 0):
                                w[k2] = v2
                    sem = dsems[key[1]] if key[0] == "d" else sems[key[1]]
                    todo.append((sem, val))
            if TRANSITIVE and (op.dsem >= 0 or op.sig):
                op.vc = dict(w)
                if op.dsem >= 0:
                    op.vc[("d", op.dsem)] = op.dcnt
                else:
                    op.vc[("e", op.eng)] = op.cnt
            standalone = todo if (op.eng == "pe" and PE_STANDALONE_WAITS) else todo[1:]
            for sem, val in standalone:
                eng.wait_ge(sem, val)
                nwait += 1
            ins = op.fn(eng)
            if todo and standalone is not todo:
                ins._wait_ge(todo[0][0], todo[0][1])
                nwait += 1
            if op.dsem >= 0:
                ins.then_inc(dsems[op.dsem], 16)
            elif op.sig:
                ins.then_inc(sems[op.eng], 1)
        if final_wait:
            sp = self.engs["sp"]
            for s in range(self.NDMA):
                if dcnt[s] > 0:
                    sp.wait_ge(dsems[s], dcnt[s])
        self.stats = dict(nops=len(self.ops), nwait=nwait,
                          nsig=sum(cnt.values()), ndma=ndma)

    def dma(self, out, in_, eng="sp"):
        return self.add(eng, lambda e, o=out, i=in_: e.dma_start(out=o, in_=i),
                        [in_], [out], dma=True)

    def _pe_rows(self, op, lhsT):
        b0 = lhsT.base_partition()
        op.rows = (b0, b0 + lhsT.ap[0][1])
        return op

    def mm(self, out, lhsT, rhs, start=True, stop=True):
        op = self.add("pe", lambda e, o=out, l=lhsT, r=rhs, s=start, t=stop:
                      e.matmul(o, l, r, start=s, stop=t), [lhsT, rhs], [out])
        n = _fsize(rhs)
        op.cost = (0.02 + 0.00085 * max(n, 64)) * (4.0 if rhs.dtype == F32 else 1.0)
        return self._pe_rows(op, lhsT)

    def tr(self, out, in_, ident):
        op = self.add("pe", lambda e, o=out, i=in_, d=ident: e.transpose(o, i, d),
                      [in_, ident], [out])
        op.cost = 0.09
        return self._pe_rows(op, in_)

    def act(self, out, in_, func, bias=None, scale=None, accum=None, eng="act"):
        reads = [in_]
        kw = {}
        if bias is not None:
            kw["bias"] = bias
            if not isinstance(bias, (int, float)):
                reads.append(bias)
        if scale is not None:
            kw["scale"] = scale
            if not isinstance(scale, (int, float)):
                reads.append(scale)
        writes = [out]
        if accum is not None:
            kw["accum_out"] = accum
            writes.append(accum)
        return self.add(eng, lambda e, o=out, i=in_, f=func, k=kw: e.activation(o, i, f, **k),
                        reads, writes)

    def tt(self, eng, out, in0, in1, op):
        return self.add(eng, lambda e, o=out, a=in0, b=in1, p=op: e.tensor_tensor(o, a, b, p),
                        [in0, in1], [out])

    def ts(self, eng, out, in0, s1, op0, s2=None, op1=None, accum=None):
        reads = [in0]
        if not isinstance(s1, (int, float)):
            reads.append(s1)
        if s2 is not None and not isinstance(s2, (int, float)):
            reads.append(s2)
        kw = {}
        if op1 is not None:
            kw["op1"] = op1
        writes = [out]
        if accum is not None:
            kw["accum_out"] = accum
            writes.append(accum)
        return self.add(eng, lambda e, o=out, a=in0, x=s1, y=s2, p=op0, k=kw:
                        e.tensor_scalar(o, a, x, y, p, **k), reads, writes)

    def stt(self, eng, out, in0, scalar, in1, op0, op1):
        reads = [in0, in1]
        if not isinstance(scalar, (int, float)):
            reads.append(scalar)
        return self.add(eng, lambda e, o=out, a=in0, s=scalar, b=in1, p=op0, q=op1:
                        e.scalar_tensor_tensor(o, a, s, b, p, q), reads, [out])

    def copy(self, eng, out, in_):
        def fn(e, o=out, i=in_):
            return e.copy(o, i) if e is self.engs["act"] else e.tensor_copy(o, i)
        return self.add(eng, fn, [in_], [out])

    def memset(self, eng, out, val):
        return self.add(eng, lambda e, o=out, v=val: e.memset(o, v), [], [out])


NCORES = 8
SEQ = 2048
DM = 1024
NSEQ = 2
NTOK = NSEQ * SEQ
TB = 128
NBLK = NTOK // TB
BPS = SEQ // TB
C = 64
INC = 6280
CDEC = math.exp(-0.5)

CV_G, CV_MU, CV_W0, CV_A0, CV_KK, CV_KA, CV_RK, CV_GNW, CV_GNB, CV_CONV, CV_ONW = \
    0, 8, 21, 25, 29, 33, 37, 41, 45, 49, 97
NCV = 98
PD = ("pool", "dve")
DP = ("dve", "pool")
AD = ("act", "dve")
DA = ("dve", "act")
SCHED = True
HORD = (0, 2, 4, 6, 1, 3, 5, 7)


def v3(ap, a, b):
    return ap.rearrange("p (a b) -> p a b", a=a, b=b)


class Arena:
    def __init__(self, nc, name, n, dtype, ap2d=None):
        self.t = ap2d if ap2d is not None else nc.alloc_sbuf_tensor(name, [128, n], dtype)
        self.n = n
        self.off = 0
        self.name = name

    def room(self):
        return self.n - self.off

    def reset(self, off=0):
        self.off = off

    def take(self, *shape, parts=128):
        n = 1
        for s in shape:
            n *= s
        assert self.off + n <= self.n, (self.name, self.off, n, self.n)
        ap = self.t[0:parts, self.off:self.off + n]
        self.off += n
        if len(shape) == 2:
            ap = ap.rearrange("p (a b) -> p a b", a=shape[0], b=shape[1])
        elif len(shape) == 3:
            ap = ap.rearrange("p (a b c) -> p a b c", a=shape[0], b=shape[1], c=shape[2])
        return ap


class Multi:
    def __init__(self, *arenas):
        self.arenas = arenas

    def take(self, *shape, parts=128):
        n = 1
        for s_ in shape:
            n *= s_
        for a in self.arenas:
            if a.room() >= n:
                return a.take(*shape, parts=parts)
        raise AssertionError(("out of scratch", shape, [a.room() for a in self.arenas]))


def build_nc(phases=(1, 2, 3), nblk=NBLK, dbg=None, trunc=None):
    nc = bass.Bass("TRN2", target_bir_lowering=False)
    P = Prog(nc)
    dt = lambda name, shape, kind="ExternalInput": nc.dram_tensor(name, shape, F32, kind=kind).ap()
    x_d = dt("x", [NTOK, DM])
    win_d = dt("w_in", [DM, INC])
    wa_d = dt("w_a", [512, DM])
    wb_d = dt("w_b", [512, DM])
    wo_d = dt("w_o", [DM, DM])
    w2_d = dt("w2", [64, 512])
    a2_d = dt("a2", [64, 512])
    cv_d = dt("cv", [128, NCV])
    nwo_d = dt("nwo", [1, DM])
    gv_d = dt("gvec", [1, 8])
    id_d = dt("ident", [128, 128])
    bo_d = dt("bo", [128, 128])
    mk_d = dt("masks", [64, 3, 64])
    rm_d = dt("resetm", [128, 128])
    mk128_d = dt("masks128", [128, 3, 128])
    out_d = dt("out", [NTOK, DM], kind="ExternalOutput")
    dbg_d = {}
    if dbg:
        for k, shp in dbg.items():
            dbg_d[k] = dt("dbg_" + k, shp, kind="ExternalOutput")

    sb = lambda name, shape, dtype=F32: nc.alloc_sbuf_tensor("s_" + name, shape, dtype)
    ytok = NTOK if not dbg else max(nblk * TB, 1024)
    YA = sb("YA", [128, 4, ytok], BF16)
    YB = sb("YB", [128, 4, ytok], BF16)
    tapbuf = sb("tapbuf", [128, 1024]) if dbg else None
    tapped = set()

    def tap(name, ap, parts=128):
        if not dbg or name not in dbg_d or name in tapped:
            return
        tapped.add(name)
        n = dbg[name][1]
        P.copy(DA, tapbuf[0:parts, 0:n], ap)
        P.dma(dbg_d[name], tapbuf[0:parts, 0:n])

    xbuf = [sb("xbuf%d" % i, [128, DM]) for i in range(2)]
    hT = [sb("hT%d" % i, [128, 8, TB + 3], BF16) for i in range(2)]
    xn = sb("xn", [128, DM], BF16)
    ss = sb("ss", [128, 1])
    rstd = sb("rstd", [128, 1])
    ss2 = sb("ss2", [128, 1])
    ss3 = sb("ss3", [128, 1])
    ss23 = sb("ss23", [128, 1])
    rstd3 = sb("rstd3", [128, 1])
    identb = sb("identb", [128, 128], BF16)
    bob = sb("bob", [128, 128], BF16)
    bo64b = sb("bo64b", [128, 128], BF16)
    onesb = sb("onesb", [128, 128], BF16)
    ones128b = sb("ones128b", [128, 128], BF16)
    onesdb = sb("onesdb", [128, 128], BF16)
    mk128b = sb("mk128b", [128, 3, 128], BF16)
    maskSI128 = sb("maskSI128", [128, 256], BF16)
    resetm = sb("resetm", [128, 128])
    cv = sb("cv", [128, NCV])
    omm = sb("omm", [128, 13])
    negh = sb("negh", [128, 1])
    cvn = sb("cvn", [128, 8])
    gvb128 = sb("gvb128", [128, 8])
    negA128 = sb("negA128", [128, 4])
    Mst = sb("Mst", [128, 4, 128])
    Mb = sb("Mb", [128, 4, 128], BF16)
    Sst = sb("Sst", [128, 4, 128])
    Sb = sb("Sb", [128, 4, 128], BF16)
    WA = Arena(nc, "WA", 32768, BF16)
    AFa = Arena(nc, "AFa", 4500, F32)
    ABa = Arena(nc, "ABa", 18660, BF16)
    if dbg:
        YBa = Arena(nc, "XTR", 16384, BF16)
    else:
        YBa = Arena(nc, "YBa", 4 * ytok, BF16, ap2d=YB[:, :, :].rearrange("p a b -> p (a b)"))

    ptp = nc.alloc_psum_tensor("ptp", [128, 1024], BF16)
    pbig = [nc.alloc_psum_tensor("pbig%d" % i, [128, 512], F32) for i in range(2)]
    psm = [nc.alloc_psum_tensor("psm%d" % i, [128, 512], F32) for i in range(4)]
    plong = nc.alloc_psum_tensor("plong", [128, 512], F32)
    rr = {"big": 0, "small": 0, 0: 0, 1: 0}

    def big():
        rr["big"] += 1
        return pbig[rr["big"] % 2]

    def big3():
        return big()

    def small(c=None):
        if c is None:
            rr["small"] += 1
            return psm[rr["small"] % 4]
        rr[c] += 1
        return psm[2 * c + rr[c] % 2]

    engrr = {"n": 0}

    def anyeng(choices=("dve", "pool")):
        engrr["n"] += 1
        return choices[engrr["n"] % len(choices)]

    identf = xbuf[0][:, 0:128]
    bof = xbuf[0][:, 128:256]
    P.dma(identf, id_d)
    P.dma(bof, bo_d)
    P.dma(resetm[:], rm_d)
    P.dma(cv[:], cv_d)
    P.dma(gvb128[:], gv_d.partition_broadcast(128))
    P.copy(DA, identb[:], identf)
    P.copy(DA, bob[:], bof)
    P.ts("dve", bo64b[:], bof, 1.0 / 64, ALU.mult)
    P.memset("pool", onesb[:], 1.0)
    P.memset("pool", ones128b[:], 128.0)
    P.memset("pool", onesdb[:], 1.0 / 128)
    for q_ in range(3):
        P.dma(xbuf[1][:, q_ * 128:(q_ + 1) * 128], mk128_d[:, q_, :])
    P.copy(DA, mk128b[:, :, :], v3(xbuf[1][:, 0:384], 3, 128))
    P.copy(DA, maskSI128[:, 0:128], xbuf[1][:, 128:256])
    P.copy(DA, maskSI128[:, 128:256], xbuf[1][:, 0:128])
    P.ts("dve", omm[:], cv[:, CV_MU:CV_MU + 13], -1.0, ALU.mult, 1.0, ALU.add)
    P.memset("pool", negh[:], -0.5)
    P.ts("dve", cvn[:, 0:8], cv[:, CV_W0:CV_W0 + 8], -1.0, ALU.mult)
    P.act(negA128[:], gvb128[:, 0:4], AF.Exp)
    P.ts("dve", negA128[:], negA128[:], -1.0, ALU.mult)

    def cvc(col):
        return cv[:, col:col + 1]

    def cvnc(col):
        return cvn[:, col:col + 1]

    def rsqrt(out, in_, tmp, bias, scale=None, pool=False):
        if pool:
            P.act(tmp, in_, AF.Identity, bias=bias, scale=scale)
            P.tt("pool", out, tmp, negh[0:out.shape[0], 0:1].to_broadcast(list(out.shape)), ALU.pow)
        else:
            P.act(tmp, in_, AF.Ln, bias=bias, scale=scale)
            P.act(out, tmp, AF.Exp, scale=-0.5)

    def sigmoid(out, in_, tmp, scale=1.0, nbias=None):
        P.act(tmp, in_, AF.Exp, scale=-scale, bias=nbias)
        P.act(tmp, tmp, AF.Ln, bias=1.0)
        P.act(out, tmp, AF.Exp, scale=-1.0)

    castrr = {"n": 0}

    def load_w(dst3, src, c0, ncols, rowscale=None, p0=0, parts=128):
        ndc = dst3.shape[1]
        for dc in range(ndc):
            for q in range(0, ncols, 1024):
                n = min(1024, ncols - q)
                k = castrr["n"] % 6
                stg = xbuf[k] if k < 2 else AFa.t[:, (k - 2) * 1024:(k - 1) * 1024]
                castrr["n"] += 1
                P.dma(stg[p0:p0 + parts, 0:n], src[dc * parts:(dc + 1) * parts, c0 + q:c0 + q + n],
                      eng=("sp", "act")[castrr["n"] % 2])
                eng = ("dve", "act")[castrr["n"] % 2]
                o = dst3[:, dc, q:q + n]
                i = stg[p0:p0 + parts, 0:n]
                if rowscale is not None:
                    if eng == "act":
                        P.act(o, i, AF.Copy, scale=cvc(rowscale + dc))
                    else:
                        P.ts(eng, o, i, cvc(rowscale + dc), ALU.mult)
                else:
                    P.copy(eng, o, i)

    def stage0(blk, need_halo=True, pool=False, dest=None):
        tok0 = blk * TB
        xt = xbuf[blk % 2]
        cur = hT[blk % 2]
        prev = hT[(blk + 1) % 2]
        P.dma(xt[:], x_d[tok0:tok0 + TB, :])
        P.memset("pool", ss[:], 0.0)
        P.act(xn[:], xt[:], AF.Square, accum=ss[:])
        rsqrt(rstd[:], ss[:], ss2[:], 1e-6, scale=1.0 / DM, pool=pool)
        P.act(xn[:], xt[:], AF.Copy, scale=rstd[:, 0:1])
        for dc in range(8):
            P.tr(ptp[:, dc * 128:(dc + 1) * 128], xn[:, dc * 128:(dc + 1) * 128], identb[:])
        P.copy(DA, dest if dest is not None else cur[:, :, 3:3 + TB], v3(ptp[:, :], 8, 128))
        if need_halo:
            if blk % BPS == 0:
                P.memset("pool", cur[:, :, 0:3], 0.0)
            else:
                P.copy(PD, cur[:, :, 0:3], prev[:, :, TB:TB + 3])
        return xt, cur

    def inproj(W3, c0, cur, halo, ps):
        for dc in range(8):
            P.mm(ps[:, 0:TB + halo], W3[:, dc, c0:c0 + 128], cur[:, dc, 3 - halo:3 + TB],
                 start=(dc == 0), stop=(dc == 7))

    def neumann(nh, P0, Q0, bufs, c=None, n=64):
        Pn, Qn, Xn = bufs
        w = nh * n
        X = Xn[0]
        P.tt(DP, X, P0, identb[0:n, 0:n].unsqueeze(1).to_broadcast([n, nh, n]), ALU.add)
        Pc, Qc = P0, Q0
        for it in range(5):
            last = (it == 4)
            Pd, Qd, Xd = Pn[it % 2], Qn[it % 2], Xn[(it + 1) % 2]
            psQ = small(c)
            for h in range(nh):
                P.mm(psQ[0:n, h * n:(h + 1) * n], Pc[:, h, :], Qc[:, h, :])
            if not last:
                psP = small(c)
                for h in range(nh):
                    P.mm(psP[0:n, h * n:(h + 1) * n], Qc[:, h, :], Pc[:, h, :])
            P.copy(AD, Qd, v3(psQ[0:n, 0:w], nh, n))
            if not last:
                P.copy(DA, Pd, v3(psP[0:n, 0:w], nh, n))
            psX = small(c)
            for h in range(nh):
                P.mm(psX[0:n, h * n:(h + 1) * n], Qd[:, h, :], X[:, h, :])
            P.tt("dve", Xd, X, v3(psX[0:n, 0:w], nh, n), ALU.add)
            X = Xd
            Pc, Qc = Pd, Qd
        return X

    def phase1():
        WA.reset(); AFa.reset(); ABa.reset()
        W1 = WA.take(8, 2176)
        w2a2 = WA.take(512)
        Brk = WA.take(4, 128)
        load_w(W1, win_d, 0, 2176, rowscale=CV_G)
        P.dma(xbuf[0][0:64, 0:512], w2_d)
        P.dma(xbuf[0][64:128, 0:512], a2_d)
        P.copy(DA, w2a2[:, :], xbuf[0][:, 0:512])
        for m in range(4):
            P.ts(PD, Brk[:, m, :], bob[:], cvc(CV_RK + m), ALU.mult)
        SC = Multi(ABa, WA, YBa)
        YBa.reset()
        f = lambda: AFa.take(TB)
        mtemps = [[f() for _ in range(15)] for _ in range(2)]
        wdad = f()
        eGCs = [AFa.take(4, 2) for _ in range(2)]
        o_msq = AFa.take(4, TB)
        o_t1 = SC.take(4, TB); o_yg = SC.take(4, TB); o_yb = SC.take(4, TB)
        twad = SC.take(TB)
        sqbs = [SC.take(TB) for _ in range(2)]
        yTb = SC.take(4, TB)
        sqy = SC.take(4, TB)
        bsets = []
        for _ in range(2):
            bsets.append(dict(
                AR=SC.take(4, 2, TB),
                BK=SC.take(4, 2, TB),
                KBh=SC.take(4, 2, TB),
                vfm=SC.take(4, TB), siluz=SC.take(4, TB), rkb=SC.take(4, TB),
                TM=SC.take(4, 512)))
        psets = []
        for _ in range(2):
            psets.append(dict(
                SA=SC.take(8, 256), SK=SC.take(8, 256), Xn=[SC.take(8, 128) for _ in range(2)],
                AVb=SC.take(8, 64), U0=SC.take(8, 64), WtT=SC.take(4, 256), Ub=SC.take(8, 64),
                Yb=SC.take(512), Y2s=SC.take(512)))
        Q0 = SC.take(8, 128)
        Pn = [[SC.take(4, 128) for _ in range(2)] for _ in range(2)]
        Qn = [[SC.take(4, 128) for _ in range(2)] for _ in range(2)]
        tmp, rr_, kk_, sg, aa, rs, kkn, t1, kp, bb, Gp, eG, eGn, Dp, eD = mtemps[0]
        sqb_ = sqbs[0]

        def shift_evac(ps, j, out):
            P.act(tmp, ps[:, 1:TB + 1], AF.Copy, scale=omm[:, j:j + 1])
            P.stt("dve", out, ps[:, 0:TB], cvc(CV_MU + j), tmp, ALU.mult, ALU.add)

        for blk in range(nblk):
            P.tag = "b%d.s0" % blk
            xt, cur = stage0(blk)
            bs = bsets[blk % 2]
            AR, BK, KBh, vfm, siluz, rkb, TM = bs["AR"], bs["BK"], bs["KBh"], bs["vfm"], bs["siluz"], bs["rkb"], bs["TM"]
            eGC = eGCs[blk % 2]
            tmp, rr_, kk_, sg, aa, rs, kkn, t1, kp, bb, Gp, eG, eGn, Dp, eD = mtemps[0]
            if blk % BPS == 0:
                P.memset("pool", Mst[:], 0.0)
                P.memset("pool", Mb[:], 0.0)
            ps = big()
            inproj(W1, 1536, cur, 1, ps)
            shift_evac(ps, 12, wdad)
            sigmoid(tmp[0:64, :], wdad[0:64, :], rs[0:64, :], scale=2.0)
            P.ts(PD, twad[0:64, :], tmp[0:64, :], 2.0, ALU.mult, -1.0, ALU.add)
            P.copy(AD, twad[64:128, :], wdad[64:128, :])
            for m in range(4):
                tmp, rr_, kk_, sg, aa, rs, kkn, t1, kp, bb, Gp, eG, eGn, Dp, eD = mtemps[m % 2]
                sqb_ = sqbs[m % 2]
                P.tag = "b%d.A%d" % (blk, m)
                ps = big(); inproj(W1, m * 128, cur, 1, ps); shift_evac(ps, m, rr_)
                ps = big(); inproj(W1, 512 + m * 128, cur, 1, ps); shift_evac(ps, 4 + m, kk_)
                ps = big(); inproj(W1, 1024 + m * 128, cur, 1, ps); shift_evac(ps, 8 + m, vfm[:, m, :])
                ps = big(); inproj(W1, 1664 + m * 128, cur, 0, ps)
                P.act(aa, ps[:, 0:TB], AF.Copy)
                sigmoid(sg, ps[:, 0:TB], rs)
                P.tt(PD, siluz[:, m, :], sg, aa, ALU.mult)
                ps = big()
                P.mm(ps[:, 0:TB], w2a2[0:64, m * 128:(m + 1) * 128], twad[0:64, :])
                sigmoid(sg, ps[:, 0:TB], rs, nbias=cvnc(m))
                ps = big()
                P.mm(ps[:, 0:TB], w2a2[64:128, m * 128:(m + 1) * 128], twad[64:128, :])
                sigmoid(aa, ps[:, 0:TB], rs, nbias=cvnc(4 + m))
                P.act(sqb_, kk_, AF.Square, scale=cvc(CV_KK + m))
                ps = big()
                P.mm(ps[:, 0:TB], bob[:], sqb_)
                rsqrt(rs, ps[:, 0:TB], t1, 1e-12)
                P.stt("dve", kkn, kk_, cvc(CV_KK + m), rs, ALU.mult, ALU.mult)
                P.ts(PD, t1, aa, -1.0, ALU.add, cvc(CV_KA + m), ALU.mult)
                P.stt("dve", kp, t1, 1.0, kk_, ALU.add, ALU.mult)
                P.tt(PD, bb, kkn, aa, ALU.mult)
                P.add("dve", lambda e, o=Gp, a=resetm[:], b=sg: e.tensor_tensor_scan(
                    o, a, b, 0.0, ALU.mult, ALU.add), [resetm[:], sg], [Gp])
                P.act(eG, Gp, AF.Exp, scale=-CDEC)
                P.act(eGn, Gp, AF.Exp, scale=CDEC)
                Gp3 = v3(Gp, 2, 64)
                P.tt(DP, v3(Dp, 2, 64), Gp3, Gp3[:, :, 63:64].to_broadcast([128, 2, 64]), ALU.subtract)
                P.act(eD, Dp, AF.Exp, scale=CDEC)
                eG3 = v3(eG, 2, 64)
                kk3 = v3(kkn, 2, 64)
                At3 = v3(AR[:, m, 0, :], 2, 64)
                P.stt("dve", At3[:, :, 1:64], kk3[:, :, 1:64], -1.0, eG3[:, :, 0:63], ALU.mult, ALU.mult)
                P.ts("dve", At3[:, :, 0:1], kk3[:, :, 0:1], -1.0, ALU.mult)
                P.tt(PD, AR[:, m, 1, :], rr_, eG, ALU.mult)
                P.tt(PD, BK[:, m, 0, :], bb, eGn, ALU.mult)
                P.tt(DP, BK[:, m, 1, :], kp, eGn, ALU.mult)
                P.tt(PD, KBh[:, m, 0, :], kp, eD, ALU.mult)
                P.tt(DP, KBh[:, m, 1, :], bb, eD, ALU.mult)
                P.tt(PD, rkb[:, m, :], rr_, kp, ALU.mult)
                P.copy(DA, eGC[:, m, :], eG3[:, :, 63])
                tap("t_r", rr_); tap("t_k", kk_); tap("t_v", vfm[:, m, :]); tap("t_sg", sg); tap("t_a", aa)
                tap("t_kkn", kkn); tap("t_kp", kp); tap("t_Gp", Gp); tap("t_eG", eG); tap("t_At", AR[:, m, 0, :])
                tap("t_wdad", wdad); tap("t_eD", eD)
            pYT = plong
            pp = psets[blk % 2]
            SA, SK, Xn, AVb, U0, WtT, Ub, Yb, Y2s = (pp[k] for k in ("SA", "SK", "Xn", "AVb", "U0", "WtT", "Ub", "Yb", "Y2s"))
            P.tag = "b%d.Bpar0" % blk
            srcs = [lambda m: AR[:, m, 0, :], lambda m: KBh[:, m, 0, :], lambda m: KBh[:, m, 1, :], lambda m: vfm[:, m, :]]
            for kind in range(4):
                ps = small()
                for m in range(4):
                    P.mm(ps[:, m * 128:(m + 1) * 128], srcs[kind](m), identb[:, :])
                P.copy(AD if kind % 2 == 0 else DA, TM[:, kind, :], ps[:, :])
            msk = maskSI128[:, :].unsqueeze(1).to_broadcast([128, 2, 256])
            for g in range(4):
                psA = small(); psK = small()
                for e in (0, 1):
                    pb = e * 64
                    P.mm(psA[:, e * 256:(e + 1) * 256], BK[pb:pb + 64, g, 0, :], AR[pb:pb + 64, g, :, :])
                    P.mm(psK[:, e * 256:(e + 1) * 256], BK[pb:pb + 64, g, 1, :], AR[pb:pb + 64, g, :, :])
                P.tt("dve", SA[:, 2 * g:2 * g + 2, :], v3(psA[:, :], 2, 256), msk, ALU.mult)
                P.tt("dve", SK[:, 2 * g:2 * g + 2, :], v3(psK[:, :], 2, 256), msk, ALU.mult)
            for half in range(2):
                psQ = small()
                for hl in (0, 2, 1, 3):
                    h = half * 4 + hl
                    m, pb = h // 2, (h % 2) * 64
                    P.mm(psQ[:, hl * 128:(hl + 1) * 128], AR[pb:pb + 64, m, 0, :], BK[pb:pb + 64, m, 0, :])
                P.tt("dve", Q0[:, half * 4:(half + 1) * 4, :], v3(psQ[:, :], 4, 128),
                     mk128b[:, 2, :].unsqueeze(1).to_broadcast([128, 4, 128]), ALU.mult)
            for half in range(2):
                hs = slice(half * 4, (half + 1) * 4)
                neumann(4, SA[:, hs, 0:128], Q0[:, hs, :], (Pn[half], Qn[half], [Xn[0][:, hs, :], Xn[1][:, hs, :]]), None, n=128)
            X = Xn[1]
            ps = small()
            for h in range(8):
                P.mm(ps[:, h * 64:(h + 1) * 64], SK[:, h, 0:128], TM[:, 3, h * 64:(h + 1) * 64])
            P.copy(AD, AVb, v3(ps[:, :], 8, 64))
            ps = small()
            for h in range(8):
                P.mm(ps[:, h * 64:(h + 1) * 64], X[:, h, :], AVb[:, h, :])
            P.copy(AD, U0, v3(ps[:, :], 8, 64))
            for q2 in range(2):
                ps = small()
                for ml in range(2):
                    m = q2 * 2 + ml
                    P.mm(ps[:, ml * 256:(ml + 1) * 256], TM[:, 0, m * 128:(m + 1) * 128], X[:, 2 * m:2 * m + 2, :])
                P.copy(DA, WtT[:, q2 * 2:q2 * 2 + 2, :], v3(ps[:, :], 2, 256))
            for c in range(2):
                pc = c * 64
                rows = slice(pc, pc + 64)
                P.tag = "b%d.Bseq%d" % (blk, c)
                psU = small()
                psY2 = small()
                for h in HORD:
                    m, pb, e = h // 2, (h % 2) * 64, h % 2
                    P.mm(psU[:, h * 64:(h + 1) * 64], WtT[pb:pb + 64, m, e * 128:(e + 1) * 128],
                         Mb[pb:pb + 64, m, e * 64:(e + 1) * 64])
                    P.mm(psY2[:, h * 64:(h + 1) * 64], AR[pb:pb + 64, m, 1, :], Mb[pb:pb + 64, m, e * 64:(e + 1) * 64])
                P.tt("dve", Ub[rows], v3(psU[rows, :], 8, 64), U0[rows], ALU.add)
                P.copy(AD, Y2s[rows, :], psY2[rows, :])
                psY = small()
                for h in range(8):
                    o = psY[:, h * 64:(h + 1) * 64]
                    P.mm(o, SK[rows, h, 128:256], TM[rows, 3, h * 64:(h + 1) * 64], start=True, stop=False)
                    P.mm(o, SA[rows, h, 128:256], Ub[rows, h, :], start=False, stop=True)
                P.tt("dve", Yb[rows, :], psY[rows, :], Y2s[rows, :], ALU.add)
                psM = small()
                for m in range(4):
                    o = psM[:, m * 128:(m + 1) * 128]
                    P.mm(o, TM[rows, 1, m * 128:(m + 1) * 128], TM[rows, 3, m * 128:(m + 1) * 128], start=True, stop=False)
                    P.mm(o, TM[rows, 2, m * 128:(m + 1) * 128], Ub[rows, 2 * m:2 * m + 2, :], start=False, stop=True)
                P.tt(PD, Mst[:], Mst[:], eGC[:, :, c:c + 1].to_broadcast([128, 4, 128]), ALU.mult)
                P.tt("dve", Mst[:], Mst[:], v3(psM[:, :], 4, 128), ALU.add)
                P.copy(AD, Mb[:], Mst[:])
                for m in range(4):
                    P.mm(pYT[:, m * 128 + c * 64:m * 128 + (c + 1) * 64], Yb[rows, m * 128:(m + 1) * 128], identb[rows, pc:pc + 64])
            P.tag = "b%d.C" % blk
            P.copy(DA, yTb, v3(pYT[:, :], 4, TB))
            P.act(sqy, yTb, AF.Square)
            psm_ = small(0); pse_ = small(1)
            for m in range(4):
                P.mm(psm_[:, m * 128:(m + 1) * 128], bo64b[:], yTb[:, m, :])
            for m in range(4):
                P.mm(pse_[:, m * 128:(m + 1) * 128], bo64b[:], sqy[:, m, :])
            P.tt("dve", o_t1, yTb, v3(psm_[:, :], 4, TB), ALU.subtract)
            P.act(o_msq, v3(psm_[:, :], 4, TB), AF.Square)
            P.tt("dve", o_msq, v3(pse_[:, :], 4, TB), o_msq, ALU.subtract)
            P.ts("dve", o_msq, o_msq, 0.0, ALU.max)
            rsqrt(o_msq, o_msq, o_msq, 64e-5)
            P.tt(PD, o_t1, o_t1, o_msq, ALU.mult)
            for m in range(4):
                P.act(o_yg[:, m, :], o_t1[:, m, :], AF.Identity, scale=cvc(CV_GNW + m), bias=cvc(CV_GNB + m))
            psb_ = small(0)
            for m in range(4):
                P.mm(psb_[:, m * 128:(m + 1) * 128], Brk[:, m, :], rkb[:, m, :])
            P.tt("dve", o_yb, v3(psb_[:, :], 4, TB), vfm, ALU.mult)
            tap("t_yg", o_yg.rearrange("p a b -> p (a b)")); tap("t_yb", o_yb.rearrange("p a b -> p (a b)"))
            tap("t_yT", yTb.rearrange("p a b -> p (a b)"))
            P.tt(PD, o_yg, o_yg, o_yb, ALU.add)
            P.tt(PD, YA[:, :, blk * TB:(blk + 1) * TB], o_yg, siluz, ALU.mult)

    def phase2():
        WA.reset(); AFa.reset(); ABa.reset()
        W2 = WA.take(8, 2056)
        load_w(W2, win_d, 2176, 2056, rowscale=CV_G)
        SC = Multi(ABa, WA)
        f = lambda: AFa.take(TB)
        jtemps = [[f() for _ in range(4)] for _ in range(2)]
        sqbs = [SC.take(TB) for _ in range(2)]
        gTri = AFa.take(4, 128)
        E0 = AFa.take(4, 128)
        EMt = AFa.take(4, 128)
        o_t = AFa.take(4, TB)
        o_r = AFa.take(4, TB)
        tba = AFa.take(4)
        gg = AFa.take(4)
        gcs = AFa.take(4)
        dgl = AFa.take(4)
        mk128f = AFa.take(3, 128)
        ones128f = AFa.take(128)
        for q_ in range(3):
            P.dma(mk128f[:, q_, :], mk128_d[:, q_, :])
        P.memset("pool", ones128f, 1.0)
        oTb = SC.take(4, TB)
        sqo = SC.take(4, TB)
        bsets = []
        for _ in range(2):
            bsets.append(dict(
                beta=AFa.take(4), nbeta=AFa.take(4), egc=AFa.take(4), ekt=AFa.take(4), egl=AFa.take(8),
                QKfm=SC.take(4, 2, TB),
                vfm=SC.take(4, TB), siluz=SC.take(4, TB), qdT=SC.take(4, TB),
                EM=SC.take(4, 2, 128),
                kg=SC.take(4, 128), kte=SC.take(4, 128), vtm=SC.take(4, 128), PQ=SC.take(4, 256),
                Xn=[SC.take(4, 128) for _ in range(2)], WnT=SC.take(4, 128), vnew=SC.take(4, 128), ob=SC.take(512)))
        Q0 = SC.take(4, 128)
        Pn = [SC.take(4, 128) for _ in range(2)]
        Qn = [SC.take(4, 128) for _ in range(2)]

        for blk in range(nblk):
            xt, cur = stage0(blk)
            bs = bsets[blk % 2]
            beta, nbeta, egc, ekt, egl, QKfm, vfm, siluz, qdT, EM, kg, kte, vtm, PQ, Xn, WnT, vnew, ob = (bs[k] for k in (
                "beta", "nbeta", "egc", "ekt", "egl", "QKfm", "vfm", "siluz", "qdT", "EM", "kg", "kte", "vtm", "PQ",
                "Xn", "WnT", "vnew", "ob"))
            if blk % BPS == 0:
                P.memset("pool", Sst[:], 0.0)
                P.memset("pool", Sb[:], 0.0)
            for j in range(12):
                accA, accB, sv, rs = jtemps[j % 2]
                sqb_ = sqbs[j % 2]
                ps = big()
                inproj(W2, j * 128, cur, 3, ps)
                cw = lambda i: cvc(CV_CONV + i * 12 + j)
                P.act(accA, ps[:, 3:TB + 3], AF.Copy, scale=cw(3))
                P.stt("dve", accB, ps[:, 2:TB + 2], cw(2), accA, ALU.mult, ALU.add)
                P.stt("dve", accA, ps[:, 1:TB + 1], cw(1), accB, ALU.mult, ALU.add)
                P.stt("dve", accB, ps[:, 0:TB], cw(0), accA, ALU.mult, ALU.add)
                h = j % 4
                sigmoid(accA, accB, accA)
                if j >= 8:
                    P.tt(PD, vfm[:, h, :], accA, accB, ALU.mult)
                else:
                    P.tt(PD, sv, accA, accB, ALU.mult)
                    P.act(sqb_, sv, AF.Square)
                    ps2 = big()
                    P.mm(ps2[:, 0:TB], (ones128b if j < 4 else onesb)[:], sqb_)
                    eps = 128e-12 if j < 4 else 1e-12
                    rsqrt(rs, ps2[:, 0:TB], accA, eps)
                    P.tt(PD, QKfm[:, h, 1 if j < 4 else 0, :], sv, rs, ALU.mult)
            for h in range(4):
                accA, accB, sv, rs = jtemps[h % 2]
                ps = big()
                inproj(W2, 1536 + h * 128, cur, 0, ps)
                P.act(accB, ps[:, 0:TB], AF.Copy)
                sigmoid(accA, ps[:, 0:TB], accA)
                P.tt(PD, siluz[:, h, :], accA, accB, ALU.mult)
            psba = small()
            for dc in range(8):
                P.mm(psba[:, 0:8], cur[:, dc, 3:3 + TB], W2[:, dc, 2048:2056], start=(dc == 0), stop=(dc == 7))
            sigmoid(beta, psba[:, 0:4], beta)
            P.tt("dve", tba, psba[:, 4:8], gvb128[:, 4:8], ALU.add)
            P.act(tba, tba, AF.Exp)
            P.act(tba, tba, AF.Ln, bias=1.0)
            P.tt("dve", gg, tba, negA128[:, :], ALU.mult)
            P.ts(PD, nbeta, beta, -1.0, ALU.mult)
            ps = small()
            P.mm(ps[:, 0:4], mk128f[:, 0, :], gg)
            P.act(egc, ps[:, 0:4], AF.Exp)
            P.copy(AD, gcs, ps[:, 0:4])
            ps = small()
            for c in range(2):
                P.mm(ps[:, c * 4:(c + 1) * 4], ones128f[c * 64:(c + 1) * 64, :], gg[c * 64:(c + 1) * 64, :])
            P.act(egl, ps[:, 0:8], AF.Exp)
            for c in range(2):
                rows = slice(c * 64, (c + 1) * 64)
                P.tt("dve", dgl[rows, :], ps[rows, c * 4:(c + 1) * 4], gcs[rows, :], ALU.subtract)
            P.act(ekt, dgl, AF.Exp)
            P.tt(DP, gTri, mk128f[:, 0, :].unsqueeze(1).to_broadcast([128, 4, 128]),
                 gg.unsqueeze(2).to_broadcast([128, 4, 128]), ALU.mult)
            ps = small()
            P.mm(ps[:, :], mk128f[:, 2, :], gTri.rearrange("p a b -> p (a b)"))
            P.act(E0, v3(ps[:, :], 4, 128), AF.Exp)
            P.tt(PD, EM[:, :, 1, :], E0, mk128f[:, 0, :].unsqueeze(1).to_broadcast([128, 4, 128]), ALU.mult)
            P.tt(DP, EMt, E0, nbeta.unsqueeze(2).to_broadcast([128, 4, 128]), ALU.mult)
            P.tt(PD, EM[:, :, 0, :], EMt, mk128f[:, 1, :].unsqueeze(1).to_broadcast([128, 4, 128]), ALU.mult)
            P.tt(DP, gTri, identb[:, :].unsqueeze(1).to_broadcast([128, 4, 128]),
                 egc.unsqueeze(2).to_broadcast([128, 4, 128]), ALU.mult)
            ps = small()
            P.mm(ps[:, :], ones128f, gTri.rearrange("p a b -> p (a b)"))
            P.tt("dve", qdT, QKfm[:, :, 1, :], v3(ps[:, :], 4, 128), ALU.mult)
            pOT = plong
            ps = small()
            for h in range(4):
                P.mm(ps[:, h * 128:(h + 1) * 128], QKfm[:, h, 0, :], identb[:, :])
            P.tt("dve", kg, v3(ps[:, :], 4, 128), egc.unsqueeze(2).to_broadcast([128, 4, 128]), ALU.mult)
            P.tt("dve", kte, v3(ps[:, :], 4, 128), ekt.unsqueeze(2).to_broadcast([128, 4, 128]), ALU.mult)
            ps = small()
            for h in range(4):
                P.mm(ps[:, h * 128:(h + 1) * 128], vfm[:, h, :], identb[:, :])
            P.copy(AD, vtm, v3(ps[:, :], 4, 128))
            for q2 in range(2):
                ps = small()
                for hl in range(2):
                    h = q2 * 2 + hl
                    P.mm(ps[:, hl * 256:(hl + 1) * 256], QKfm[:, h, 0, :], QKfm[:, h, :, :])
                P.tt("dve", PQ[:, q2 * 2:q2 * 2 + 2, :], v3(ps[:, :], 2, 256),
                     EM[:, q2 * 2:q2 * 2 + 2, :, :].rearrange("p a b c -> p a (b c)"), ALU.mult)
            ps = small()
            for h in range(4):
                P.mm(ps[:, h * 128:(h + 1) * 128], PQ[:, h, 0:128], identb[:, :])
            P.copy(AD, Q0, v3(ps[:, :], 4, 128))
            X = neumann(4, PQ[:, :, 0:128], Q0, (Pn, Qn, Xn), None, n=128)
            ps = small()
            for h in range(4):
                P.mm(ps[:, h * 128:(h + 1) * 128], kg[:, h, :], X[:, h, :])
            P.ts("dve", WnT, v3(ps[:, :], 4, 128), -1.0, ALU.mult)
            for c in range(2):
                pc = c * 64
                rows = slice(pc, pc + 64)
                psV = small()
                for h in range(4):
                    o = psV[:, h * 128:(h + 1) * 128]
                    P.mm(o, X[rows, h, :], vtm[rows, h, :], start=True, stop=False)
                    P.mm(o, WnT[:, h, :], Sb[:, h, :], start=False, stop=True)
                P.tt("dve", vnew[rows], v3(psV[rows, :], 4, 128), beta[rows, :].unsqueeze(2).to_broadcast([64, 4, 128]), ALU.mult)
                psO = small()
                for h in range(4):
                    o = psO[:, h * 128:(h + 1) * 128]
                    P.mm(o, qdT[:, h, :], Sb[:, h, :], start=True, stop=False)
                    P.mm(o, PQ[rows, h, 128:256], vnew[rows, h, :], start=False, stop=True)
                P.copy(AD, ob[rows, :], psO[rows, :])
                psS = small()
                for h in range(4):
                    P.mm(psS[:, h * 128:(h + 1) * 128], kte[rows, h, :], vnew[rows, h, :])
                P.tt(PD, Sst[:], Sst[:], egl[:, c * 4:(c + 1) * 4].unsqueeze(2).to_broadcast([128, 4, 128]), ALU.mult)
                P.tt("dve", Sst[:], Sst[:], v3(psS[:, :], 4, 128), ALU.add)
                P.copy(AD, Sb[:], Sst[:])
                for h in range(4):
                    P.mm(pOT[:, h * 128 + c * 64:h * 128 + (c + 1) * 64], ob[rows, h * 128:(h + 1) * 128], identb[rows, pc:pc + 64])
            P.copy(DA, oTb, v3(pOT[:, :], 4, TB))
            P.act(sqo, oTb, AF.Square)
            ps = small(0)
            for h in range(4):
                P.mm(ps[:, h * 128:(h + 1) * 128], onesdb[:], sqo[:, h, :])
            rsqrt(o_r, v3(ps[:, :], 4, TB), o_t, 1e-6)
            P.tt(PD, o_t, oTb, o_r, ALU.mult)
            P.stt("dve", YB[:, :, blk * TB:(blk + 1) * TB], o_t, cvc(CV_ONW), siluz, ALU.mult, ALU.mult)

    def phase3():
        WA.reset(); AFa.reset(); ABa.reset()
        Wg = WA.take(8, 2048)
        Wa = WA.take(4, 1024)
        Wb = WA.take(4, 1024)
        Wo = WA.take(8, 1024)
        load_w(Wg, win_d, 4232, 2048, rowscale=CV_G)
        load_w(Wa, wa_d, 0, 1024)
        load_w(Wb, wb_d, 0, 1024)
        load_w(Wo, wo_d, 0, 1024)
        T3 = 4 * TB
        xrs = [AFa.take(DM) for _ in range(2)]
        xres = AFa.take(DM)
        nwbc = AFa.take(DM)
        hT3 = [ABa.take(8, T3) for _ in range(2)]
        mgs = [ABa.take(8, T3) for _ in range(2)]
        sa = ABa.take(T3); sbb = ABa.take(T3)
        xn3 = ABa.take(2 * T3)
        t1 = xn3[:, 0:T3]; t2 = xn3[:, T3:2 * T3]
        P.dma(nwbc, nwo_d.partition_broadcast(128))
        for B in range(nblk // 4):
            h3 = hT3[B % 2]
            mg = mgs[B % 2]
            for t in range(4):
                stage0(B * 4 + t, need_halo=False, pool=True, dest=h3[:, :, t * TB:(t + 1) * TB])
            tok = slice(B * T3, (B + 1) * T3)
            for cc in range(8):
                psa = big3()
                for dc in range(8):
                    P.mm(psa[:, :], Wg[:, dc, cc * 128:(cc + 1) * 128], h3[:, dc, :], start=(dc == 0), stop=(dc == 7))
                psb = big3()
                for dc in range(8):
                    P.mm(psb[:, :], Wg[:, dc, 1024 + cc * 128:1024 + (cc + 1) * 128], h3[:, dc, :], start=(dc == 0), stop=(dc == 7))
                pya = small()
                for kc in range(4):
                    P.mm(pya[:, :], Wa[:, kc, cc * 128:(cc + 1) * 128], YA[:, kc, tok], start=(kc == 0), stop=(kc == 3))
                pyb = small()
                for kc in range(4):
                    P.mm(pyb[:, :], Wb[:, kc, cc * 128:(cc + 1) * 128], YB[:, kc, tok], start=(kc == 0), stop=(kc == 3))
                P.act(sa, psa[:, :], AF.Sigmoid)
                P.act(sbb, psb[:, :], AF.Sigmoid)
                P.tt("dve", t1, pya[:, :], sa, ALU.mult)
                P.tt("dve", t2, pyb[:, :], sbb, ALU.mult)
                P.tt(PD, mg[:, cc, :], t1, t2, ALU.add)
            for t in range(4):
                blk = B * 4 + t
                xr = xrs[blk % 2]
                rows = slice(blk * TB, (blk + 1) * TB)
                P.dma(xres, x_d[rows, :])
                for half in range(2):
                    pso = big3()
                    for kc in range(8):
                        P.mm(pso[:, :], mg[:, kc, t * TB:(t + 1) * TB], Wo[:, kc, half * 512:(half + 1) * 512],
                             start=(kc == 0), stop=(kc == 7))
                    P.tt("dve", xr[:, half * 512:(half + 1) * 512], pso[:, :], xres[:, half * 512:(half + 1) * 512], ALU.add)
                P.memset("pool", ss3[:], 0.0)
                P.act(xn3, xr, AF.Square, accum=ss3[:])
                rsqrt(rstd3[:], ss3[:], ss23[:], 1e-6, scale=1.0 / DM, pool=True)
                P.stt("dve", xr, xr, rstd3[:, 0:1], nwbc, ALU.mult, ALU.mult)
                P.dma(out_d[rows, :], xr)

    if 1 in phases:
        phase1()
    if 2 in phases:
        phase2()
    if 3 in phases:
        phase3()
    if dbg:
        if "YA" in dbg_d:
            stg = AFa.t
            for m in range(4):
                for q in range(0, dbg["YA"][2], 1024):
                    n = min(1024, dbg["YA"][2] - q)
                    P.copy(DA, stg[:, 0:n], YA[:, m, q:q + n])
                    P.dma(dbg_d["YA"][:, m, q:q + n], stg[:, 0:n])
        if "YB" in dbg_d:
            stg = AFa.t
            for m in range(4):
                for q in range(0, dbg["YB"][2], 1024):
                    n = min(1024, dbg["YB"][2] - q)
                    P.copy(DA, stg[:, 0:n], YB[:, m, q:q + n])
                    P.dma(dbg_d["YB"][:, m, q:q + n], stg[:, 0:n])
    if trunc:
        P.ops = P.ops[:trunc]
    if SCHED:
        P.schedule()
    P.emit()
    return nc, P


def host_inputs(inputs):
    f = lambda a: np.ascontiguousarray(np.asarray(a, dtype=np.float32))
    x = f(inputs["x"])
    vec4 = lambda v: f(v).reshape(-1, 128).T
    cw = f(inputs["gd_conv_w"])[0]
    cols = [vec4(inputs["norm_in_w"][0]), vec4(inputs["rw_mu"][0]), vec4(inputs["rw_w0"][0]),
            vec4(inputs["rw_a0"][0]), vec4(inputs["rw_k_k"][0]), vec4(inputs["rw_k_a"][0]),
            vec4(f(inputs["rw_r_k"])[0].reshape(-1)), vec4(inputs["rw_gn_w"][0]), vec4(inputs["rw_gn_b"][0])]
    cols += [vec4(cw[i]) for i in range(4)]
    cols += [f(inputs["gd_o_norm_w"])[0].reshape(128, 1)]
    cvh = np.ascontiguousarray(np.concatenate(cols, axis=1))
    assert cvh.shape == (128, NCV), cvh.shape
    p = np.arange(128)
    bo = (p[:, None] // 64 == p[None, :] // 64).astype(np.float32)
    q = np.arange(64)
    masks = np.stack([(q[:, None] <= q[None, :]), (q[:, None] < q[None, :]), (q[:, None] > q[None, :])],
                     axis=1).astype(np.float32)
    resetm = np.ones((128, 128), np.float32)
    resetm[:, 0] = 0.0
    resetm[:, 64] = 0.0
    shared = {
        "w_in": f(inputs["w_in"])[0], "w_a": f(inputs["w_branch_a"])[0], "w_b": f(inputs["w_branch_b"])[0],
        "w_o": f(inputs["w_out"])[0], "w2": f(inputs["rw_w2"])[0], "a2": f(inputs["rw_a2"])[0],
        "cv": cvh, "nwo": f(inputs["norm_out_w"]).reshape(1, DM),
        "gvec": np.concatenate([f(inputs["gd_A_log"])[0], f(inputs["gd_dt_bias"])[0]]).reshape(1, 8),
        "ident": np.eye(128, dtype=np.float32), "bo": bo, "masks": np.ascontiguousarray(masks), "resetm": resetm,
        "masks128": np.ascontiguousarray(np.kron(np.eye(2, dtype=np.float32)[:, None, :], masks).astype(np.float32)),
    }
    in_maps = []
    for c in range(NCORES):
        m = dict(shared)
        m["x"] = np.ascontiguousarray(x[NSEQ * c:NSEQ * (c + 1)].reshape(NTOK, DM))
        in_maps.append(m)
    return in_maps


def kernel(**inputs):
    in_maps = host_inputs(inputs)
    nc, _ = build_nc()
    res = run_bass_kernel_spmd(nc, in_maps, core_ids=list(range(NCORES)))
    outs = [np.asarray(r["out"], dtype=np.float32).reshape(NSEQ, SEQ, DM) for r in res.results]
    return np.concatenate(outs, axis=0)
```
